# Optimizing a Trainium2 kernel written in Bass

```python
import jax
import jax.numpy as jnp
from jax import lax
import numpy as np

D_MODEL = 1024
BATCH = 2
SEQ = 8192
DEPTH = 4

CHUNK = 64
RWKV_HEADS = 8
RWKV_HEAD_DIM = 64
RWKV_WIDTH = RWKV_HEADS * RWKV_HEAD_DIM
DECAY_RANK = 64
ICLR_RANK = 64
GATE_RANK = 128
VRES_RANK = 32
ATT_HEADS = 8
ATT_HEAD_DIM = 64
ATT_WIDTH = ATT_HEADS * ATT_HEAD_DIM
LEFT_CHUNKS = 8
BAND_CHUNKS = LEFT_CHUNKS + 1
BAND = BAND_CHUNKS * CHUNK
REL_MIN = -(CHUNK - 1)
REL_MAX = 128
N_REL = REL_MAX - REL_MIN + 1
MEM_TOKENS = 256
MEM_HEADS = 4
MEM_HEAD_DIM = 128
MEM_WIDTH = MEM_HEADS * MEM_HEAD_DIM
N_BRANCHES = 3
D_FF = 2816
RMS_EPS = 1e-6
GN_EPS = 64e-5
L2_EPS = 1e-12
NEG_INF = -1e30

RWKV_IN = 3 * RWKV_WIDTH + DECAY_RANK + ICLR_RANK + GATE_RANK
ATT_IN = 3 * ATT_WIDTH
MEM_IN = MEM_WIDTH
D_IN = RWKV_IN + ATT_IN + MEM_IN
RWKV_SPLITS = (RWKV_WIDTH, 2 * RWKV_WIDTH, 3 * RWKV_WIDTH,
               3 * RWKV_WIDTH + DECAY_RANK, 3 * RWKV_WIDTH + DECAY_RANK + ICLR_RANK)

kernel_name = 'hybrid_rwkv7_chunkattn_memory_macaron'


def rms_norm(x, gain):
    xf = x.astype(jnp.float32)
    y = xf * lax.rsqrt(jnp.mean(xf * xf, axis=-1, keepdims=True) + RMS_EPS)
    return (y * gain.astype(jnp.float32)).astype(x.dtype)


def swiglu(x, w_in, w_out):
    gate, up = jnp.split(x @ w_in, 2, axis=-1)
    return (jax.nn.silu(gate) * up) @ w_out


def shift_one(u):
    return jnp.pad(u, ((0, 0), (1, 0), (0, 0)))[:, :-1]


def split_heads(u, n_heads):
    return u.reshape(u.shape[0], u.shape[1], n_heads, -1)


def wkv7_scan(r, w, k, v, kk, b):
    def step(state, inp):
        r_t, w_t, k_t, v_t, kk_t, b_t = inp
        sa = jnp.einsum('bhvk,bhk->bhv', state, kk_t)
        state = (state * w_t[:, :, None, :]
                 - sa[..., None] * b_t[:, :, None, :]
                 + v_t[..., None] * k_t[:, :, None, :])
        return state, jnp.einsum('bhvk,bhk->bhv', state, r_t)
    xs = tuple(jnp.moveaxis(t, 1, 0) for t in (r, w, k, v, kk, b))
    bsz, _, nh, n = r.shape
    s0 = jnp.zeros((bsz, nh, n, n), jnp.float32)
    _, ys = lax.scan(step, s0, xs)
    return jnp.moveaxis(ys, 0, 1)


def rwkv7_time_mix(p, h, mu, w0, decay_b, a0, iclr_b, gate_b, k_k, k_a, r_k, gn_g, gn_b,
                   v_first, v0, vres_a, vres_b, use_vres):
    f32 = jnp.float32
    bsz, seq = p.shape[0], p.shape[1]
    p = p + (shift_one(p) - p) * mu
    r, k, v, xw, xa, xg = jnp.split(p, RWKV_SPLITS, axis=-1)
    w_log = -jax.nn.softplus(-(w0 + jnp.tanh(xw) @ decay_b).astype(f32)) - 0.5
    decay = jnp.exp(-jnp.exp(w_log))
    a = jax.nn.sigmoid((a0 + xa @ iclr_b).astype(f32))
    g = jax.nn.sigmoid(xg) @ gate_b
    if use_vres:
        v = v + (v_first - v) * jax.nn.sigmoid(v0 + (h @ vres_a) @ vres_b)
    else:
        v_first = v
    r, k, v = (t.astype(f32) for t in (r, k, v))
    kk = split_heads(k * k_k, RWKV_HEADS)
    kk = kk / jnp.maximum(jnp.linalg.norm(kk, axis=-1, keepdims=True), L2_EPS)
    k = k * (1.0 + (a - 1.0) * k_a)
    rh = split_heads(r, RWKV_HEADS)
    kh = split_heads(k, RWKV_HEADS)
    vh = split_heads(v, RWKV_HEADS)
    ah = split_heads(a, RWKV_HEADS)
    y = wkv7_scan(rh, split_heads(decay, RWKV_HEADS), kh, vh, kk, kk * ah)
    mean = jnp.mean(y, axis=-1, keepdims=True)
    var = jnp.mean(jnp.square(y - mean), axis=-1, keepdims=True)
    y = ((y - mean) * lax.rsqrt(var + GN_EPS)).reshape(bsz, seq, RWKV_WIDTH) * gn_g + gn_b
    bonus = jnp.sum(rh * kh * r_k, axis=-1, keepdims=True) * vh
    y = (y + bonus.reshape(bsz, seq, RWKV_WIDTH)) * g.astype(f32)
    return y.astype(p.dtype), v_first


def chunk_band(t):
    bsz, seq, nh, dh = t.shape
    nc = seq // CHUNK
    tc = t.reshape(bsz, nc, CHUNK, nh, dh)
    tp = jnp.pad(tc, ((0, 0), (LEFT_CHUNKS, 0), (0, 0), (0, 0), (0, 0)))
    band = jnp.stack([tp[:, j:j + nc] for j in range(BAND_CHUNKS)], axis=2)
    return band.reshape(bsz, nc, BAND, nh, dh)


def chunked_rel_attention(q, k, v, rel_table):
    bsz, seq, nh, dh = q.shape
    nc = seq // CHUNK
    qc = q.reshape(bsz, nc, CHUNK, nh, dh)
    kb = chunk_band(k)
    vb = chunk_band(v)
    scores = jnp.einsum('bnqhd,bnkhd->bnhqk', qc, kb,
                        preferred_element_type=jnp.float32) * (dh ** -0.5)
    q_off = jnp.arange(CHUNK)[:, None]
    k_off = jnp.arange(BAND)[None, :]
    dist = LEFT_CHUNKS * CHUNK + q_off - k_off
    bias = rel_table[:, jnp.clip(dist, REL_MIN, REL_MAX) - REL_MIN]
    key_chunk = jnp.arange(nc)[:, None] - LEFT_CHUNKS + jnp.arange(BAND_CHUNKS)[None, :]
    valid = jnp.repeat(key_chunk >= 0, CHUNK, axis=1)
    scores = scores + bias.astype(jnp.float32)[None, None]
    scores = jnp.where(valid[None, :, None, None, :], scores, NEG_INF)
    probs = jax.nn.softmax(scores, axis=-1).astype(v.dtype)
    out = jnp.einsum('bnhqk,bnkhd->bnqhd', probs, vb)
    return out.reshape(bsz, seq, nh * dh)


def memory_attention(q, mk, mv):
    bsz, seq, nh, dh = q.shape
    scores = jnp.einsum('bshd,bmhd->bhsm', q, mk,
                        preferred_element_type=jnp.float32) * (dh ** -0.5)
    probs = jax.nn.softmax(scores, axis=-1).astype(mv.dtype)
    out = jnp.einsum('bhsm,bmhd->bshd', probs, mv)
    return out.reshape(bsz, seq, nh * dh)


def setup_inputs(seed: int = 0) -> dict:
    key = jax.random.key(seed)
    keys = iter(jax.random.split(key, 48))
    f32 = jnp.float32
    L = DEPTH
    D = D_MODEL

    def nrm(shape, scale):
        return jax.random.normal(next(keys), shape, f32) * scale

    def gain(shape):
        return 1.0 + nrm(shape, 0.02)

    def unif(shape, lo, hi):
        return jax.random.uniform(next(keys), shape, f32, lo, hi)

    return {
        'x': nrm((BATCH, SEQ, D), 1.0),
        'mem': nrm((BATCH, MEM_TOKENS, D), 1.0),
        'norm_ffn1': gain((L, D)),
        'ffn1_w_in': nrm((L, D, 2 * D_FF), D ** -0.5),
        'ffn1_w_out': nrm((L, D_FF, D), D_FF ** -0.5),
        'norm_mix': gain((L, D)),
        'w_in': nrm((L, D, D_IN), D ** -0.5),
        'shift_mu': unif((L, RWKV_IN), 0.0, 1.0),
        'decay_w0': unif((L, RWKV_WIDTH), -3.0, 0.0),
        'decay_lora_b': nrm((L, DECAY_RANK, RWKV_WIDTH), 0.5 * DECAY_RANK ** -0.5),
        'iclr_a0': nrm((L, RWKV_WIDTH), 0.5),
        'iclr_lora_b': nrm((L, ICLR_RANK, RWKV_WIDTH), 0.5 * ICLR_RANK ** -0.5),
        'gate_lora_b': nrm((L, GATE_RANK, RWKV_WIDTH), GATE_RANK ** -0.5),
        'rwkv_k_k': 0.85 + nrm((L, RWKV_WIDTH), 0.05),
        'rwkv_k_a': 1.0 + nrm((L, RWKV_WIDTH), 0.05),
        'rwkv_r_k': nrm((L, RWKV_HEADS, RWKV_HEAD_DIM), 0.1),
        'rwkv_gn_g': gain((L, RWKV_WIDTH)),
        'rwkv_gn_b': nrm((L, RWKV_WIDTH), 0.02),
        'vres_v0': nrm((L - 1, RWKV_WIDTH), 0.5),
        'vres_lora_a': nrm((L - 1, D, VRES_RANK), D ** -0.5),
        'vres_lora_b': nrm((L - 1, VRES_RANK, RWKV_WIDTH), VRES_RANK ** -0.5),
        'att_q_norm': gain((L, ATT_HEAD_DIM)),
        'att_k_norm': gain((L, ATT_HEAD_DIM)),
        'att_rel_bias': nrm((L, ATT_HEADS, N_REL), 0.5),
        'norm_mem': gain((L, D)),
        'mem_w_kv': nrm((L, D, 2 * MEM_WIDTH), D ** -0.5),
        'mem_q_norm': gain((L, MEM_HEAD_DIM)),
        'mem_k_norm': gain((L, MEM_HEAD_DIM)),
        'w_branch_rwkv': nrm((L, RWKV_WIDTH, D), RWKV_WIDTH ** -0.5),
        'w_branch_att': nrm((L, ATT_WIDTH, D), ATT_WIDTH ** -0.5),
        'w_branch_mem': nrm((L, MEM_WIDTH, D), MEM_WIDTH ** -0.5),
        'w_gate': nrm((L, D, N_BRANCHES * D), D ** -0.5),
        'b_gate': nrm((L, N_BRANCHES * D), 0.02),
        'w_out': nrm((L, D, D), D ** -0.5),
        'norm_ffn2': gain((L, D)),
        'ffn2_w_in': nrm((L, D, 2 * D_FF), D ** -0.5),
        'ffn2_w_out': nrm((L, D_FF, D), D_FF ** -0.5),
    }


def reference(x, mem, norm_ffn1, ffn1_w_in, ffn1_w_out, norm_mix, w_in, shift_mu, decay_w0,
              decay_lora_b, iclr_a0, iclr_lora_b, gate_lora_b, rwkv_k_k, rwkv_k_a, rwkv_r_k,
              rwkv_gn_g, rwkv_gn_b, vres_v0, vres_lora_a, vres_lora_b, att_q_norm, att_k_norm,
              att_rel_bias, norm_mem, mem_w_kv, mem_q_norm, mem_k_norm, w_branch_rwkv,
              w_branch_att, w_branch_mem, w_gate, b_gate, w_out, norm_ffn2, ffn2_w_in, ffn2_w_out):
    v_first = None
    for l in range(DEPTH):
        x = x + 0.5 * swiglu(rms_norm(x, norm_ffn1[l]), ffn1_w_in[l], ffn1_w_out[l])

        h = rms_norm(x, norm_mix[l])
        proj = h @ w_in[l]
        p_rwkv = proj[..., :RWKV_IN]
        p_att = proj[..., RWKV_IN:RWKV_IN + ATT_IN]
        p_mem = proj[..., RWKV_IN + ATT_IN:]

        use_vres = l > 0
        li = max(l - 1, 0)
        y_rwkv, v_first = rwkv7_time_mix(
            p_rwkv, h, shift_mu[l], decay_w0[l], decay_lora_b[l], iclr_a0[l], iclr_lora_b[l],
            gate_lora_b[l], rwkv_k_k[l], rwkv_k_a[l], rwkv_r_k[l], rwkv_gn_g[l], rwkv_gn_b[l],
            v_first, vres_v0[li], vres_lora_a[li], vres_lora_b[li], use_vres)

        aq, ak, av = jnp.split(p_att, 3, axis=-1)
        aq = rms_norm(split_heads(aq, ATT_HEADS), att_q_norm[l])
        ak = rms_norm(split_heads(ak, ATT_HEADS), att_k_norm[l])
        av = split_heads(av, ATT_HEADS)
        y_att = chunked_rel_attention(aq, ak, av, att_rel_bias[l])

        mkv = rms_norm(mem, norm_mem[l]) @ mem_w_kv[l]
        mk, mv = jnp.split(mkv, 2, axis=-1)
        mk = rms_norm(split_heads(mk, MEM_HEADS), mem_k_norm[l])
        mv = split_heads(mv, MEM_HEADS)
        mq = rms_norm(split_heads(p_mem, MEM_HEADS), mem_q_norm[l])
        y_mem = memory_attention(mq, mk, mv)

        g_rwkv, g_att, g_mem = jnp.split(jax.nn.sigmoid(h @ w_gate[l] + b_gate[l]), N_BRANCHES, axis=-1)
        merged = (g_rwkv * (y_rwkv @ w_branch_rwkv[l])
                  + g_att * (y_att @ w_branch_att[l])
                  + g_mem * (y_mem @ w_branch_mem[l]))
        x = x + merged @ w_out[l]

        x = x + 0.5 * swiglu(rms_norm(x, norm_ffn2[l]), ffn2_w_in[l], ffn2_w_out[l])
    return x
```

```python
import contextlib
import numpy as np
import concourse.bass as bass
import concourse.mybir as mybir
from concourse.bass_utils import run_bass_kernel_spmd

F32 = mybir.dt.float32
BF16 = mybir.dt.bfloat16
AF = mybir.ActivationFunctionType
ALU = mybir.AluOpType
AX = mybir.AxisListType

D = 1024
DFF = 2816
NJ = DFF // 128
RWKV_IN = 1792
D_IN = 3840
NH = 8
HD = 64
CH = 64
MEMT = 256
RMS_EPS = 1e-6
GN_EPS = 64e-5

PV = {}
_off = 0
for _n, _w in [("norm_ffn1", 8), ("norm_mix", 8), ("norm_ffn2", 8), ("shift_mu", 14), ("decay_w0", 4),
               ("iclr_a0", 4), ("k_k", 4), ("k_a", 4), ("r_k", 4), ("gn_g", 4), ("gn_b", 4), ("vres_v0", 4),
               ("att_q_norm", 1), ("att_k_norm", 1), ("mem_q_norm", 1), ("mem_k_norm", 1), ("norm_mem", 8),
               ("b_gate", 24)]:
    PV[_n] = _off
    _off += _w
NPV = _off


_UID = [0]


def uniq(name):
    _UID[0] += 1
    return "%s_u%d" % (name, _UID[0])


class Buf:
    __slots__ = ("name", "w", "r", "const", "sem", "sem_sw")

    def __init__(self, name, const=False):
        self.name = name
        self.w = None
        self.r = {}
        self.const = const
        self.sem = None
        self.sem_sw = None


class Ctx:
    def __init__(self, nc, n_dma_sems=36):
        self.nc = nc
        self.es = contextlib.ExitStack()
        self.eng = {"pe": nc.tensor, "act": nc.scalar, "dve": nc.vector, "pool": nc.gpsimd, "sp": nc.sync}
        self.sems = {}
        self.cnt = {}
        for e in ["pe", "act", "dve", "pool", "sp"]:
            self.sems[e] = self.es.enter_context(nc.semaphore("prog_" + e))
            self.cnt[e] = 0
        self.known = {e: {} for e in self.eng}
        self.free_dma = []
        for i in range(n_dma_sems):
            k = "dma%d" % i
            self.sems[k] = self.es.enter_context(nc.semaphore(k))
            self.cnt[k] = 0
            self.free_dma.append(k)
        self.ninst = 0
        self.phase_sems = []
        self.free_sw = []
        self.free_cc = []
        for i in range(12):
            k = "cc%d" % i
            self.sems[k] = self.es.enter_context(nc.semaphore(k))
            self.cnt[k] = 0
            self.free_cc.append(k)
        for i in range(48):
            k = "swd%d" % i
            self.sems[k] = self.es.enter_context(nc.semaphore(k))
            self.cnt[k] = 0
            self.free_sw.append(k)

    def dma_sem(self):
        return self.free_dma.pop(0)

    def release_dma_sem(self, k):
        self.free_dma.append(k)

    def _wait(self, e, deps):
        eng = self.eng[e]
        kn = self.known[e]
        best = {}
        for (s, v) in deps:
            if s == "pe" and e == "pe":
                continue
            if kn.get(s, 0) < v and best.get(s, 0) < v:
                best[s] = v
        for s, v in best.items():
            eng.wait_ge(self.sems[s], v)
            kn[s] = v

    def op(self, e, fn, reads=(), writes=(), dma=None, signal=True, coll=False):
        deps = []
        for b in reads:
            if b.w is not None:
                deps.append(b.w)
        for b in writes:
            if b.w is not None:
                deps.append(b.w)
            deps.extend(b.r.items())
        self._wait(e, deps)
        inst = fn(self.eng[e])
        if coll:
            k = self.free_cc.pop(0)
            self.cnt[k] += 1
            inst.then_inc(self.sems[k], 1)
            tok = (k, self.cnt[k])
        elif dma is not None:
            if e == "pool":
                if dma.sem_sw is None:
                    dma.sem_sw = self.free_sw.pop(0)
                    self.phase_sems.append(dma)
                dma = dma.sem_sw
            else:
                if dma.sem is None:
                    dma.sem = self.free_dma.pop(0)
                    self.phase_sems.append(dma)
                dma = dma.sem
            self.cnt[dma] += 16
            inst.then_inc(self.sems[dma], 16)
            tok = (dma, self.cnt[dma])
        elif signal:
            self.cnt[e] += 1
            inst.then_inc(self.sems[e], 1)
            tok = (e, self.cnt[e])
        else:
            tok = (e, self.cnt[e] + 1)
        for b in reads:
            if not b.const:
                if b.r.get(tok[0], 0) < tok[1]:
                    b.r[tok[0]] = tok[1]
        for b in writes:
            b.w = tok
            b.r = {}
        self.ninst += 1
        return tok

    def barrier(self):
        allk = [(k, v) for k, v in self.cnt.items() if v > 0]
        for e in self.eng:
            self._wait(e, [(k, v) for (k, v) in allk if not (k == e)])

    def end_phase(self):
        self.barrier()
        for b in self.phase_sems:
            if b.sem is not None:
                self.free_dma.append(b.sem)
                b.sem = None
            if b.sem_sw is not None:
                self.free_sw.append(b.sem_sw)
                b.sem_sw = None
        self.phase_sems = []

    def keep_sems(self):
        self.phase_sems = []

    def final_wait(self):
        allk = [(k, v) for k, v in self.cnt.items() if v > 0]
        self._wait("sp", [(k, v) for (k, v) in allk if k != "sp"])


class Ring:
    def __init__(self, cx, es, name, shape, dtype, n):
        self.cx = cx
        self.slots = []
        for i in range(n):
            t = es.enter_context(cx.nc.sbuf_tensor(uniq("%s%d" % (name, i)), shape, dtype))
            bf = Buf("%s%d" % (name, i))
            self.slots.append((t, bf, bf))
        self.i = 0

    def next(self):
        s = self.slots[self.i % len(self.slots)]
        self.i += 1
        return s


class DramT:
    def __init__(self, nc, name, shape, dtype, kind="Internal", gran=512):
        self.t = nc.dram_tensor(name, shape, dtype, kind=kind).ap()
        self.gran = gran
        self.name = name
        self.bufs = {}

    def b(self, t0, t1):
        out = []
        for g in range(t0 // self.gran, (t1 - 1) // self.gran + 1):
            if g not in self.bufs:
                self.bufs[g] = Buf("%s_%d" % (self.name, g))
            out.append(self.bufs[g])
        return out


def fm_layout(W):
    K, N = W.shape
    return np.ascontiguousarray(W.reshape(K // 128, 128, N // 128, 128).transpose(2, 1, 0, 3))


def mv_layout(W):
    K, N = W.shape
    return np.ascontiguousarray(W.reshape(K // 128, 128, N).transpose(1, 0, 2))


class Builder:
    def __init__(self, T, L, debug=False):
        self.T = T
        self.L = L
        self.debug = debug
        nc = self.nc = bass.Bass("TRN2", target_bir_lowering=False)
        cx = self.cx = Ctx(nc)
        es = self.es = cx.es
        self.xin = DramT(nc, "xT", [8, 128, T], F32, kind="ExternalInput")
        self.out = DramT(nc, "outT", [8, 128, T], F32, kind="ExternalOutput")
        self.XS = DramT(nc, "XS", [8, 128, T], F32)
        self.win = {}
        self.wbf = {}
        self.wbuf = {}

        def wdecl(name, shape, cast=True):
            self.win[name] = nc.dram_tensor(name, shape, F32, kind="ExternalInput").ap()
            if cast:
                self.wbf[name] = nc.dram_tensor(name + "_bf", shape, BF16, kind="Internal").ap()
                if name.startswith("ffn"):
                    self.wbuf[name] = [Buf("%s_l%d" % (name, i)) for i in range(shape[0])]
                else:
                    if not hasattr(self, "_restbuf"):
                        self._restbuf = [Buf("wrest_l%d" % i) for i in range(shape[0])]
                    self.wbuf[name] = self._restbuf
        self.wdecl = wdecl
        wdecl("ffn_wi", [L, 2, NJ, 128, 8 * 256])
        wdecl("ffn_wo", [L, 2, 8, 128, NJ * 128])
        wdecl("pvec", [L, 128, NPV], cast=False)
        wdecl("consts", [128, 1024], cast=False)
        self.pvec = es.enter_context(nc.sbuf_tensor("sb_pvec", [128, L, NPV], F32))
        self.pvb = Buf("pvec", const=True)
        self.consts = es.enter_context(nc.sbuf_tensor("sb_consts", [128, 1024], F32))
        self.cb = Buf("consts", const=True)
        cx.op("sp", lambda e: e.dma_start(out=self.pvec[:], in_=self.win["pvec"].rearrange("l p n -> p l n")),
              writes=[self.pvb], dma=self.pvb)
        cx.op("sp", lambda e: e.dma_start(out=self.consts[:], in_=self.win["consts"]), writes=[self.cb], dma=self.cb)
        cx.keep_sems()
        self.ones = self.consts[:, 0:128]
        self.ident = self.consts[:, 128:256]
        self.blk64 = self.consts[:, 256:384]
        self.eps_rms = self.consts[:, 384:385]
        self.eps_q = self.consts[:, 385:386]
        self.eps_mq = self.consts[:, 386:387]
        self.eps_gn = self.consts[:, 387:388]
        self.cmask = self.consts[:, 512:1024]
        self.ps = []
        self.psb = []
        for i in range(8):
            self.ps.append(es.enter_context(nc.psum_tensor("ps%d" % i, [128, 512], F32)))
            self.psb.append(Buf("ps%d" % i))

    def pv(self, l, name, i=0, n=1):
        c = PV[name] + i
        return self.pvec[:, l, c:c + n]

    def cast_weights(self):
        cx = self.cx
        for l in range(self.L):
            for name, dst in self.wbf.items():
                src = self.win[name]
                wb = self.wbuf[name][l]
                cx.op("pool", lambda e: e.dma_start(out=dst[l], in_=src[l]), writes=[wb], dma=wb)
                wb.const = True
        cx.keep_sems()

    def rmsnorm_fm(self, xT, xb, cs, gains, xn, xnb, tmp, nft=8, w=512, dim=D):
        cx, ps, psb = self.cx, self.ps, self.psb
        sq, sqb, ss, ssb, rstd, rsb = tmp
        cx.op("act", lambda e: e.activation(out=sq[:, 0:nft, 0:w], in_=xT[:, 0:nft, cs], func=AF.Square),
              reads=[xb], writes=[sqb])
        cx.op("dve", lambda e: e.tensor_reduce(out=ss[:, 0:w], in_=sq[:, 0:nft, 0:w].rearrange("p f t -> p t f"),
                                               axis=AX.X, op=ALU.add), reads=[sqb], writes=[ssb])
        cx.op("pe", lambda e: e.matmul(out=ps[6][:, 0:w], lhsT=self.ones, rhs=ss[:, 0:w], start=True, stop=True),
              reads=[ssb, self.cb], writes=[psb[6]])
        cx.op("act", lambda e: e.activation(out=rstd[:, 0:w], in_=ps[6][:, 0:w], func=AF.Ln,
                                            scale=1.0 / dim, bias=self.eps_rms), reads=[psb[6], self.cb], writes=[rsb])
        cx.op("act", lambda e: e.activation(out=rstd[:, 0:w], in_=rstd[:, 0:w], func=AF.Exp, scale=-0.5), reads=[rsb], writes=[rsb])
        for ft in range(nft):
            cx.op("dve", lambda e: e.scalar_tensor_tensor(
                out=xn[:, ft, cs], in0=xT[:, ft, cs], scalar=gains(ft), in1=rstd[:, 0:w],
                op0=ALU.mult, op1=ALU.mult), reads=[xb, rsb, self.pvb], writes=[xnb])

    def norm_tmp(self, es, pfx):
        nc = self.nc
        sq = es.enter_context(nc.sbuf_tensor(uniq(pfx + "_sq"), [128, 8, 512], F32))
        ss = es.enter_context(nc.sbuf_tensor(uniq(pfx + "_ss"), [128, 512], F32))
        rstd = es.enter_context(nc.sbuf_tensor(uniq(pfx + "_rstd"), [128, 512], F32))
        return (sq, Buf(pfx + "_sq"), ss, Buf(pfx + "_ss"), rstd, Buf(pfx + "_rstd"))

    def phase_ffn(self, l, f, src, dst, gname):
        cx, nc, T = self.cx, self.nc, self.T
        TB = min(1024, T)
        NTC = TB // 512
        ps, psb = self.ps, self.psb
        with contextlib.ExitStack() as es:
            xT = es.enter_context(nc.sbuf_tensor(uniq("f_xT"), [128, 8, TB], F32))
            xn = es.enter_context(nc.sbuf_tensor(uniq("f_xn"), [128, 8, TB], BF16))
            hT = es.enter_context(nc.sbuf_tensor(uniq("f_hT"), [128, NJ, TB], BF16))
            ntmp = self.norm_tmp(es, "f")
            sg = [es.enter_context(nc.sbuf_tensor(uniq("f_sg%d" % i), [128, 512], F32)) for i in range(2)]
            xb = [Buf("f_xT%d" % i) for i in range(NTC)]
            xnb = [Buf("f_xn%d" % i) for i in range(NTC)]
            hb = [Buf("f_hT%d" % i) for i in range(NTC)]
            sgb = [Buf("f_sg0"), Buf("f_sg1")]
            w1r = Ring(cx, es, "f_w1", [128, 8, 256], BF16, 3)
            wor = Ring(cx, es, "f_wo", [128, NJ, 128], BF16, 2)
            w1d = self.wbf["ffn_wi"]
            wod = self.wbf["ffn_wo"]
            it = 0
            for blk in range(T // TB):
                t0 = blk * TB
                for tc in range(NTC):
                    a, b = t0 + tc * 512, t0 + (tc + 1) * 512
                    cs = slice(tc * 512, (tc + 1) * 512)
                    cx.op("sp", lambda e: e.dma_start(out=xT[:, :, cs], in_=src.t[:, :, a:b].rearrange("f p t -> p f t")),
                          reads=src.b(a, b), writes=[xb[tc]], dma=xb[tc])
                    self.rmsnorm_fm(xT, xb[tc], cs, lambda ft: self.pv(l, gname, ft), xn, xnb[tc], ntmp)
                for j in range(NJ):
                    w1, w1b, w1k = w1r.next()
                    cx.op("sp", lambda e: e.dma_start(out=w1[:], in_=w1d[l, f, j].rearrange("p (k c) -> p k c", c=256)),
                          reads=[self.wbuf["ffn_wi"][l]], writes=[w1b], dma=w1k)
                    for tc in range(NTC):
                        cs = slice(tc * 512, (tc + 1) * 512)
                        pg, pu = it % 2, 2 + it % 2
                        for kt in range(8):
                            cx.op("pe", lambda e: e.matmul(out=ps[pg][:], lhsT=w1[:, kt, 0:128], rhs=xn[:, kt, cs],
                                                           start=(kt == 0), stop=(kt == 7)),
                                  reads=[w1b, xnb[tc]], writes=[psb[pg]], signal=(kt == 7))
                        for kt in range(8):
                            cx.op("pe", lambda e: e.matmul(out=ps[pu][:], lhsT=w1[:, kt, 128:256], rhs=xn[:, kt, cs],
                                                           start=(kt == 0), stop=(kt == 7)),
                                  reads=[w1b, xnb[tc]], writes=[psb[pu]], signal=(kt == 7))
                        s = it % 2
                        cx.op("act", lambda e: e.activation(out=sg[s][:], in_=ps[pg][:], func=AF.Silu),
                              reads=[psb[pg]], writes=[sgb[s]])
                        cx.op("dve", lambda e: e.tensor_tensor(out=hT[:, j, cs], in0=sg[s][:], in1=ps[pu][:], op=ALU.mult),
                              reads=[sgb[s], psb[pu]], writes=[hb[tc]])
                        it += 1
                for o in range(8):
                    wo, wob, wok = wor.next()
                    cx.op("sp", lambda e: e.dma_start(out=wo[:], in_=wod[l, f, o].rearrange("p (k c) -> p k c", c=128)),
                          reads=[self.wbuf["ffn_wo"][l]], writes=[wob], dma=wok)
                    for tc in range(NTC):
                        cs = slice(tc * 512, (tc + 1) * 512)
                        po = 4 + it % 2
                        it += 1
                        for kt in range(NJ):
                            cx.op("pe", lambda e: e.matmul(out=ps[po][:], lhsT=wo[:, kt, :], rhs=hT[:, kt, cs],
                                                           start=(kt == 0), stop=(kt == NJ - 1)),
                                  reads=[wob, hb[tc]], writes=[psb[po]], signal=(kt == NJ - 1))
                        cx.op("dve", lambda e: e.scalar_tensor_tensor(out=xT[:, o, cs], in0=ps[po][:], scalar=0.5,
                                                                      in1=xT[:, o, cs], op0=ALU.mult, op1=ALU.add),
                              reads=[psb[po], xb[tc]], writes=[xb[tc]])
                for tc in range(NTC):
                    a, b = t0 + tc * 512, t0 + (tc + 1) * 512
                    cs = slice(tc * 512, (tc + 1) * 512)
                    cx.op("pool", lambda e: e.dma_start(out=dst.t[:, :, a:b].rearrange("f p t -> p f t"), in_=xT[:, :, cs]),
                          reads=[xb[tc]], writes=dst.b(a, b), dma=xb[tc])
            cx.end_phase()


def make_consts():
    c = np.zeros((128, 1024), np.float32)
    c[:, 0:128] = 1.0
    c[:, 128:256] = np.eye(128, dtype=np.float32)
    blk = np.zeros((128, 128), np.float32)
    blk[:64, :64] = 1.0
    blk[64:, 64:] = 1.0
    c[:, 256:384] = blk
    c[:, 384] = RMS_EPS
    c[:, 385] = 64 * RMS_EPS
    c[:, 386] = 128 * RMS_EPS
    c[:, 387] = GN_EPS
    c[:, 512:1024] = 1.0
    c[:, 512:1024:64] = 0.0
    return c


def prep_shared(inp, L):
    g = {k: np.asarray(v, np.float32) for k, v in inp.items() if k not in ("x", "mem")}
    sh = {}
    wi = np.zeros((L, 2, NJ, 128, 8 * 256), np.float32)
    wo = np.zeros((L, 2, 8, 128, NJ * 128), np.float32)
    for l in range(L):
        for f, (a, b) in enumerate([("ffn1_w_in", "ffn1_w_out"), ("ffn2_w_in", "ffn2_w_out")]):
            W = g[a][l]
            G = fm_layout(W[:, :DFF])
            U = fm_layout(W[:, DFF:])
            wi[l, f] = np.concatenate([G, U], axis=3).reshape(NJ, 128, 8 * 256)
            wo[l, f] = fm_layout(g[b][l]).reshape(8, 128, NJ * 128)
    sh["ffn_wi"] = wi
    sh["ffn_wo"] = wo
    pv = np.zeros((L, 128, NPV), np.float32)

    def put(l, name, vec):
        n = vec.size // 128
        pv[l, :, PV[name]:PV[name] + n] = vec.reshape(n, 128).T
    for l in range(L):
        put(l, "norm_ffn1", g["norm_ffn1"][l])
        put(l, "norm_mix", g["norm_mix"][l])
        put(l, "norm_ffn2", g["norm_ffn2"][l])
        put(l, "shift_mu", g["shift_mu"][l])
        put(l, "decay_w0", g["decay_w0"][l])
        put(l, "iclr_a0", g["iclr_a0"][l])
        put(l, "k_k", g["rwkv_k_k"][l])
        put(l, "k_a", g["rwkv_k_a"][l])
        put(l, "r_k", g["rwkv_r_k"][l].reshape(-1))
        put(l, "gn_g", g["rwkv_gn_g"][l])
        put(l, "gn_b", g["rwkv_gn_b"][l])
        if l > 0:
            put(l, "vres_v0", g["vres_v0"][l - 1])
        put(l, "att_q_norm", np.tile(g["att_q_norm"][l], 2))
        put(l, "att_k_norm", np.tile(g["att_k_norm"][l], 2))
        put(l, "mem_q_norm", g["mem_q_norm"][l])
        put(l, "mem_k_norm", g["mem_k_norm"][l])
        put(l, "norm_mem", g["norm_mem"][l])
        put(l, "b_gate", g["b_gate"][l])
    sh["pvec"] = pv
    sh["consts"] = make_consts()
    return sh


def to_fm(x):
    T = x.shape[0]
    return np.ascontiguousarray(x.T.reshape(x.shape[1] // 128, 128, T))


def from_fm(y):
    n, p, T = y.shape
    return np.ascontiguousarray(y.reshape(n * p, T).T)


def _decl_mix(self):
    L, T, nc = self.L, self.T, self.nc
    self.wdecl("wmix", [L, 26, 128, 8 * 128])
    self.wdecl("wvres", [L, 128, 8 * 32])
    self.wdecl("wv", [L, 128, 8 * 512])
    self.wdecl("wmk", [L, 4, 128, 8 * 128])
    self.wdecl("wmv", [L, 128, 8 * 512])
    self.memT = DramT(nc, "memT", [8, 128, MEMT], F32, kind="ExternalInput")
    self.HN = DramT(nc, "HN", [8, 128, T], BF16)
    self.PR = DramT(nc, "PR", [15, 128, T], F32)
    self.QT = DramT(nc, "QT", [4, 128, T], BF16)
    self.KT = DramT(nc, "KT", [4, 128, T], BF16)
    self.V1 = DramT(nc, "V1", [T // 128, 128, 520], BF16)
    self.YM = DramT(nc, "YM", [4, 128, T], BF16)
    self.YA = DramT(nc, "YA", [4, 128, T], BF16)
    self.YR = DramT(nc, "YR", [4, 128, T], BF16)
    self.VF = DramT(nc, "VF", [4, 128, T], F32)


def _phase_proj(self, l, src, hook=None):
    cx, nc, T = self.cx, self.nc, self.T
    TB = min(1024, T)
    NTC = TB // 512
    ps, psb = self.ps, self.psb
    with contextlib.ExitStack() as es:
        sb = lambda n, sh, dt: es.enter_context(nc.sbuf_tensor(uniq("p_" + n), sh, dt))
        xT = sb("xT", [128, 8, TB], F32)
        hn = sb("hn", [128, 8, TB], BF16)
        xb = [Buf("p_xT%d" % i) for i in range(NTC)]
        hb = [Buf("p_hn%d" % i) for i in range(NTC)]
        ntmp = self.norm_tmp(es, "p")
        wr = Ring(cx, es, "p_w", [128, 8, 128], BF16, 3)
        stf = Ring(cx, es, "p_stf", [128, 512], F32, 3)
        stb = Ring(cx, es, "p_stb", [128, 512], BF16, 3)
        stv = Ring(cx, es, "p_stv", [128, 8, 65], BF16, 2)
        wv = sb("wv", [128, 8, 512], BF16)
        wvb = Buf("p_wv")
        wvr = sb("wvr", [128, 8, 32], BF16)
        wvrb = Buf("p_wvr")
        sq2 = [sb("sq2_%d" % i, [128, 512], F32) for i in range(2)]
        sq2b = [Buf("p_sq2_%d" % i) for i in range(2)]
        rs2 = [sb("rs2_%d" % i, [128, 512], F32) for i in range(2)]
        rs2b = [Buf("p_rs2_%d" % i) for i in range(2)]
        mqn = sb("mqn", [128, 512], BF16)
        mqnb = Buf("p_mqn")
        esc = sb("esc", [128, 2, 512], BF16)
        escb = Buf("p_esc")
        rden = sb("rden", [128, 512], F32)
        rdenb = Buf("p_rden")
        onesb = sb("onesb", [128, 128], BF16)
        onesbb = Buf("p_onesb")
        memx = sb("memx", [128, 8, MEMT], F32)
        memn = sb("memn", [128, 8, MEMT], BF16)
        memxb, memnb = Buf("p_memx"), Buf("p_memn")
        mkT = sb("mkT", [128, 4, MEMT], BF16)
        mv = sb("mv", [128, 2, 512], BF16)
        mkb, mvb = Buf("p_mkT"), Buf("p_mv")
        wmv = sb("wmv", [128, 8, 512], BF16)
        wmvb = Buf("p_wmv")
        cx.op("dve", lambda e: e.tensor_copy(out=onesb[:], in_=self.ones), reads=[self.cb], writes=[onesbb])
        for s_ in stv.slots:
            cx.op("dve", lambda e: e.memset(s_[0][:, :, 64:65], 1.0), writes=[s_[1]])
        cx.op("sp", lambda e: e.dma_start(out=wv[:], in_=self.wbf["wv"][l].rearrange("p (k c) -> p k c", c=512)),
              reads=[self.wbuf["wv"][l]], writes=[wvb], dma=wvb)
        cx.op("sp", lambda e: e.dma_start(out=wvr[:], in_=self.wbf["wvres"][l].rearrange("p (k c) -> p k c", c=32)),
              reads=[self.wbuf["wvres"][l]], writes=[wvrb], dma=wvrb)
        cx.op("sp", lambda e: e.dma_start(out=wmv[:], in_=self.wbf["wmv"][l].rearrange("p (k c) -> p k c", c=512)),
              reads=[self.wbuf["wmv"][l]], writes=[wmvb], dma=wmvb)
        cx.op("sp", lambda e: e.dma_start(out=memx[:], in_=self.memT.t.rearrange("f p t -> p f t")),
              writes=[memxb], dma=memxb)
        self.rmsnorm_fm(memx, memxb, slice(0, MEMT), lambda ft: self.pv(l, "norm_mem", ft), memn, memnb, ntmp, w=MEMT)
        itp = 0
        for h in range(4):
            w, wb, wk = wr.next()
            cx.op("sp", lambda e: e.dma_start(out=w[:], in_=self.wbf["wmk"][l, h].rearrange("p (k c) -> p k c", c=128)),
                  reads=[self.wbuf["wmk"][l]], writes=[wb], dma=wk)
            for kt in range(8):
                cx.op("pe", lambda e: e.matmul(out=ps[0][:, 0:MEMT], lhsT=w[:, kt, :], rhs=memn[:, kt, :],
                                               start=(kt == 0), stop=(kt == 7)), reads=[wb, memnb], writes=[psb[0]],
                      signal=(kt == 7))
            cx.op("act", lambda e: e.activation(out=sq2[0][:, 0:MEMT], in_=ps[0][:, 0:MEMT], func=AF.Square),
                  reads=[psb[0]], writes=[sq2b[0]])
            cx.op("pe", lambda e: e.matmul(out=ps[2][:, 0:MEMT], lhsT=self.ones, rhs=sq2[0][:, 0:MEMT], start=True, stop=True),
                  reads=[sq2b[0], self.cb], writes=[psb[2]])
            cx.op("act", lambda e: e.activation(out=rs2[0][:, 0:MEMT], in_=ps[2][:, 0:MEMT], func=AF.Sqrt,
                                                scale=1.0 / 128, bias=self.eps_rms), reads=[psb[2], self.cb], writes=[rs2b[0]])
            cx.op("dve", lambda e: e.reciprocal(out=rs2[0][:, 0:MEMT], in_=rs2[0][:, 0:MEMT]), reads=[rs2b[0]], writes=[rs2b[0]])
            cx.op("dve", lambda e: e.scalar_tensor_tensor(out=mkT[:, h, :], in0=ps[0][:, 0:MEMT], scalar=self.pv(l, "mem_k_norm"),
                                                          in1=rs2[0][:, 0:MEMT], op0=ALU.mult, op1=ALU.mult),
                  reads=[psb[0], rs2b[0], self.pvb], writes=[mkb])
        for mt in range(2):
            for kt in range(8):
                cx.op("pe", lambda e: e.matmul(out=ps[1][:], lhsT=memn[:, kt, mt * 128:(mt + 1) * 128], rhs=wmv[:, kt, :],
                                               start=(kt == 0), stop=(kt == 7)), reads=[wmvb, memnb], writes=[psb[1]],
                      signal=(kt == 7))
            cx.op("act", lambda e: e.activation(out=mv[:, mt, :], in_=ps[1][:], func=AF.Copy), reads=[psb[1]], writes=[mvb])
        order = list(range(T // TB))
        if hook is not None:
            order = order[::-1]
        for bidx, blk in enumerate(order):
            if bidx == 1 and hook is not None:
                hook()
            t0 = blk * TB
            for tc in range(NTC):
                a, b = t0 + tc * 512, t0 + (tc + 1) * 512
                cs = slice(tc * 512, (tc + 1) * 512)
                cx.op("sp", lambda e: e.dma_start(out=xT[:, :, cs], in_=src.t[:, :, a:b].rearrange("f p t -> p f t")),
                      reads=src.b(a, b), writes=[xb[tc]], dma=xb[tc])
                self.rmsnorm_fm(xT, xb[tc], cs, lambda ft: self.pv(l, "norm_mix", ft), hn, hb[tc], ntmp)
                cx.op("act", lambda e: e.dma_start(out=self.HN.t[:, :, a:b].rearrange("f p t -> p f t"), in_=hn[:, :, cs]),
                      reads=[hb[tc]], writes=self.HN.b(a, b), dma=hb[tc])
            for j in range(26):
                w, wb, wk = wr.next()
                cx.op("sp", lambda e: e.dma_start(out=w[:], in_=self.wbf["wmix"][l, j].rearrange("p (k c) -> p k c", c=128)),
                      reads=[self.wbuf["wmix"][l]], writes=[wb], dma=wk)
                for tc in range(NTC):
                    a, b = t0 + tc * 512, t0 + (tc + 1) * 512
                    cs = slice(tc * 512, (tc + 1) * 512)
                    pb = itp % 2
                    itp += 1
                    for kt in range(8):
                        cx.op("pe", lambda e: e.matmul(out=ps[pb][:], lhsT=w[:, kt, :], rhs=hn[:, kt, cs],
                                                       start=(kt == 0), stop=(kt == 7)), reads=[wb, hb[tc]], writes=[psb[pb]],
                              signal=(kt == 7))
                    if j < 14:
                        st, stbuf, stk = stf.next()
                        cx.op("act", lambda e: e.activation(out=st[:], in_=ps[pb][:], func=AF.Copy), reads=[psb[pb]], writes=[stbuf])
                        if getattr(self, "seg", False) and b == T:
                            cx.op("act", lambda e: e.activation(out=self.shcol[:, j:j + 1], in_=st[:, 511:512], func=AF.Copy),
                                  reads=[stbuf], writes=[self.shcolb])
                        cx.op("act", lambda e: e.dma_start(out=self.PR.t[j, :, a:b], in_=st[:]), reads=[stbuf],
                              writes=self.PR.b(a, b), dma=stk)
                        continue
                    i2 = pb
                    cx.op("act", lambda e: e.activation(out=sq2[i2][:], in_=ps[pb][:], func=AF.Square), reads=[psb[pb]], writes=[sq2b[i2]])
                    red = self.ones if j >= 22 else self.blk64
                    cx.op("pe", lambda e: e.matmul(out=ps[2 + i2][:], lhsT=red, rhs=sq2[i2][:], start=True, stop=True),
                          reads=[sq2b[i2], self.cb], writes=[psb[2 + i2]])
                    if j < 18:
                        sc_, bi_, gn_ = 1.0, self.eps_q, "att_q_norm"
                    elif j < 22:
                        sc_, bi_, gn_ = 1.0 / 64, self.eps_rms, "att_k_norm"
                    else:
                        sc_, bi_, gn_ = 1.0, self.eps_mq, "mem_q_norm"
                    cx.op("act", lambda e: e.activation(out=rs2[i2][:], in_=ps[2 + i2][:], func=AF.Ln, scale=sc_, bias=bi_),
                          reads=[psb[2 + i2], self.cb], writes=[rs2b[i2]])
                    cx.op("act", lambda e: e.activation(out=rs2[i2][:], in_=rs2[i2][:], func=AF.Exp, scale=-0.5), reads=[rs2b[i2]], writes=[rs2b[i2]])
                    if j < 22:
                        st, stbuf, stk = stb.next()
                        cx.op("dve", lambda e: e.scalar_tensor_tensor(out=st[:], in0=ps[pb][:], scalar=self.pv(l, gn_), in1=rs2[i2][:],
                                                                      op0=ALU.mult, op1=ALU.mult),
                              reads=[psb[pb], rs2b[i2], self.pvb], writes=[stbuf])
                        dstT = self.QT if j < 18 else self.KT
                        jj = (j - 14) % 4
                        cx.op("act", lambda e: e.dma_start(out=dstT.t[jj, :, a:b], in_=st[:]), reads=[stbuf],
                              writes=dstT.b(a, b), dma=stk)
                        continue
                    h = j - 22
                    cx.op("dve", lambda e: e.scalar_tensor_tensor(out=mqn[:], in0=ps[pb][:], scalar=self.pv(l, gn_), in1=rs2[i2][:],
                                                                  op0=ALU.mult, op1=ALU.mult),
                          reads=[psb[pb], rs2b[i2], self.pvb], writes=[mqnb])
                    for mt in range(2):
                        cx.op("pe", lambda e: e.matmul(out=ps[4 + mt][:], lhsT=mkT[:, h, mt * 128:(mt + 1) * 128], rhs=mqn[:],
                                                       start=True, stop=True), reads=[mkb, mqnb], writes=[psb[4 + mt]])
                        cx.op("act", lambda e: e.activation(out=esc[:, mt, :], in_=ps[4 + mt][:], func=AF.Exp),
                              reads=[psb[4 + mt]], writes=[escb])
                    for mt in range(2):
                        cx.op("pe", lambda e: e.matmul(out=ps[7][:], lhsT=onesb[:], rhs=esc[:, mt, :], start=(mt == 0), stop=(mt == 1)),
                              reads=[onesbb, escb], writes=[psb[7]], signal=(mt == 1))
                    for mt in range(2):
                        cx.op("pe", lambda e: e.matmul(out=ps[6][:], lhsT=mv[:, mt, h * 128:(h + 1) * 128], rhs=esc[:, mt, :],
                                                       start=(mt == 0), stop=(mt == 1)), reads=[mvb, escb], writes=[psb[6]],
                              signal=(mt == 1))
                    cx.op("dve", lambda e: e.reciprocal(out=rden[:], in_=ps[7][:]), reads=[psb[7]], writes=[rdenb])
                    st, stbuf, stk = stb.next()
                    cx.op("dve", lambda e: e.tensor_tensor(out=st[:], in0=ps[6][:], in1=rden[:], op=ALU.mult),
                          reads=[psb[6], rdenb], writes=[stbuf])
                    cx.op("act", lambda e: e.dma_start(out=self.YM.t[h, :, a:b], in_=st[:]), reads=[stbuf],
                          writes=self.YM.b(a, b), dma=stk)
            if l > 0:
                for tc in range(NTC):
                    a, b = t0 + tc * 512, t0 + (tc + 1) * 512
                    cs = slice(tc * 512, (tc + 1) * 512)
                    pb = itp % 2
                    itp += 1
                    for kt in range(8):
                        cx.op("pe", lambda e: e.matmul(out=ps[pb][0:32, :], lhsT=wvr[:, kt, :], rhs=hn[:, kt, cs],
                                                       start=(kt == 0), stop=(kt == 7)), reads=[wvrb, hb[tc]], writes=[psb[pb]],
                              signal=(kt == 7))
                    st, stbuf, stk = stf.next()
                    cx.op("act", lambda e: e.activation(out=st[0:32, :], in_=ps[pb][0:32, :], func=AF.Copy), reads=[psb[pb]], writes=[stbuf])
                    cx.op("act", lambda e: e.dma_start(out=self.PR.t[14, 0:32, a:b], in_=st[0:32, :]), reads=[stbuf],
                          writes=self.PR.b(a, b), dma=stk)
            for tt in range(TB // 128):
                tc = tt // 4
                pb = itp % 2
                itp += 1
                for kt in range(8):
                    cx.op("pe", lambda e: e.matmul(out=ps[pb][:], lhsT=hn[:, kt, tt * 128:(tt + 1) * 128], rhs=wv[:, kt, :],
                                                   start=(kt == 0), stop=(kt == 7)), reads=[wvb, hb[tc]], writes=[psb[pb]],
                          signal=(kt == 7))
                st, stbuf, stk = stv.next()
                cx.op("act", lambda e: e.activation(out=st[:, :, 0:64], in_=ps[pb][:].rearrange("p (h d) -> p h d", d=64), func=AF.Copy),
                      reads=[psb[pb]], writes=[stbuf])
                ta = t0 + tt * 128
                cx.op("act", lambda e: e.dma_start(out=self.V1.t[ta // 128], in_=st[:].rearrange("p h d -> p (h d)")),
                      reads=[stbuf], writes=self.V1.b(ta, ta + 128), dma=stk)
        if hook is not None and len(order) == 1:
            hook()
        cx.end_phase()


Builder.decl_mix = _decl_mix
Builder.phase_proj = _phase_proj


def prep_mix(inp, L, sh):
    g = {k: np.asarray(v, np.float32) for k, v in inp.items() if k not in ("x", "mem")}
    wmix = np.zeros((L, 26, 128, 8 * 128), np.float32)
    wvres = np.zeros((L, 128, 8 * 32), np.float32)
    wv = np.zeros((L, 128, 8 * 512), np.float32)
    wmk = np.zeros((L, 4, 128, 8 * 128), np.float32)
    wmv = np.zeros((L, 128, 8 * 512), np.float32)
    for l in range(L):
        W = g["w_in"][l]
        cols = np.concatenate([np.arange(0, 1792), np.arange(1792, 1792 + 1024), np.arange(1792 + 1536, 3840)])
        wmix[l] = fm_layout(W[:, cols]).reshape(26, 128, 1024)
        wv[l] = mv_layout(W[:, 1792 + 1024:1792 + 1536]).reshape(128, 8 * 512)
        if l > 0:
            wvres[l] = mv_layout(g["vres_lora_a"][l - 1]).reshape(128, 8 * 32)
        wmk[l] = fm_layout(g["mem_w_kv"][l][:, :512]).reshape(4, 128, 1024)
        wmv[l] = mv_layout(g["mem_w_kv"][l][:, 512:]).reshape(128, 8 * 512)
    sh.update(wmix=wmix, wvres=wvres, wv=wv, wmk=wmk, wmv=wmv)


def _decl_att(self):
    self.wdecl("relb", [self.L, 128, 5 * 8 * 128], cast=False)
    self.wdecl("amask", [128, 5 * 128], cast=False)


def _phase_att(self, l):
    cx, nc, T = self.cx, self.nc, self.T
    ps, psb = self.ps, self.psb
    QB = 512
    with contextlib.ExitStack() as es:
        sb = lambda n, sh, dt: es.enter_context(nc.sbuf_tensor(uniq("a_" + n), sh, dt))
        M = sb("M", [128, 5, 8, 128], F32)
        Mb = Buf("a_M")
        am = sb("am", [128, 5, 128], F32)
        amb = Buf("a_am")
        cx.op("sp", lambda e: e.dma_start(out=M[:].rearrange("p i h q -> p (i h q)"), in_=self.win["relb"][l]), writes=[Mb], dma=Mb)
        cx.op("sp", lambda e: e.dma_start(out=am[:].rearrange("p i q -> p (i q)"), in_=self.win["amask"]), writes=[amb], dma=amb)
        cx.op("act", lambda e: e.activation(out=M[:].rearrange("p i h q -> p (i h q)"), in_=M[:].rearrange("p i h q -> p (i h q)"), func=AF.Exp),
              reads=[Mb], writes=[Mb])
        for i in range(5):
            cx.op("dve", lambda e: e.tensor_tensor(out=M[:, i], in0=M[:, i], in1=am[:, i:i + 1, :].broadcast_to([128, 8, 128]), op=ALU.mult),
                  reads=[Mb, amb], writes=[Mb])
        Mb.const = True
        qr = Ring(cx, es, "a_q", [128, 4, QB], BF16, 2)
        kr = Ring(cx, es, "a_k", [128, 4, 2 * QB], BF16, 2)
        vr = Ring(cx, es, "a_v", [128, 8, 520], BF16, 2)
        yst = Ring(cx, es, "a_yst", [128, 4, QB], BF16, 2)
        etmp = [sb("etmp%d" % i, [128, 512], F32) for i in range(2)]
        etb = [Buf("a_etmp%d" % i) for i in range(2)]
        expS = sb("expS", [128, 5, 8, 128], BF16)
        expSb = [Buf("a_expS%d" % i) for i in range(5)]
        rd = sb("rd", [128, 8], F32)
        rdb = Buf("a_rd")
        y = sb("y", [128, 8, 64], F32)
        yb = Buf("a_y")
        it = 0
        for blk in range(T // QB):
            a = blk * QB
            q, qb, qk = qr.next()
            k, kb, kk_ = kr.next()
            v, vb, vk = vr.next()
            ys, ysb, ysk = yst.next()
            cx.op("sp", lambda e: e.dma_start(out=q[:], in_=self.QT.t[:, :, a:a + QB].rearrange("f p t -> p f t")),
                  reads=self.QT.b(a, a + QB), writes=[qb], dma=qk)
            k0 = max(0, a - QB)
            off = k0 - (a - QB)
            if a == 0 and getattr(self, "seg", False):
                cx.op("sp", lambda e: e.dma_start(out=k[:, :, 0:QB], in_=self.KH.t.rearrange("f p t -> p f t")),
                      reads=self.KH.b(0, 512), writes=[kb], dma=kk_)
                cx.op("sp", lambda e: e.dma_start(out=v[:, 0:4, :], in_=self.VH.t.rearrange("n p c -> p n c")),
                      reads=self.VH.b(0, 512), writes=[vb], dma=vk)
            cx.op("sp", lambda e: e.dma_start(out=k[:, :, off:2 * QB], in_=self.KT.t[:, :, k0:a + QB].rearrange("f p t -> p f t")),
                  reads=self.KT.b(k0, a + QB), writes=[kb], dma=kk_)
            cx.op("sp", lambda e: e.dma_start(out=v[:, off // 128:8, :], in_=self.V1.t[k0 // 128:(a + QB) // 128].rearrange("n p c -> p n c")),
                  reads=self.V1.b(k0, a + QB), writes=[vb], dma=vk)
            for qp in range(QB // 128):
                if getattr(self, "att_dbg", 9) < 1:
                    break
                a1 = a + qp * 128
                valid = [i for i in range(5) if a1 - 512 + i * 128 >= 0 or getattr(self, "seg", False)]
                qs = slice(qp * 128, (qp + 1) * 128)
                for i in valid:
                    kc = slice(qp * 128 + i * 128, qp * 128 + (i + 1) * 128)
                    for par in range(2):
                        bk = it % 4
                        et = it % 2
                        it += 1
                        hp = par * 64
                        for hh in range(4):
                            jt = hh
                            cx.op("pe", lambda e: e.matmul(out=ps[bk][:, hh * 128:(hh + 1) * 128], lhsT=k[hp:hp + 64, jt, kc],
                                                           rhs=q[hp:hp + 64, jt, qs], start=True, stop=True),
                                  reads=[kb, qb], writes=[psb[bk]], signal=(hh == 3))
                        cx.op("act", lambda e: e.activation(out=etmp[et][:], in_=ps[bk][:], func=AF.Exp), reads=[psb[bk]], writes=[etb[et]])
                        cx.op("dve", lambda e: e.tensor_tensor(out=expS[:, i, par:8:2, :], in0=etmp[et][:].rearrange("p (h q) -> p h q", q=128),
                                                               in1=M[:, i, par:8:2, :], op=ALU.mult),
                              reads=[etb[et], Mb], writes=[expSb[i]])
                if getattr(self, "att_dbg", 9) < 2:
                    continue
                for g in range(2):
                    for hh in range(4):
                        h = 4 * g + hh
                        for i in valid:
                            cx.op("pe", lambda e: e.matmul(out=ps[4 + g][:, hh * 65:(hh + 1) * 65], lhsT=expS[:, i, h, :],
                                                           rhs=v[:, qp + i, h * 65:(h + 1) * 65], start=(i == valid[0]), stop=(i == valid[-1])),
                                  reads=[expSb[i], vb], writes=[psb[4 + g]], signal=(hh == 3 and i == valid[-1]))
                    if getattr(self, "att_dbg", 9) < 3:
                        continue
                    pv_ = ps[4 + g][:, 0:260].rearrange("p (h e) -> p h e", e=65)
                    cx.op("dve", lambda e: e.reciprocal(out=rd[:, 4 * g:4 * g + 4].rearrange("p (h o) -> p h o", o=1), in_=pv_[:, :, 64:65]),
                          reads=[psb[4 + g]], writes=[rdb])
                    cx.op("dve", lambda e: e.tensor_tensor(out=y[:, 4 * g:4 * g + 4, :], in0=pv_[:, :, 0:64],
                                                           in1=rd[:, 4 * g:4 * g + 4].rearrange("p (h o) -> p h o", o=1).broadcast_to([128, 4, 64]),
                                                           op=ALU.mult), reads=[psb[4 + g], rdb], writes=[yb])
                if getattr(self, "att_dbg", 9) < 4:
                    continue
                for jt in range(4):
                    cx.op("pe", lambda e: e.transpose(out=ps[6][:, jt * 128:(jt + 1) * 128],
                                                      in_=y[:, 2 * jt:2 * jt + 2, :].rearrange("p h d -> p (h d)"), identity=self.ident),
                          reads=[yb, self.cb], writes=[psb[6]], signal=(jt == 3))
                cx.op("act", lambda e: e.activation(out=ys[:, :, qs], in_=ps[6][:].rearrange("p (j q) -> p j q", q=128), func=AF.Copy),
                      reads=[psb[6]], writes=[ysb])
            cx.op("pool", lambda e: e.dma_start(out=self.YA.t[:, :, a:a + QB].rearrange("f p t -> p f t"), in_=ys[:]),
                  reads=[ysb], writes=self.YA.b(a, a + QB), dma=ysk)
        cx.end_phase()


Builder.decl_att = _decl_att
Builder.phase_att = _phase_att


def prep_att(inp, L, sh):
    rel = np.asarray(inp["att_rel_bias"], np.float32)
    p = np.arange(128)[:, None, None]
    i = np.arange(5)[None, :, None]
    q = np.arange(128)[None, None, :]
    kpos = i * 128 + p
    dist = q - kpos + 512
    idx = np.clip(dist, -63, 128) + 63
    cq = q // 64
    ck = kpos // 64
    mask = ((ck >= cq) & (ck <= cq + 8)).astype(np.float32)
    relb = np.zeros((L, 128, 5, 8, 128), np.float32)
    for l in range(L):
        for h in range(8):
            relb[l, :, :, h, :] = rel[l, h][idx]
    sh["relb"] = relb.reshape(L, 128, 5 * 8 * 128)
    sh["amask"] = np.ascontiguousarray(np.broadcast_to(mask, (128, 5, 128))).reshape(128, 640)


def _decl_merge(self):
    L = self.L
    self.wdecl("wgate", [L, 24, 128, 8 * 128])
    self.wdecl("wbr", [L, 3, 8, 128, 4 * 128])
    self.wdecl("wout", [L, 8, 128, 8 * 128])


def _phase_merge(self, l, src, dst):
    cx, nc, T = self.cx, self.nc, self.T
    TB = min(1024, T)
    NTC = TB // 512
    ps, psb = self.ps, self.psb
    with contextlib.ExitStack() as es:
        sb = lambda n, sh, dt: es.enter_context(nc.sbuf_tensor(uniq("m_" + n), sh, dt))
        xT = sb("xT", [128, 8, TB], F32)
        hn = sb("hn", [128, 8, TB], BF16)
        yb3 = [sb("y%d" % i, [128, 4, TB], BF16) for i in range(3)]
        mg = sb("mg", [128, 8, TB], BF16)
        macc = [sb("macc%d" % i, [128, 512], F32) for i in range(NTC)]
        gt = [sb("gt%d" % i, [128, 512], F32) for i in range(2)]
        xb = [Buf("m_xT%d" % i) for i in range(NTC)]
        hb = [Buf("m_hn%d" % i) for i in range(NTC)]
        ybb = [[Buf("m_y%d_%d" % (i, t)) for t in range(NTC)] for i in range(3)]
        mgb = [Buf("m_mg%d" % i) for i in range(NTC)]
        maccb = [Buf("m_macc%d" % i) for i in range(NTC)]
        gtb = [Buf("m_gt%d" % i) for i in range(2)]
        wgr = Ring(cx, es, "m_wg", [128, 8, 128], BF16, 3)
        wbrr = Ring(cx, es, "m_wb", [128, 4, 128], BF16, 3)
        ysrc = [self.YR, self.YA, self.YM]
        it = 0
        for blk in range(T // TB):
            t0 = blk * TB
            for tc in range(NTC):
                a, b = t0 + tc * 512, t0 + (tc + 1) * 512
                cs = slice(tc * 512, (tc + 1) * 512)
                cx.op("sp", lambda e: e.dma_start(out=xT[:, :, cs], in_=src.t[:, :, a:b].rearrange("f p t -> p f t")),
                      reads=src.b(a, b), writes=[xb[tc]], dma=xb[tc])
                cx.op("sp", lambda e: e.dma_start(out=hn[:, :, cs], in_=self.HN.t[:, :, a:b].rearrange("f p t -> p f t")),
                      reads=self.HN.b(a, b), writes=[hb[tc]], dma=hb[tc])
                for i in range(3):
                    cx.op("sp", lambda e: e.dma_start(out=yb3[i][:, :, cs], in_=ysrc[i].t[:, :, a:b].rearrange("f p t -> p f t")),
                          reads=ysrc[i].b(a, b), writes=[ybb[i][tc]], dma=ybb[i][tc])
            for o in range(8):
                for br in range(3):
                    wg, wgb, _ = wgr.next()
                    wb_, wbb, _ = wbrr.next()
                    cx.op("sp", lambda e: e.dma_start(out=wg[:], in_=self.wbf["wgate"][l, br * 8 + o].rearrange("p (k c) -> p k c", c=128)),
                          reads=[self.wbuf["wgate"][l]], writes=[wgb], dma=wgb)
                    cx.op("sp", lambda e: e.dma_start(out=wb_[:], in_=self.wbf["wbr"][l, br, o].rearrange("p (k c) -> p k c", c=128)),
                          reads=[self.wbuf["wbr"][l]], writes=[wbb], dma=wbb)
                    for tc in range(NTC):
                        cs = slice(tc * 512, (tc + 1) * 512)
                        pg, pb = it % 2, 2 + it % 2
                        gi = it % 2
                        it += 1
                        for kt in range(8):
                            cx.op("pe", lambda e: e.matmul(out=ps[pg][:], lhsT=wg[:, kt, :], rhs=hn[:, kt, cs], start=(kt == 0), stop=(kt == 7)),
                                  reads=[wgb, hb[tc]], writes=[psb[pg]], signal=(kt == 7))
                        for kt in range(4):
                            cx.op("pe", lambda e: e.matmul(out=ps[pb][:], lhsT=wb_[:, kt, :], rhs=yb3[br][:, kt, cs], start=(kt == 0), stop=(kt == 3)),
                                  reads=[wbb, ybb[br][tc]], writes=[psb[pb]], signal=(kt == 3))
                        cx.op("act", lambda e: e.activation(out=gt[gi][:], in_=ps[pg][:], func=AF.Sigmoid, bias=self.pv(l, "b_gate", br * 8 + o)),
                              reads=[psb[pg], self.pvb], writes=[gtb[gi]])
                        if br == 0:
                            cx.op("dve", lambda e: e.tensor_tensor(out=macc[tc][:], in0=gt[gi][:], in1=ps[pb][:], op=ALU.mult),
                                  reads=[gtb[gi], psb[pb]], writes=[maccb[tc]])
                        else:
                            cx.op("dve", lambda e: e.tensor_tensor(out=gt[gi][:], in0=gt[gi][:], in1=ps[pb][:], op=ALU.mult),
                                  reads=[gtb[gi], psb[pb]], writes=[gtb[gi]])
                            if br == 1:
                                cx.op("pool", lambda e: e.tensor_tensor(out=macc[tc][:], in0=macc[tc][:], in1=gt[gi][:], op=ALU.add),
                                      reads=[gtb[gi], maccb[tc]], writes=[maccb[tc]])
                            else:
                                cx.op("pool", lambda e: e.tensor_tensor(out=mg[:, o, cs], in0=macc[tc][:], in1=gt[gi][:], op=ALU.add),
                                      reads=[gtb[gi], maccb[tc]], writes=[mgb[tc]])
            for o in range(8):
                wg, wgb, _ = wgr.next()
                cx.op("sp", lambda e: e.dma_start(out=wg[:], in_=self.wbf["wout"][l, o].rearrange("p (k c) -> p k c", c=128)),
                      reads=[self.wbuf["wout"][l]], writes=[wgb], dma=wgb)
                for tc in range(NTC):
                    cs = slice(tc * 512, (tc + 1) * 512)
                    po = 4 + it % 2
                    it += 1
                    for kt in range(8):
                        cx.op("pe", lambda e: e.matmul(out=ps[po][:], lhsT=wg[:, kt, :], rhs=mg[:, kt, cs], start=(kt == 0), stop=(kt == 7)),
                              reads=[wgb, mgb[tc]], writes=[psb[po]], signal=(kt == 7))
                    cx.op("dve", lambda e: e.tensor_tensor(out=xT[:, o, cs], in0=ps[po][:], in1=xT[:, o, cs], op=ALU.add),
                          reads=[psb[po], xb[tc]], writes=[xb[tc]])
            for tc in range(NTC):
                a, b = t0 + tc * 512, t0 + (tc + 1) * 512
                cs = slice(tc * 512, (tc + 1) * 512)
                cx.op("pool", lambda e: e.dma_start(out=dst.t[:, :, a:b].rearrange("f p t -> p f t"), in_=xT[:, :, cs]),
                      reads=[xb[tc]], writes=dst.b(a, b), dma=xb[tc])
        cx.end_phase()


Builder.decl_merge = _decl_merge
Builder.phase_merge = _phase_merge


def prep_merge(inp, L, sh):
    g = {k: np.asarray(inp[k], np.float32) for k in ["w_gate", "w_branch_rwkv", "w_branch_att", "w_branch_mem", "w_out"]}
    wgate = np.zeros((L, 24, 128, 1024), np.float32)
    wbr = np.zeros((L, 3, 8, 128, 512), np.float32)
    wout = np.zeros((L, 8, 128, 1024), np.float32)
    for l in range(L):
        wgate[l] = fm_layout(g["w_gate"][l]).reshape(24, 128, 1024)
        for i, n in enumerate(["w_branch_rwkv", "w_branch_att", "w_branch_mem"]):
            wbr[l, i] = fm_layout(g[n][l]).reshape(8, 128, 512)
        wout[l] = fm_layout(g["w_out"][l]).reshape(8, 128, 1024)
    sh.update(wgate=wgate, wbr=wbr, wout=wout)


WC_ = 0.6065306597126334


def _decl_rwkv(self):
    L = self.L
    self.wdecl("lora", [L, 128, 3 * 512], cast=False)
    self.wdecl("rmask", [128, 5 * 128], cast=False)


def _phase_rwkv(self, l, mode="full"):
    emit_y = mode in ("full", "A")
    gn = mode == "full"
    seg = mode == "A"
    cx, nc, T = self.cx, self.nc, self.T
    ps, psb = self.ps, self.psb
    RB = 256
    NP = RB // 128
    NCH = RB // 64
    with contextlib.ExitStack() as es:
        def sb(n, sh, dt=F32):
            return es.enter_context(nc.sbuf_tensor(uniq("r_" + n), sh, dt)), Buf("r_" + n)
        PRb, PRbb = sb("PRb", [128, 15, RB + 1])
        Dt, Db = (None, None) if mode == "A" else sb("D", [128, 14, RB])
        S, Sb = sb("S", [128, 4, RB])
        A, Ab = sb("A", [128, 4, RB])
        G, Gb = sb("G", [128, 4, RB])
        KK, KKb = sb("KK", [128, 4, RB])
        CS, CSb = sb("CS", [128, 4, RB])
        E1, E1b = sb("E1", [128, 4, RB])
        E3, E3b = sb("E3", [128, 4, RB])
        Bh, Bhb = sb("Bh", [128, 4, RB])
        Bc, Bcb = sb("Bc", [128, 4, RB])
        Kc, Kcb = sb("Kc", [128, 4, RB])
        bonus, bonb = sb("bonus", [128, 4, RB])
        tmp4, tmp4b = sb("tmp4", [128, 4, RB])
        VFb, VFbb = sb("VFb", [128, 4, RB])
        tw, twb = sb("tw", [128, RB])
        sgx, sgxb = sb("sgx", [128, RB])
        lora, lorab = sb("lora", [128, 3, 512])
        rmask, rmb = sb("rmask", [128, 5, 128])
        TM = [sb("TM%d" % i, [128, NP, 512]) for i in range(4)]
        Amat, Amb = sb("Amat", [128, 8, 5, 128])
        if mode == "A":
            Dt = Amat[:].rearrange("p a b c -> p (a b c)")[:, 0:14 * RB].rearrange("p (a t) -> p a t", a=14)
            Db = Amb
        Pw = [sb("Pw%d_%d" % (g, i), [128, 4, 2, 128]) for g in range(2) for i in range(2)]
        Acc = [sb("Acc%d_%d" % (g, i), [128, 4, 128]) for g in range(2) for i in range(2)]
        P1s, P1b = sb("P1s", [128, 8, 64])
        P2T, P2b = sb("P2T", [128, 8, 128])
        Vt, Vtb = sb("Vt", [128, 8, 64])
        H = [sb("H%d" % i, [128, 4, 64]) for i in range(2)]
        Ht, Htb = sb("Ht", [128, 4, 64])
        if mode == "A":
            TnTq = [sb("TnTq%d" % i, [128, 2, 4, 128]) for i in range(2)]
            GnDq = [sb("GnDq%d" % i, [128, 2, 4, 64]) for i in range(2)]
            Y1Tq = [[sb("Y1Tq%d_%d" % (q_, i), [128, 4, 128]) for i in range(2)] for q_ in range(2)]
            Ycq = [sb("Ycq%d" % i, [64, 2, 512]) for i in range(2)]
            WCq = [sb("WCq%d" % i, [128, 4, 2]) for i in range(2)]
            TnT, TnTb = TnTq[0]
            GnD, GnDb = GnDq[0]
            Y1T = Y1Tq[0]
            yst = None
        else:
            TnT, TnTb = sb("TnT", [128, 2, 4, 128])
            GnD, GnDb = sb("GnD", [128, 2, 4, 64])
            Y1T = [sb("Y1T%d" % i, [128, 4, 128]) for i in range(2)]
            ysq, ysqb = sb("ysq", [64, 512])
            yn, ynb = sb("yn", [64, 512])
            st1, st1b = sb("st1", [64, 8])
            st2, st2b = sb("st2", [64, 8])
            st3, st3b = sb("st3", [64, 8])
            yo, yob = sb("yo", [128, 4, 128])
            yst = Ring(cx, es, "r_yst", [128, 4, RB], BF16, 2)
        prevc, prevcb = sb("prevc", [128, 14, 1])
        P_ = PRb[:, :, 1:RB + 1]
        r_, k_, v_ = P_[:, 0:4, :], P_[:, 4:8, :], P_[:, 8:12, :]

        cx.op("sp", lambda e: e.dma_start(out=lora[:].rearrange("p a c -> p (a c)"), in_=self.win["lora"][l]), writes=[lorab], dma=lorab)
        cx.op("sp", lambda e: e.dma_start(out=rmask[:].rearrange("p a c -> p (a c)"), in_=self.win["rmask"]), writes=[rmb], dma=rmb)
        lorab.const = True
        rmb.const = True
        if mode == "A":
            Ap = [sb("Ap%d" % i, [128, 4, 128]) for i in range(2)]
            ApT, ApTb = sb("ApT", [128, 4, 128])
            pay, payb = sb("pay", [128, 768])
            zst = Ring(cx, es, "r_zst", [128, 512], F32, 2)
            y0st = Ring(cx, es, "r_y0st", [64, 512], F32, 2)
        apcur = 0
        cx.op("dve", lambda e: e.memset(H[0][0][:], 0.0), writes=[H[0][1]])
        if mode == "A":
            cx.op("dve", lambda e: e.tensor_copy(out=Ap[0][0][:], in_=self.ident.rearrange("p (o c) -> p o c", o=1).broadcast_to([128, 4, 128])),
                  reads=[self.cb], writes=[Ap[0][1]])
        for (yt_, ytb_) in ([x_ for q_ in Y1Tq for x_ in q_] if mode == "A" else Y1T):
            cx.op("dve", lambda e: e.memset(yt_[:], 0.0), writes=[ytb_])
        deferred = []
        stt = {"h": 0, "a": 0, "pair": 0}

        def pump(n):
            for _ in range(n):
                if deferred:
                    deferred.pop(0)()
        pv = lambda name, j: self.pv(l, name, j)
        hcur = 0
        bi = [0]

        def bank():
            bi[0] += 1
            return bi[0] % 8

        for blk in range(T // RB):
            t0 = blk * RB
            ntile = 15 if l > 0 else 14
            cx.op("sp", lambda e: e.dma_start(out=PRb[:, 0:14, 1:RB + 1], in_=self.PR.t[0:14, :, t0:t0 + RB].rearrange("f p t -> p f t")),
                  reads=self.PR.b(t0, t0 + RB), writes=[PRbb], dma=PRbb)
            if l > 0:
                cx.op("sp", lambda e: e.dma_start(out=PRb[0:32, 14, 1:RB + 1], in_=self.PR.t[14, 0:32, t0:t0 + RB]),
                      reads=self.PR.b(t0, t0 + RB), writes=[PRbb], dma=PRbb)
            if t0 == 0 and seg:
                cx.op("sp", lambda e: e.dma_start(out=prevc[:].rearrange("p a o -> p (a o)"), in_=self.SH), reads=[self.SHb], writes=[prevcb], dma=prevcb)
                cx.op("pool", lambda e: e.tensor_copy(out=PRb[:, 0:14, 0:1], in_=prevc[:]), reads=[prevcb], writes=[PRbb])
            elif t0 == 0:
                cx.op("pool", lambda e: e.memset(PRb[:, 0:14, 0:1], 0.0), writes=[PRbb])
            else:
                cx.op("pool", lambda e: e.tensor_copy(out=PRb[:, 0:14, 0:1], in_=prevc[:]), reads=[prevcb], writes=[PRbb])
            cx.op("pool", lambda e: e.tensor_copy(out=prevc[:], in_=PRb[:, 0:14, RB:RB + 1]), reads=[PRbb], writes=[prevcb])
            if l > 0:
                cx.op("sp", lambda e: e.dma_start(out=VFb[:], in_=self.VF.t[:, :, t0:t0 + RB].rearrange("f p t -> p f t")),
                      reads=self.VF.b(t0, t0 + RB), writes=[VFbb], dma=VFbb)
            cx.op("pool", lambda e: e.tensor_tensor(out=Dt[:], in0=PRb[:, 0:14, 0:RB], in1=PRb[:, 0:14, 1:RB + 1], op=ALU.subtract),
                  reads=[PRbb], writes=[Db])
            for j in range(14):
                cx.op("dve", lambda e: e.scalar_tensor_tensor(out=P_[:, j, :], in0=Dt[:, j, :], scalar=pv("shift_mu", j), in1=P_[:, j, :],
                                                              op0=ALU.mult, op1=ALU.add), reads=[Db, PRbb, self.pvb], writes=[PRbb])
            cx.op("act", lambda e: e.activation(out=tw[0:64, :], in_=P_[0:64, 12, :], func=AF.Tanh), reads=[PRbb], writes=[twb])
            cx.op("act", lambda e: e.activation(out=sgx[:], in_=P_[:, 13, :], func=AF.Sigmoid), reads=[PRbb], writes=[sgxb])
            for jt in range(4):
                js = slice(jt * 128, (jt + 1) * 128)
                cx.op("pe", lambda e: e.matmul(out=ps[0][:, 0:RB], lhsT=lora[0:64, 0, js], rhs=tw[0:64, :], start=True, stop=True),
                      reads=[lorab, twb], writes=[psb[0]])
                cx.op("act", lambda e: e.activation(out=S[:, jt, :], in_=ps[0][:, 0:RB], func=AF.Sigmoid, bias=pv("decay_w0", jt)),
                      reads=[psb[0], self.pvb], writes=[Sb])
                cx.op("pe", lambda e: e.matmul(out=ps[1][:, 0:RB], lhsT=lora[64:128, 0, js], rhs=P_[64:128, 12, :], start=True, stop=True),
                      reads=[lorab, PRbb], writes=[psb[1]])
                cx.op("act", lambda e: e.activation(out=A[:, jt, :], in_=ps[1][:, 0:RB], func=AF.Sigmoid, bias=pv("iclr_a0", jt)),
                      reads=[psb[1], self.pvb], writes=[Ab])
                if emit_y:
                    cx.op("pe", lambda e: e.matmul(out=ps[2][:, 0:RB], lhsT=lora[:, 1, js], rhs=sgx[:], start=True, stop=True),
                          reads=[lorab, sgxb], writes=[psb[2]])
                    cx.op("act", lambda e: e.activation(out=G[:, jt, :], in_=ps[2][:, 0:RB], func=AF.Copy), reads=[psb[2]], writes=[Gb])
                if l > 0:
                    cx.op("pe", lambda e: e.matmul(out=ps[3][:, 0:RB], lhsT=lora[0:32, 2, js], rhs=P_[0:32, 14, :], start=True, stop=True),
                          reads=[lorab, PRbb], writes=[psb[3]])
                    cx.op("act", lambda e: e.activation(out=tmp4[:, jt, :], in_=ps[3][:, 0:RB], func=AF.Sigmoid, bias=pv("vres_v0", jt)),
                          reads=[psb[3], self.pvb], writes=[tmp4b])
            if l > 0:
                cx.op("dve", lambda e: e.tensor_tensor(out=VFb[:], in0=VFb[:], in1=v_, op=ALU.subtract), reads=[VFbb, PRbb], writes=[VFbb])
                cx.op("dve", lambda e: e.tensor_tensor(out=VFb[:], in0=VFb[:], in1=tmp4[:], op=ALU.mult), reads=[VFbb, tmp4b], writes=[VFbb])
                cx.op("dve", lambda e: e.tensor_tensor(out=v_, in0=v_, in1=VFb[:], op=ALU.add), reads=[VFbb, PRbb], writes=[PRbb])
            else:
                cx.op("sp", lambda e: e.dma_start(out=self.VF.t[:, :, t0:t0 + RB].rearrange("f p t -> p f t"), in_=v_),
                      reads=[PRbb], writes=self.VF.b(t0, t0 + RB), dma=PRbb)
            for jt in range(4):
                cx.op("dve", lambda e: e.tensor_scalar(out=KK[:, jt, :], in0=k_[:, jt, :], scalar1=pv("k_k", jt), scalar2=None, op0=ALU.mult),
                      reads=[PRbb, self.pvb], writes=[KKb])
            cx.op("act", lambda e: e.activation(out=tmp4[:], in_=KK[:], func=AF.Square), reads=[KKb], writes=[tmp4b])
            for hf in range(2):
                cx.op("pe", lambda e: e.matmul(out=ps[4 + hf][:], lhsT=self.blk64, rhs=tmp4[:, 2 * hf:2 * hf + 2, :].rearrange("p a t -> p (a t)"),
                                               start=True, stop=True), reads=[tmp4b, self.cb], writes=[psb[4 + hf]])
            for hf in range(2):
                cx.op("act", lambda e: e.activation(out=tmp4[:, 2 * hf:2 * hf + 2, :].rearrange("p a t -> p (a t)"), in_=ps[4 + hf][:], func=AF.Sqrt),
                      reads=[psb[4 + hf]], writes=[tmp4b])
            cx.op("dve", lambda e: e.tensor_scalar(out=tmp4[:], in0=tmp4[:], scalar1=1e-12, scalar2=None, op0=ALU.max), reads=[tmp4b], writes=[tmp4b])
            cx.op("dve", lambda e: e.reciprocal(out=tmp4[:], in_=tmp4[:]), reads=[tmp4b], writes=[tmp4b])
            cx.op("dve", lambda e: e.tensor_tensor(out=KK[:], in0=KK[:], in1=tmp4[:], op=ALU.mult), reads=[KKb, tmp4b], writes=[KKb])
            for jt in range(4):
                cx.op("dve", lambda e: e.tensor_scalar(out=tmp4[:, jt, :], in0=A[:, jt, :], scalar1=-1.0, scalar2=pv("k_a", jt), op0=ALU.add, op1=ALU.mult),
                      reads=[Ab, self.pvb], writes=[tmp4b])
            cx.op("dve", lambda e: e.scalar_tensor_tensor(out=k_, in0=tmp4[:], scalar=1.0, in1=k_, op0=ALU.add, op1=ALU.mult),
                  reads=[tmp4b, PRbb], writes=[PRbb])
            for jt in (range(4) if emit_y else []):
                cx.op("dve", lambda e: e.scalar_tensor_tensor(out=tmp4[:, jt, :], in0=r_[:, jt, :], scalar=pv("r_k", jt), in1=k_[:, jt, :],
                                                              op0=ALU.mult, op1=ALU.mult), reads=[PRbb, self.pvb], writes=[tmp4b])
            for hf in (range(2) if emit_y else []):
                cx.op("pe", lambda e: e.matmul(out=ps[6 + hf][:], lhsT=self.blk64, rhs=tmp4[:, 2 * hf:2 * hf + 2, :].rearrange("p a t -> p (a t)"),
                                               start=True, stop=True), reads=[tmp4b, self.cb], writes=[psb[6 + hf]])
            for hf in (range(2) if emit_y else []):
                cx.op("dve", lambda e: e.tensor_tensor(out=bonus[:, 2 * hf:2 * hf + 2, :].rearrange("p a t -> p (a t)"), in0=ps[6 + hf][:],
                                                       in1=v_[:, 2 * hf:2 * hf + 2, :], op=ALU.mult) if False else
                      e.tensor_tensor(out=bonus[:, 2 * hf:2 * hf + 2, :], in0=ps[6 + hf][:].rearrange("p (a t) -> p a t", t=RB),
                                      in1=v_[:, 2 * hf:2 * hf + 2, :], op=ALU.mult), reads=[psb[6 + hf], PRbb], writes=[bonb])
            for jt in range(4):
                cx.op("dve", lambda e: e.tensor_tensor_scan(out=CS[:, jt, :], data0=self.cmask[:, 0:RB], data1=S[:, jt, :], initial=0.0,
                                                            op0=ALU.mult, op1=ALU.add), reads=[Sb, self.cb], writes=[CSb])
            cx.op("pool", lambda e: e.tensor_tensor(out=E3[:], in0=CS[:], in1=S[:], op=ALU.subtract), reads=[CSb, Sb], writes=[E3b])
            cx.op("act", lambda e: e.activation(out=E3[:], in_=E3[:], func=AF.Exp, scale=-WC_), reads=[E3b], writes=[E3b])
            cx.op("act", lambda e: e.activation(out=E1[:], in_=CS[:], func=AF.Exp, scale=-WC_), reads=[CSb], writes=[E1b])
            cx.op("act", lambda e: e.activation(out=CS[:], in_=CS[:], func=AF.Exp, scale=WC_), reads=[CSb], writes=[CSb])
            cx.op("dve", lambda e: e.tensor_tensor(out=Bh[:], in0=KK[:], in1=A[:], op=ALU.mult), reads=[KKb, Ab], writes=[Bhb])
            cx.op("dve", lambda e: e.tensor_tensor(out=Bh[:], in0=Bh[:], in1=CS[:], op=ALU.mult), reads=[Bhb, CSb], writes=[Bhb])
            cx.op("pool", lambda e: e.tensor_tensor(out=k_, in0=k_, in1=CS[:], op=ALU.mult), reads=[PRbb, CSb], writes=[PRbb])
            cx.op("pool", lambda e: e.tensor_tensor(out=KK[:], in0=KK[:], in1=E3[:], op=ALU.mult), reads=[KKb, E3b], writes=[KKb])
            cx.op("dve", lambda e: e.tensor_tensor(out=r_, in0=r_, in1=E1[:], op=ALU.mult), reads=[PRbb, E1b], writes=[PRbb])
            wcb = E1[:].rearrange("p a (c t) -> p a c t", t=64)[:, :, :, 63:64].broadcast_to([128, 4, NCH, 64])
            cx.op("dve", lambda e: e.tensor_tensor(out=Bc[:].rearrange("p a (c t) -> p a c t", t=64), in0=Bh[:].rearrange("p a (c t) -> p a c t", t=64),
                                                   in1=wcb, op=ALU.mult), reads=[Bhb, E1b], writes=[Bcb])
            cx.op("pool", lambda e: e.tensor_tensor(out=Kc[:].rearrange("p a (c t) -> p a c t", t=64), in0=k_.rearrange("p a (c t) -> p a c t", t=64),
                                                    in1=wcb, op=ALU.mult), reads=[PRbb, E1b], writes=[Kcb])
            srcs = [(v_, PRbb, 1.0), (KK[:], KKb, 1.0), (Bc[:], Bcb, -1.0), (Kc[:], Kcb, 1.0)]
            for qi, (sap, sbuf_, sc_) in enumerate(srcs):
                for pr in range(NP):
                    bk = bank()
                    for jt in range(4):
                        cx.op("pe", lambda e: e.transpose(out=ps[bk][:, jt * 128:(jt + 1) * 128], in_=sap[:, jt, pr * 128:(pr + 1) * 128], identity=self.ident),
                              reads=[sbuf_, self.cb], writes=[psb[bk]], signal=(jt == 3))
                    cx.op("act", lambda e: e.activation(out=TM[qi][0][:, pr, :], in_=ps[bk][:], func=AF.Copy, scale=sc_), reads=[psb[bk]], writes=[TM[qi][1]])
            Vtm, KKtm, Bntm, Kctm = [t[0] for t in TM]
            Vtmb, KKtmb, Bntmb, Kctmb = [t[1] for t in TM]
            if yst is not None:
                ys, ysb, _ = yst.next()
            for pr in range(NP):
                pc = slice(pr * 128, (pr + 1) * 128)
                if mode == "A":
                    q = stt["pair"] % 2
                    stt["pair"] += 1
                    TnT, TnTb = TnTq[q]
                    GnD, GnDb = GnDq[q]
                    Y1T = Y1Tq[q]
                for h in range(8):
                    jt, hp = h // 2, (h % 2) * 64
                    hs = slice(hp, hp + 64)
                    bx = h % 2
                    by = 2 + h % 2
                    kkt, bh, kh, rt = KK[hs, jt, pc], Bh[hs, jt, pc], k_[hs, jt, pc], r_[hs, jt, pc]
                    for qi, (lt, rh, rb_) in enumerate([(kkt, bh, Bhb), (kkt, kh, PRbb), (bh, kkt, KKb), (bh, rt, PRbb)]):
                        cx.op("pe", lambda e: e.matmul(out=ps[bx][:, qi * 128:(qi + 1) * 128], lhsT=lt, rhs=rh, start=True, stop=True),
                              reads=[KKb, Bhb, PRbb], writes=[psb[bx]], signal=(qi == 3))
                    cx.op("pe", lambda e: e.matmul(out=ps[by][:, (h // 2) * 128:(h // 2 + 1) * 128], lhsT=kh, rhs=rt, start=True, stop=True),
                          reads=[PRbb], writes=[psb[by]], signal=(h >= 6))
                    cx.op("dve", lambda e: e.tensor_tensor(out=Amat[:, h, 0:4, :], in0=ps[bx][:].rearrange("p (a c) -> p a c", c=128), in1=rmask[:, 0:4, :],
                                                           op=ALU.mult), reads=[psb[bx], rmb], writes=[Amb])
                for par in range(2):
                    cx.op("dve", lambda e: e.tensor_tensor(out=Amat[:, par:8:2, 4, :], in0=ps[2 + par][:].rearrange("p (a c) -> p a c", c=128),
                                                           in1=rmask[:, 4:5, :].broadcast_to([128, 4, 128]), op=ALU.mult),
                          reads=[psb[2 + par], rmb], writes=[Amb])
                pump(2)
                cur = [None, None]
                for g in range(2):
                    a0_, a0b = Acc[2 * g]
                    cx.op("dve", lambda e: e.tensor_tensor(out=a0_[:], in0=Amat[:, 4 * g:4 * g + 4, 2, :],
                                                           in1=self.ident.rearrange("p (o c) -> p o c", o=1).broadcast_to([128, 4, 128]), op=ALU.add),
                          reads=[Amb, self.cb], writes=[a0b])
                    cur[g] = 0
                for lev in range(1, 6):
                    for g in range(2):
                        pwn, pwnb = Pw[2 * g + lev % 2]
                        pwo, pwob = Pw[2 * g + (lev - 1) % 2]
                        for hh in range(4):
                            h = 4 * g + hh
                            if lev == 1:
                                Mo, No, rdb_ = Amat[:, h, 2, :], Amat[:, h, 0, :], Amb
                            else:
                                Mo, No, rdb_ = pwo[:, hh, 0, :], pwo[:, hh, 1, :], pwob
                            bp = 4 + 2 * g + hh // 2
                            c0 = (hh % 2) * 256
                            if lev < 5:
                                cx.op("pe", lambda e: e.matmul(out=ps[bp][:, c0:c0 + 128], lhsT=No, rhs=Mo, start=True, stop=True),
                                      reads=[rdb_], writes=[psb[bp]], signal=False)
                            cx.op("pe", lambda e: e.matmul(out=ps[bp][:, c0 + 128:c0 + 256], lhsT=Mo, rhs=No, start=True, stop=True),
                                  reads=[rdb_], writes=[psb[bp]], signal=(hh % 2 == 1))
                        for half in range(2):
                            bp = 4 + 2 * g + half
                            cx.op("act", lambda e: e.activation(out=pwn[:, 2 * half:2 * half + 2, :, :], in_=ps[bp][:].rearrange("p (a b c) -> p a b c", b=2, c=128),
                                                                func=AF.Copy), reads=[psb[bp]], writes=[pwnb])
                        ao, aob = Acc[2 * g + cur[g]]
                        an, anb = Acc[2 * g + 1 - cur[g]]
                        bc_ = 2 + g
                        for hh in range(4):
                            cx.op("pe", lambda e: e.matmul(out=ps[bc_][:, hh * 128:(hh + 1) * 128], lhsT=pwn[:, hh, 1, :], rhs=ao[:, hh, :], start=True, stop=True),
                                  reads=[pwnb, aob], writes=[psb[bc_]], signal=(hh == 3))
                        cx.op("dve", lambda e: e.tensor_tensor(out=an[:], in0=ao[:], in1=ps[bc_][:].rearrange("p (a c) -> p a c", c=128), op=ALU.add),
                              reads=[aob, psb[bc_]], writes=[anb])
                        cur[g] = 1 - cur[g]
                    pump(1)
                XT = lambda h: Acc[2 * (h // 4) + cur[h // 4]][0][:, h % 4, :]
                XTb = lambda h: Acc[2 * (h // 4) + cur[h // 4]][1]
                b1 = bank()
                for h in range(8):
                    cx.op("pe", lambda e: e.matmul(out=ps[b1][:, h * 64:(h + 1) * 64], lhsT=XT(h), rhs=KKtm[:, pr, h * 64:(h + 1) * 64], start=True, stop=True),
                          reads=[XTb(h), KKtmb], writes=[psb[b1]], signal=(h == 7))
                cx.op("act", lambda e: e.activation(out=P1s[:].rearrange("p a c -> p (a c)"), in_=ps[b1][:], func=AF.Copy), reads=[psb[b1]], writes=[P1b])
                for g in range(2):
                    b2 = bank()
                    for hh in range(4):
                        h = 4 * g + hh
                        cx.op("pe", lambda e: e.matmul(out=ps[b2][:, hh * 128:(hh + 1) * 128], lhsT=Amat[:, h, 1, :], rhs=XT(h), start=True, stop=True),
                              reads=[Amb, XTb(h)], writes=[psb[b2]], signal=(hh == 3))
                    cx.op("act", lambda e: e.activation(out=P2T[:, 4 * g:4 * g + 4, :].rearrange("p a c -> p (a c)"), in_=ps[b2][:], func=AF.Copy),
                          reads=[psb[b2]], writes=[P2b])
                pump(1)
                b3 = bank()
                for h in range(8):
                    cx.op("pe", lambda e: e.matmul(out=ps[b3][:, h * 64:(h + 1) * 64], lhsT=P2T[:, h, :], rhs=Vtm[:, pr, h * 64:(h + 1) * 64], start=True, stop=True),
                          reads=[P2b, Vtmb], writes=[psb[b3]], signal=(h == 7))
                cx.op("act", lambda e: e.activation(out=Vt[:].rearrange("p a c -> p (a c)"), in_=ps[b3][:], func=AF.Copy), reads=[psb[b3]], writes=[Vtb])
                for c in range(2):
                    rs = slice(c * 64, (c + 1) * 64)
                    bt = bank()
                    while bt % 2 != c:
                        bt = bank()
                    for jt in range(4):
                        js = slice(jt * 128, (jt + 1) * 128)
                        cx.op("pe", lambda e: e.matmul(out=ps[bt][:, js], lhsT=P1s[rs, 2 * jt:2 * jt + 2, :].rearrange("p a c -> p (a c)"), rhs=Bntm[rs, pr, js],
                                                       start=True, stop=True), reads=[P1b, Bntmb], writes=[psb[bt]], signal=(jt == 3))
                    cx.op("dve", lambda e: e.tensor_tensor(out=TnT[:, c, :, :], in0=ps[bt][:].rearrange("p (a c) -> p a c", c=128),
                                                           in1=self.blk64.rearrange("p (o c) -> p o c", o=1).broadcast_to([128, 4, 128]), op=ALU.mult),
                          reads=[psb[bt], self.cb], writes=[TnTb])
                    bg = bank()
                    while bg % 2 != c:
                        bg = bank()
                    for jt in range(4):
                        js = slice(jt * 128, (jt + 1) * 128)
                        cx.op("pe", lambda e: e.matmul(out=ps[bg][:, js], lhsT=Kctm[rs, pr, js], rhs=Vtm[rs, pr, js], start=True, stop=False),
                              reads=[Kctmb, Vtmb], writes=[psb[bg]], signal=False)
                        cx.op("pe", lambda e: e.matmul(out=ps[bg][:, js], lhsT=Bntm[rs, pr, js], rhs=Vt[rs, 2 * jt:2 * jt + 2, :].rearrange("p a c -> p (a c)"),
                                                       start=False, stop=True), reads=[Bntmb, Vtb], writes=[psb[bg]], signal=(jt == 3))
                    for par in range(2):
                        hs = slice(par * 64, (par + 1) * 64)
                        cx.op("act", lambda e: e.activation(out=GnD[hs, c, :, :], in_=ps[bg][hs, :].rearrange("p (a c) -> p a c", c=128)[:, :, par * 64:(par + 1) * 64],
                                                            func=AF.Copy), reads=[psb[bg]], writes=[GnDb])
                for par in (range(2) if emit_y else []):
                    b4 = bank()
                    for hh in range(4):
                        h = 2 * hh + par
                        cx.op("pe", lambda e: e.matmul(out=ps[b4][:, hh * 128:(hh + 1) * 128], lhsT=P1s[:, 2 * hh:2 * hh + 2, :].rearrange("p a c -> p (a c)"),
                                                       rhs=Amat[:, h, 3, :], start=True, stop=True), reads=[P1b, Amb], writes=[psb[b4]], signal=(hh == 3))
                    hs = slice(par * 64, (par + 1) * 64)
                    cx.op("dve", lambda e: e.tensor_tensor(out=Y1T[par][0][hs, :, :], in0=ps[b4][hs, :].rearrange("p (a c) -> p a c", c=128), in1=r_[hs, :, pc], op=ALU.add),
                          reads=[psb[b4], PRbb], writes=[Y1T[par][1]])
                if mode == "A":
                    Yc, Ycb = Ycq[q]
                    WC, WCb = WCq[q]
                    for c in range(2):
                        cc = slice(c * 64, (c + 1) * 64)
                        byc = bank()
                        for h in range(8):
                            o_ = ps[byc][0:64, h * 64:(h + 1) * 64]
                            cx.op("pe", lambda e: e.matmul(out=o_, lhsT=Amat[:, h, 4, cc], rhs=Vtm[:, pr, h * 64:(h + 1) * 64], start=True, stop=False),
                                  reads=[Amb, Vtmb], writes=[psb[byc]], signal=False)
                            cx.op("pe", lambda e: e.matmul(out=o_, lhsT=Amat[:, h, 3, cc], rhs=Vt[:, h, :], start=False, stop=True),
                                  reads=[Amb, Vtb], writes=[psb[byc]], signal=(h == 7))
                        cx.op("act", lambda e: e.activation(out=Yc[:, c, :], in_=ps[byc][0:64, :], func=AF.Copy), reads=[psb[byc]], writes=[Ycb])
                    cx.op("pool", lambda e: e.tensor_copy(out=WC[:], in_=E1[:, :, pr * 128 + 63:pr * 128 + 128:64]), reads=[E1b], writes=[WCb])

                    def mk_units(c, Y1T=Y1T, TnT=TnT, TnTb=TnTb, GnD=GnD, GnDb=GnDb, Yc=Yc, Ycb=Ycb, WC=WC, WCb=WCb, nchunk=(t0 // 64) + pr * 2):
                        cc = slice(c * 64, (c + 1) * 64)
                        nck = nchunk + c

                        def u_y0():
                            Hc, Hcb = H[stt["h"]]
                            byc = bank()
                            for h in range(8):
                                jt, par = h // 2, h % 2
                                cx.op("pe", lambda e: e.matmul(out=ps[byc][0:64, h * 64:(h + 1) * 64], lhsT=Y1T[par][0][:, jt, cc], rhs=Hc[:, jt, :], start=True, stop=True),
                                      reads=[Y1T[par][1], Hcb], writes=[psb[byc]], signal=(h == 7))
                            y0t, y0b, _ = y0st.next()
                            cx.op("dve", lambda e: e.tensor_tensor(out=y0t[:], in0=ps[byc][0:64, :], in1=Yc[:, c, :], op=ALU.add), reads=[psb[byc], Ycb], writes=[y0b])
                            cx.op("sp", lambda e: e.dma_start(out=self.Y0.t[nck], in_=y0t[:]), reads=[y0b], writes=self.Y0.b(nck * 64, nck * 64 + 64), dma=y0b)

                        def u_zt():
                            Apc, Apcb = Ap[stt["a"]]
                            bz_ = bank()
                            for h in range(8):
                                jt, par = h // 2, h % 2
                                cx.op("pe", lambda e: e.matmul(out=ps[bz_][:, h * 64:(h + 1) * 64], lhsT=Apc[:, jt, :], rhs=Y1T[par][0][:, jt, cc], start=True, stop=True),
                                      reads=[Apcb, Y1T[par][1]], writes=[psb[bz_]], signal=(h == 7))
                            zt_, ztb, _ = zst.next()
                            cx.op("act", lambda e: e.activation(out=zt_[:], in_=ps[bz_][:], func=AF.Copy), reads=[psb[bz_]], writes=[ztb])
                            cx.op("sp", lambda e: e.dma_start(out=self.ZS.t[nck], in_=zt_[:]), reads=[ztb], writes=self.ZS.b(nck * 64, nck * 64 + 64), dma=ztb)

                        def u_h():
                            Hc, Hcb = H[stt["h"]]
                            Hn, Hnb = H[1 - stt["h"]]
                            bh_ = bank()
                            for jt in range(4):
                                cx.op("pe", lambda e: e.matmul(out=ps[bh_][:, jt * 64:(jt + 1) * 64], lhsT=TnT[:, c, jt, :], rhs=Hc[:, jt, :], start=True, stop=True),
                                      reads=[TnTb, Hcb], writes=[psb[bh_]], signal=(jt == 3))
                            cx.op("pool", lambda e: e.tensor_tensor(out=Ht[:], in0=Hc[:], in1=WC[:, :, c:c + 1].broadcast_to([128, 4, 64]), op=ALU.mult),
                                  reads=[Hcb, WCb], writes=[Htb])
                            cx.op("pool", lambda e: e.tensor_tensor(out=Ht[:], in0=Ht[:], in1=GnD[:, c, :, :], op=ALU.add), reads=[Htb, GnDb], writes=[Htb])
                            cx.op("dve", lambda e: e.tensor_tensor(out=Hn[:], in0=Ht[:], in1=ps[bh_][:, 0:256].rearrange("p (a c) -> p a c", c=64), op=ALU.add),
                                  reads=[Htb, psb[bh_]], writes=[Hnb])
                            stt["h"] = 1 - stt["h"]

                        def u_ap():
                            Apc, Apcb = Ap[stt["a"]]
                            Apn, Apnb = Ap[1 - stt["a"]]
                            ba_ = bank()
                            for jt in range(4):
                                cx.op("pe", lambda e: e.matmul(out=ps[ba_][:, jt * 128:(jt + 1) * 128], lhsT=TnT[:, c, jt, :], rhs=Apc[:, jt, :], start=True, stop=True),
                                      reads=[TnTb, Apcb], writes=[psb[ba_]], signal=(jt == 3))
                            cx.op("pool", lambda e: e.tensor_tensor(out=ApT[:], in0=Apc[:], in1=WC[:, :, c:c + 1].broadcast_to([128, 4, 128]), op=ALU.mult),
                                  reads=[Apcb, WCb], writes=[ApTb])
                            cx.op("dve", lambda e: e.tensor_tensor(out=Apn[:], in0=ApT[:], in1=ps[ba_][:].rearrange("p (a c) -> p a c", c=128), op=ALU.add),
                                  reads=[ApTb, psb[ba_]], writes=[Apnb])
                            stt["a"] = 1 - stt["a"]
                        return [u_y0, u_zt, u_h, u_ap]
                    pump(len(deferred))
                    deferred.extend(mk_units(0) + mk_units(1))
                    continue
                by_ = bank()
                for c in range(2):
                    cc = slice(c * 64, (c + 1) * 64)
                    Hc, Hcb = H[hcur]
                    Hn, Hnb = H[1 - hcur]
                    byc = bank()
                    for h in (range(8) if emit_y else []):
                        jt, par = h // 2, h % 2
                        o_ = ps[byc][0:64, h * 64:(h + 1) * 64]
                        cx.op("pe", lambda e: e.matmul(out=o_, lhsT=Y1T[par][0][:, jt, cc], rhs=Hc[:, jt, :], start=True, stop=False),
                              reads=[Y1T[par][1], Hcb], writes=[psb[byc]], signal=False)
                        cx.op("pe", lambda e: e.matmul(out=o_, lhsT=Amat[:, h, 4, cc], rhs=Vtm[:, pr, h * 64:(h + 1) * 64], start=False, stop=False),
                              reads=[Amb, Vtmb], writes=[psb[byc]], signal=False)
                        cx.op("pe", lambda e: e.matmul(out=o_, lhsT=Amat[:, h, 3, cc], rhs=Vt[:, h, :], start=False, stop=True),
                              reads=[Amb, Vtb], writes=[psb[byc]], signal=(h == 7))
                    bh_ = bank()
                    for jt in range(4):
                        cx.op("pe", lambda e: e.matmul(out=ps[bh_][:, jt * 64:(jt + 1) * 64], lhsT=TnT[:, c, jt, :], rhs=Hc[:, jt, :], start=True, stop=True),
                              reads=[TnTb, Hcb], writes=[psb[bh_]], signal=(jt == 3))
                    ci = pr * 2 + c
                    wc1 = E1[:, :, ci * 64 + 63:ci * 64 + 64].broadcast_to([128, 4, 64])
                    cx.op("pool", lambda e: e.tensor_tensor(out=Ht[:], in0=Hc[:], in1=wc1, op=ALU.mult), reads=[Hcb, E1b], writes=[Htb])
                    cx.op("pool", lambda e: e.tensor_tensor(out=Ht[:], in0=Ht[:], in1=GnD[:, c, :, :], op=ALU.add), reads=[Htb, GnDb], writes=[Htb])
                    cx.op("dve", lambda e: e.tensor_tensor(out=Hn[:], in0=Ht[:], in1=ps[bh_][:, 0:256].rearrange("p (a c) -> p a c", c=64), op=ALU.add),
                          reads=[Htb, psb[bh_]], writes=[Hnb])
                    hcur = 1 - hcur
                    if mode == "A":
                        Apc, Apcb = Ap[apcur]
                        Apn, Apnb = Ap[1 - apcur]
                        y0t, y0b, _ = y0st.next()
                        cx.op("act", lambda e: e.activation(out=y0t[:], in_=ps[byc][0:64, :], func=AF.Copy), reads=[psb[byc]], writes=[y0b])
                        nchunk = (t0 // 64) + pr * 2 + c
                        cx.op("pool", lambda e: e.dma_start(out=self.Y0.t[nchunk], in_=y0t[:]), reads=[y0b], writes=self.Y0.b(nchunk * 64, nchunk * 64 + 64), dma=y0b)
                        bz_ = bank()
                        for h in range(8):
                            jt, par = h // 2, h % 2
                            cx.op("pe", lambda e: e.matmul(out=ps[bz_][:, h * 64:(h + 1) * 64], lhsT=Apc[:, jt, :], rhs=Y1T[par][0][:, jt, cc], start=True, stop=True),
                                  reads=[Apcb, Y1T[par][1]], writes=[psb[bz_]], signal=(h == 7))
                        zt_, ztb, _ = zst.next()
                        cx.op("act", lambda e: e.activation(out=zt_[:], in_=ps[bz_][:], func=AF.Copy), reads=[psb[bz_]], writes=[ztb])
                        cx.op("pool", lambda e: e.dma_start(out=self.ZS.t[nchunk], in_=zt_[:]), reads=[ztb], writes=self.ZS.b(nchunk * 64, nchunk * 64 + 64), dma=ztb)
                        ba_ = bank()
                        for jt in range(4):
                            cx.op("pe", lambda e: e.matmul(out=ps[ba_][:, jt * 128:(jt + 1) * 128], lhsT=TnT[:, c, jt, :], rhs=Apc[:, jt, :], start=True, stop=True),
                                  reads=[TnTb, Apcb], writes=[psb[ba_]], signal=(jt == 3))
                        wc2 = E1[:, :, ci * 64 + 63:ci * 64 + 64].broadcast_to([128, 4, 128])
                        cx.op("pool", lambda e: e.tensor_tensor(out=ApT[:], in0=Apc[:], in1=wc2, op=ALU.mult), reads=[Apcb, E1b], writes=[ApTb])
                        cx.op("dve", lambda e: e.tensor_tensor(out=Apn[:], in0=ApT[:], in1=ps[ba_][:].rearrange("p (a c) -> p a c", c=128), op=ALU.add),
                              reads=[ApTb, psb[ba_]], writes=[Apnb])
                        apcur = 1 - apcur
                    if not gn:
                        continue
                    y3 = ps[byc][0:64, :].rearrange("p (h d) -> p h d", d=64)
                    cx.op("dve", lambda e: e.tensor_reduce(out=st1[:], in_=y3, axis=AX.X, op=ALU.add), reads=[psb[byc]], writes=[st1b])
                    cx.op("act", lambda e: e.activation(out=ysq[:], in_=ps[byc][0:64, :], func=AF.Square), reads=[psb[byc]], writes=[ysqb])
                    cx.op("dve", lambda e: e.tensor_reduce(out=st2[:], in_=ysq[:].rearrange("p (h d) -> p h d", d=64), axis=AX.X, op=ALU.add),
                          reads=[ysqb], writes=[st2b])
                    cx.op("dve", lambda e: e.tensor_scalar(out=st1[:], in0=st1[:], scalar1=1.0 / 64, scalar2=None, op0=ALU.mult), reads=[st1b], writes=[st1b])
                    cx.op("dve", lambda e: e.tensor_tensor(out=st3[:], in0=st1[:], in1=st1[:], op=ALU.mult), reads=[st1b], writes=[st3b])
                    cx.op("dve", lambda e: e.scalar_tensor_tensor(out=st2[:], in0=st2[:], scalar=1.0 / 64, in1=st3[:], op0=ALU.mult, op1=ALU.subtract),
                          reads=[st2b, st3b], writes=[st2b])
                    cx.op("act", lambda e: e.activation(out=st2[:], in_=st2[:], func=AF.Sqrt, bias=self.eps_gn[0:64, :]), reads=[st2b, self.cb], writes=[st2b])
                    cx.op("dve", lambda e: e.reciprocal(out=st2[:], in_=st2[:]), reads=[st2b], writes=[st2b])
                    cx.op("dve", lambda e: e.tensor_tensor(out=yn[:].rearrange("p (h d) -> p h d", d=64), in0=y3,
                                                           in1=st1[:].rearrange("p (h o) -> p h o", o=1).broadcast_to([64, 8, 64]), op=ALU.subtract),
                          reads=[psb[byc], st1b], writes=[ynb])
                    cx.op("dve", lambda e: e.tensor_tensor(out=yn[:].rearrange("p (h d) -> p h d", d=64), in0=yn[:].rearrange("p (h d) -> p h d", d=64),
                                                           in1=st2[:].rearrange("p (h o) -> p h o", o=1).broadcast_to([64, 8, 64]), op=ALU.mult),
                          reads=[ynb, st2b], writes=[ynb])
                    for jt in range(4):
                        cx.op("pe", lambda e: e.transpose(out=ps[by_][:, jt * 128 + c * 64:jt * 128 + (c + 1) * 64], in_=yn[:, jt * 128:(jt + 1) * 128],
                                                          identity=self.ident[0:64, 0:64]), reads=[ynb, self.cb], writes=[psb[by_]], signal=(jt == 3))
                if not gn:
                    continue
                for jt in range(4):
                    cx.op("act", lambda e: e.activation(out=yo[:, jt, :], in_=ps[by_][:, jt * 128:(jt + 1) * 128], func=AF.Identity,
                                                        scale=pv("gn_g", jt), bias=pv("gn_b", jt)), reads=[psb[by_], self.pvb], writes=[yob])
                cx.op("dve", lambda e: e.tensor_tensor(out=yo[:], in0=yo[:], in1=bonus[:, :, pc], op=ALU.add), reads=[yob, bonb], writes=[yob])
                cx.op("dve", lambda e: e.tensor_tensor(out=ys[:, :, pc], in0=yo[:], in1=G[:, :, pc], op=ALU.mult), reads=[yob, Gb], writes=[ysb])
            if mode == "A":
                cx.op("sp", lambda e: e.dma_start(out=self.GS.t[:, :, t0:t0 + RB].rearrange("f p t -> p f t"), in_=G[:]),
                      reads=[Gb], writes=self.GS.b(t0, t0 + RB), dma=Gb)
                cx.op("sp", lambda e: e.dma_start(out=self.BS.t[:, :, t0:t0 + RB].rearrange("f p t -> p f t"), in_=bonus[:]),
                      reads=[bonb], writes=self.BS.b(t0, t0 + RB), dma=bonb)
            if gn:
                cx.op("pool", lambda e: e.dma_start(out=self.YR.t[:, :, t0:t0 + RB].rearrange("f p t -> p f t"), in_=ys[:]),
                      reads=[ysb], writes=self.YR.b(t0, t0 + RB), dma=ysb)
        if mode == "A":
            pump(len(deferred))
            Apc, Apcb = Ap[stt["a"]]
            Hc, Hcb = H[stt["h"]]
            bq = bank()
            for jt in range(4):
                cx.op("pe", lambda e: e.transpose(out=ps[bq][:, jt * 128:(jt + 1) * 128], in_=Apc[:, jt, :], identity=self.ident),
                      reads=[Apcb, self.cb], writes=[psb[bq]], signal=(jt == 3))
            cx.op("act", lambda e: e.activation(out=pay[:, 0:512], in_=ps[bq][:], func=AF.Copy), reads=[psb[bq]], writes=[payb])
            cx.op("dve", lambda e: e.tensor_copy(out=pay[:, 512:768], in_=Hc[:].rearrange("p a c -> p (a c)")), reads=[Hcb], writes=[payb])
            cx.op("pool", lambda e: e.dma_start(out=self.EX2s, in_=pay[:]), reads=[payb], writes=[self.EX2sb], dma=payb)
            cx.op("pool", lambda e: e.collective_compute("AllGather", ALU.bypass, replica_groups=GROUPS, ins=[self.EX2s], outs=[self.EX2d]),
                  reads=[self.EX2sb], writes=[self.EX2db], coll=True)
        cx.end_phase()


def _phase_rwkv_out(self, l):
    cx, nc, T = self.cx, self.nc, self.T
    ps, psb = self.ps, self.psb
    RB = 256
    pv = lambda name, j: self.pv(l, name, j)
    with contextlib.ExitStack() as es:
        def sb(n, sh, dt=F32):
            return es.enter_context(nc.sbuf_tensor(uniq("o_" + n), sh, dt)), Buf("o_" + n)
        g2, g2b = sb("g2", [128, 4, 768])
        H = [sb("H%d" % i, [128, 4, 64]) for i in range(2)]
        Ht, Htb = sb("Ht", [128, 4, 64])
        zr = Ring(cx, es, "o_z", [128, 512], F32, 3)
        y0r = Ring(cx, es, "o_y0", [64, 512], F32, 3)
        gr = Ring(cx, es, "o_g", [128, 4, RB], F32, 2)
        br = Ring(cx, es, "o_b", [128, 4, RB], F32, 2)
        yst = Ring(cx, es, "o_yst", [128, 4, RB], BF16, 2)
        ysum = [sb("ysum%d" % i, [64, 512]) for i in range(2)]
        ysq = [sb("ysq%d" % i, [64, 512]) for i in range(2)]
        yn = [sb("yn%d" % i, [64, 512]) for i in range(2)]
        st1 = [sb("st1_%d" % i, [64, 8]) for i in range(2)]
        st2 = [sb("st2_%d" % i, [64, 8]) for i in range(2)]
        st3 = [sb("st3_%d" % i, [64, 8]) for i in range(2)]
        yo = [sb("yo%d" % i, [128, 4, 128]) for i in range(2)]
        cx.op("dve", lambda e: e.memset(H[0][0][:], 0.0), writes=[H[0][1]])
        cx.op("sp", lambda e: e.dma_start(out=g2[:], in_=self.EX2d.rearrange("(r p) c -> p r c", r=4)), reads=[self.EX2db], writes=[g2b], dma=g2b)
        hc_ = 0
        for r in range(4):
            Hc, Hcb = H[hc_]
            Hn, Hnb = H[1 - hc_]
            bz = 7
            for jt in range(4):
                cx.op("pe", lambda e: e.matmul(out=ps[bz][:, jt * 64:(jt + 1) * 64], lhsT=g2[:, r, jt * 128:(jt + 1) * 128], rhs=Hc[:, jt, :], start=True, stop=True),
                      reads=[g2b, Hcb], writes=[psb[bz]], signal=(jt == 3))
            cx.op("dve", lambda e: e.tensor_tensor(out=Ht[:].rearrange("p a c -> p (a c)"), in0=ps[bz][:, 0:256], in1=g2[:, r, 512:768], op=ALU.add),
                  reads=[psb[bz], g2b], writes=[Htb])
            cx.op("dve", lambda e: e.tensor_tensor(out=Ht[:], in0=Ht[:], in1=Hc[:], op=ALU.subtract), reads=[Htb, Hcb], writes=[Htb])
            cx.op("dve", lambda e: e.scalar_tensor_tensor(out=Hn[:], in0=Ht[:], scalar=self.selsb[:, 4 + r:5 + r], in1=Hc[:], op0=ALU.mult, op1=ALU.add),
                  reads=[Htb, Hcb, self.selb], writes=[Hnb])
            hc_ = 1 - hc_
        Hs, Hsb = H[hc_]
        it = 0
        for blk in range(T // RB):
            t0 = blk * RB
            G, Gb, _ = gr.next()
            bonus, bonb, _ = br.next()
            ys, ysb, _ = yst.next()
            cx.op("sp", lambda e: e.dma_start(out=G[:], in_=self.GS.t[:, :, t0:t0 + RB].rearrange("f p t -> p f t")),
                  reads=self.GS.b(t0, t0 + RB), writes=[Gb], dma=Gb)
            cx.op("sp", lambda e: e.dma_start(out=bonus[:], in_=self.BS.t[:, :, t0:t0 + RB].rearrange("f p t -> p f t")),
                  reads=self.BS.b(t0, t0 + RB), writes=[bonb], dma=bonb)
            for pr in range(RB // 128):
                pc = slice(pr * 128, (pr + 1) * 128)
                by_ = 4 + (it // 2) % 2
                for c in range(2):
                    n = t0 // 64 + pr * 2 + c
                    d = it % 2
                    it += 1
                    zt, ztb, _ = zr.next()
                    y0, y0b, _ = y0r.next()
                    cx.op("sp", lambda e: e.dma_start(out=zt[:], in_=self.ZS.t[n]), reads=self.ZS.b(n * 64, n * 64 + 64), writes=[ztb], dma=ztb)
                    cx.op("sp", lambda e: e.dma_start(out=y0[:], in_=self.Y0.t[n]), reads=self.Y0.b(n * 64, n * 64 + 64), writes=[y0b], dma=y0b)
                    byc = d
                    for h in range(8):
                        cx.op("pe", lambda e: e.matmul(out=ps[byc][0:64, h * 64:(h + 1) * 64], lhsT=zt[:, h * 64:(h + 1) * 64], rhs=Hs[:, h // 2, :], start=True, stop=True),
                              reads=[ztb, Hsb], writes=[psb[byc]], signal=(h == 7))
                    ysm, ysmb = ysum[d]
                    cx.op("dve", lambda e: e.tensor_tensor(out=ysm[:], in0=ps[byc][0:64, :], in1=y0[:], op=ALU.add), reads=[psb[byc], y0b], writes=[ysmb])
                    y3 = ysm[:].rearrange("p (h d) -> p h d", d=64)
                    s1, s1b = st1[d]
                    s2, s2b = st2[d]
                    s3, s3b = st3[d]
                    yq, yqb = ysq[d]
                    ynn, ynb = yn[d]
                    cx.op("dve", lambda e: e.tensor_reduce(out=s1[:], in_=y3, axis=AX.X, op=ALU.add), reads=[ysmb], writes=[s1b])
                    cx.op("act", lambda e: e.activation(out=yq[:], in_=ysm[:], func=AF.Square), reads=[ysmb], writes=[yqb])
                    cx.op("dve", lambda e: e.tensor_reduce(out=s2[:], in_=yq[:].rearrange("p (h d) -> p h d", d=64), axis=AX.X, op=ALU.add),
                          reads=[yqb], writes=[s2b])
                    cx.op("dve", lambda e: e.tensor_scalar(out=s1[:], in0=s1[:], scalar1=1.0 / 64, scalar2=None, op0=ALU.mult), reads=[s1b], writes=[s1b])
                    cx.op("dve", lambda e: e.tensor_tensor(out=s3[:], in0=s1[:], in1=s1[:], op=ALU.mult), reads=[s1b], writes=[s3b])
                    cx.op("dve", lambda e: e.scalar_tensor_tensor(out=s2[:], in0=s2[:], scalar=1.0 / 64, in1=s3[:], op0=ALU.mult, op1=ALU.subtract),
                          reads=[s2b, s3b], writes=[s2b])
                    cx.op("act", lambda e: e.activation(out=s2[:], in_=s2[:], func=AF.Sqrt, bias=self.eps_gn[0:64, :]), reads=[s2b, self.cb], writes=[s2b])
                    cx.op("dve", lambda e: e.reciprocal(out=s2[:], in_=s2[:]), reads=[s2b], writes=[s2b])
                    cx.op("pool", lambda e: e.tensor_tensor(out=ynn[:].rearrange("p (h d) -> p h d", d=64), in0=y3,
                                                            in1=s1[:].rearrange("p (h o) -> p h o", o=1).broadcast_to([64, 8, 64]), op=ALU.subtract),
                          reads=[ysmb, s1b], writes=[ynb])
                    cx.op("dve", lambda e: e.tensor_tensor(out=ynn[:].rearrange("p (h d) -> p h d", d=64), in0=ynn[:].rearrange("p (h d) -> p h d", d=64),
                                                           in1=s2[:].rearrange("p (h o) -> p h o", o=1).broadcast_to([64, 8, 64]), op=ALU.mult),
                          reads=[ynb, s2b], writes=[ynb])
                    for jt in range(4):
                        cx.op("pe", lambda e: e.transpose(out=ps[by_][:, jt * 128 + c * 64:jt * 128 + (c + 1) * 64], in_=ynn[:, jt * 128:(jt + 1) * 128],
                                                          identity=self.ident[0:64, 0:64]), reads=[ynb, self.cb], writes=[psb[by_]], signal=(jt == 3))
                yo_, yob = yo[(it // 2) % 2]
                for jt in range(4):
                    cx.op("act", lambda e: e.activation(out=yo_[:, jt, :], in_=ps[by_][:, jt * 128:(jt + 1) * 128], func=AF.Identity,
                                                        scale=pv("gn_g", jt), bias=pv("gn_b", jt)), reads=[psb[by_], self.pvb], writes=[yob])
                cx.op("pool", lambda e: e.tensor_tensor(out=yo_[:], in0=yo_[:], in1=bonus[:, :, pc], op=ALU.add), reads=[yob, bonb], writes=[yob])
                cx.op("dve", lambda e: e.tensor_tensor(out=ys[:, :, pc], in0=yo_[:], in1=G[:, :, pc], op=ALU.mult), reads=[yob, Gb], writes=[ysb])
            cx.op("pool", lambda e: e.dma_start(out=self.YR.t[:, :, t0:t0 + RB].rearrange("f p t -> p f t"), in_=ys[:]),
                  reads=[ysb], writes=self.YR.b(t0, t0 + RB), dma=ysb)
        cx.end_phase()


Builder.phase_rwkv_out = _phase_rwkv_out


Builder.decl_rwkv = _decl_rwkv
Builder.phase_rwkv = _phase_rwkv


def prep_rwkv(inp, L, sh):
    lora = np.zeros((L, 128, 3, 512), np.float32)
    for l in range(L):
        lora[l, 0:64, 0] = inp["decay_lora_b"][l]
        lora[l, 64:128, 0] = inp["iclr_lora_b"][l]
        lora[l, :, 1] = inp["gate_lora_b"][l]
        if l > 0:
            lora[l, 0:32, 2] = inp["vres_lora_b"][l - 1]
    sh["lora"] = lora.reshape(L, 128, 3 * 512)
    i = np.arange(128)[:, None]
    j = np.arange(128)[None, :]
    same = (i // 64) == (j // 64)
    SL = (same & (j < i)).astype(np.float32)
    SU = (same & (j > i)).astype(np.float32)
    UI = (same & (j >= i)).astype(np.float32)
    rm = np.stack([-SL, SL, -SU, -UI, UI], axis=1)
    sh["rmask"] = np.ascontiguousarray(rm).reshape(128, 640)


def build_full(T, L, seg=False):
    b = Builder(T, L)
    b.decl_mix()
    b.decl_att()
    b.decl_merge()
    b.decl_rwkv()
    if seg:
        b.decl_seg()
    b.cast_weights()
    for l in range(L):
        src = b.xin if l == 0 else b.XS
        b.phase_ffn(l, 0, src, b.XS, "norm_ffn1")
        if seg:
            b.phase_proj(l, b.XS, hook=lambda: b.phase_ex1(l))
            b.phase_ex1_select(l)
            b.phase_rwkv(l, "A")
            b.phase_att(l)
            b.phase_rwkv_out(l)
        else:
            b.phase_proj(l, b.XS)
            b.phase_rwkv(l)
            b.phase_att(l)
        b.phase_merge(l, b.XS, b.XS)
        b.phase_ffn(l, 1, b.XS, b.out if l == L - 1 else b.XS, "norm_ffn2")
    b.cx.final_wait()
    return b


def seg_selectors(s):
    sel = np.zeros((128, 4), np.float32)
    pre = np.zeros((128, 4), np.float32)
    if s > 0:
        sel[:, s - 1] = 1.0
    pre[:, :s] = 1.0
    return sel, pre


def prep_all(inp, L):
    sh = prep_shared(inp, L)
    prep_mix(inp, L, sh)
    prep_att(inp, L, sh)
    prep_merge(inp, L, sh)
    prep_rwkv(inp, L, sh)
    return sh


def kernel(**inputs):
    x = np.asarray(inputs["x"], np.float32)
    mem = np.asarray(inputs["mem"], np.float32)
    B, S, _ = x.shape
    L = int(np.asarray(inputs["norm_ffn1"]).shape[0])
    NSEG = 4
    T = S // NSEG
    b = build_full(T, L, seg=True)
    sh = prep_all(inputs, L)
    in_maps = []
    for c in range(B * NSEG):
        bi, si = c // NSEG, c % NSEG
        m = {k: sh[k] for k in b.win if k in sh}
        m["sel"], m["pre"] = seg_selectors(si)
        m["xT"] = to_fm(x[bi, si * T:(si + 1) * T])
        m["memT"] = to_fm(mem[bi])
        in_maps.append(m)
    res = run_bass_kernel_spmd(b.nc, in_maps, core_ids=list(range(B * NSEG)))
    out = np.zeros((B, S, D), np.float32)
    for c, r in enumerate(res.results):
        bi, si = c // NSEG, c % NSEG
        out[bi, si * T:(si + 1) * T] = from_fm(np.asarray(r["outT"], np.float32))
    return out


GROUPS = [[0, 1, 2, 3], [4, 5, 6, 7]]
EXA = 2048
EXB = 2112


def _decl_seg(self):
    nc, es = self.nc, self.es
    self.seg = True
    self.wdecl("sel", [128, 4], cast=False)
    self.wdecl("pre", [128, 4], cast=False)
    self.EX1s = nc.dram_tensor("EX1s", [128, EXA], BF16, kind="Internal").ap()
    self.EX1d = nc.dram_tensor("EX1d", [4 * 128, EXA], BF16, kind="Internal").ap()
    self.EX1bs = nc.dram_tensor("EX1bs", [128, EXB], BF16, kind="Internal").ap()
    self.EX1bd = nc.dram_tensor("EX1bd", [4 * 128, EXB], BF16, kind="Internal").ap()
    self.EX1bsb, self.EX1bdb = Buf("EX1bs"), Buf("EX1bd")
    self.EX2s = nc.dram_tensor("EX2s", [128, 768], F32, kind="Internal").ap()
    self.EX2d = nc.dram_tensor("EX2d", [4 * 128, 768], F32, kind="Internal").ap()
    self.EX1sb, self.EX1db, self.EX2sb, self.EX2db = Buf("EX1s"), Buf("EX1d"), Buf("EX2s"), Buf("EX2d")
    self.KH = DramT(nc, "KH", [4, 128, 512], BF16)
    self.VH = DramT(nc, "VH", [4, 128, 520], BF16)
    T = self.T
    self.Y0 = DramT(nc, "Y0", [T // 64, 64, 512], F32, gran=64)
    self.ZS = DramT(nc, "ZS", [T // 64, 128, 512], F32, gran=64)
    self.GS = DramT(nc, "GS", [4, 128, T], F32)
    self.BS = DramT(nc, "BS", [4, 128, T], F32)
    self.SH = nc.dram_tensor("SH", [128, 14], F32, kind="Internal").ap()
    self.SHb = Buf("SH")
    self.selsb = es.enter_context(nc.sbuf_tensor("sb_sel", [128, 8], F32))
    self.selb = Buf("sel", const=True)
    self.shcol = es.enter_context(nc.sbuf_tensor("sb_shcol", [128, 14], F32))
    self.shcolb = Buf("shcol")
    cx = self.cx
    cx.op("sp", lambda e: e.dma_start(out=self.selsb[:, 0:4], in_=self.win["sel"]), writes=[self.selb], dma=self.selb)
    cx.op("sp", lambda e: e.dma_start(out=self.selsb[:, 4:8], in_=self.win["pre"]), writes=[self.selb], dma=self.selb)
    cx.keep_sems()


def _phase_ex1(self, l):
    cx, nc, T = self.cx, self.nc, self.T
    cx.op("sp", lambda e: e.dma_start(out=self.EX1s[:, 0:2048].rearrange("p (f t) -> p f t", f=4),
                                      in_=self.KT.t[:, :, T - 512:T].rearrange("f p t -> p f t")),
          reads=self.KT.b(T - 512, T), writes=[self.EX1sb], dma=self.EX1sb)
    cx.op("sp", lambda e: e.dma_start(out=self.EX1bs[:, 0:2080].rearrange("p (n c) -> p n c", n=4),
                                      in_=self.V1.t[T // 128 - 4:T // 128].rearrange("n p c -> p n c")),
          reads=self.V1.b(T - 512, T), writes=[self.EX1bsb], dma=self.EX1bsb)
    cx.op("sp", lambda e: e.dma_start(out=self.EX1bs[:, 2080:2108].bitcast(F32), in_=self.shcol[:]),
          reads=[self.shcolb], writes=[self.EX1bsb], dma=self.EX1bsb)
    cx.op("pool", lambda e: e.collective_compute("AllGather", ALU.bypass, replica_groups=GROUPS, ins=[self.EX1s], outs=[self.EX1d]),
          reads=[self.EX1sb], writes=[self.EX1db], coll=True)
    cx.op("pool", lambda e: e.collective_compute("AllGather", ALU.bypass, replica_groups=GROUPS, ins=[self.EX1bs], outs=[self.EX1bd]),
          reads=[self.EX1bsb], writes=[self.EX1bdb], coll=True)


def _phase_ex1_select(self, l):
    cx, nc, T = self.cx, self.nc, self.T
    with contextlib.ExitStack() as es:
        g = es.enter_context(nc.sbuf_tensor(uniq("x1_g"), [128, 4, EXA], BF16))
        g2_ = es.enter_context(nc.sbuf_tensor(uniq("x1_g2"), [128, 4, EXB], BF16))
        acc = es.enter_context(nc.sbuf_tensor(uniq("x1_acc"), [128, 4128], BF16))
        accf = es.enter_context(nc.sbuf_tensor(uniq("x1_accf"), [128, 14], F32))
        gb, g2b_, accb, accb2, accfb = Buf("x1_g"), Buf("x1_g2"), Buf("x1_acc"), Buf("x1_acc2"), Buf("x1_accf")
        cx.op("sp", lambda e: e.dma_start(out=g[:], in_=self.EX1d.rearrange("(r p) c -> p r c", r=4)), reads=[self.EX1db], writes=[gb], dma=gb)
        cx.op("sp", lambda e: e.dma_start(out=g2_[:], in_=self.EX1bd.rearrange("(r p) c -> p r c", r=4)), reads=[self.EX1bdb], writes=[g2b_], dma=g2b_)
        for r in range(4):
            sc = self.selsb[:, r:r + 1]
            gf = g2_[:, r, 2080:2108].bitcast(F32)
            if r == 0:
                cx.op("dve", lambda e: e.tensor_scalar(out=acc[:, 0:2048], in0=g[:, r, :], scalar1=sc, scalar2=None, op0=ALU.mult),
                      reads=[gb, self.selb], writes=[accb])
                cx.op("dve", lambda e: e.tensor_scalar(out=acc[:, 2048:4128], in0=g2_[:, r, 0:2080], scalar1=sc, scalar2=None, op0=ALU.mult),
                      reads=[g2b_, self.selb], writes=[accb2])
                cx.op("pool", lambda e: e.tensor_scalar(out=accf[:], in0=gf, scalar1=sc, scalar2=None, op0=ALU.mult),
                      reads=[g2b_, self.selb], writes=[accfb])
            else:
                cx.op("dve", lambda e: e.scalar_tensor_tensor(out=acc[:, 0:2048], in0=g[:, r, :], scalar=sc, in1=acc[:, 0:2048], op0=ALU.mult, op1=ALU.add),
                      reads=[gb, self.selb, accb], writes=[accb])
                cx.op("dve", lambda e: e.scalar_tensor_tensor(out=acc[:, 2048:4128], in0=g2_[:, r, 0:2080], scalar=sc, in1=acc[:, 2048:4128], op0=ALU.mult, op1=ALU.add),
                      reads=[g2b_, self.selb, accb2], writes=[accb2])
                cx.op("dve", lambda e: e.scalar_tensor_tensor(out=accf[:], in0=gf, scalar=sc, in1=accf[:], op0=ALU.mult, op1=ALU.add),
                      reads=[g2b_, self.selb, accfb], writes=[accfb])
        cx.op("pool", lambda e: e.dma_start(out=self.KH.t.rearrange("f p t -> p f t"), in_=acc[:, 0:2048].rearrange("p (f t) -> p f t", f=4)),
              reads=[accb], writes=self.KH.b(0, 512), dma=accb)
        cx.op("pool", lambda e: e.dma_start(out=self.VH.t.rearrange("n p c -> p n c"), in_=acc[:, 2048:4128].rearrange("p (n c) -> p n c", n=4)),
              reads=[accb2], writes=self.VH.b(0, 512), dma=accb2)
        cx.op("pool", lambda e: e.dma_start(out=self.SH, in_=accf[:]), reads=[accfb], writes=[self.SHb], dma=accfb)
        cx.end_phase()


Builder.decl_seg = _decl_seg
Builder.phase_ex1 = _phase_ex1
Builder.phase_ex1_select = _phase_ex1_select
```

```python
import contextlib
import numpy as np
import concourse.bass as bass
import concourse.mybir as mybir
from concourse.bass_utils import run_bass_kernel_spmd

F32 = mybir.dt.float32
BF16 = mybir.dt.bfloat16
AF = mybir.ActivationFunctionType
ALU = mybir.AluOpType
AX = mybir.AxisListType

D = 1024
DFF = 2816
NJ = DFF // 128
RWKV_IN = 1792
D_IN = 3840
NH = 8
HD = 64
CH = 64
MEMT = 256
RMS_EPS = 1e-6
GN_EPS = 64e-5

PV = {}
_off = 0
for _n, _w in [("norm_ffn1", 8), ("norm_mix", 8), ("norm_ffn2", 8), ("shift_mu", 14), ("decay_w0", 4),
               ("iclr_a0", 4), ("k_k", 4), ("k_a", 4), ("r_k", 4), ("gn_g", 4), ("gn_b", 4), ("vres_v0", 4),
               ("att_q_norm", 1), ("att_k_norm", 1), ("mem_q_norm", 1), ("mem_k_norm", 1), ("norm_mem", 8),
               ("b_gate", 24)]:
    PV[_n] = _off
    _off += _w
NPV = _off


_UID = [0]


def uniq(name):
    _UID[0] += 1
    return "%s_u%d" % (name, _UID[0])


class Buf:
    __slots__ = ("name", "w", "r", "const", "sem", "sem_sw")

    def __init__(self, name, const=False):
        self.name = name
        self.w = None
        self.r = {}
        self.const = const
        self.sem = None
        self.sem_sw = None


class Ctx:
    def __init__(self, nc, n_dma_sems=36):
        self.nc = nc
        self.es = contextlib.ExitStack()
        self.eng = {"pe": nc.tensor, "act": nc.scalar, "dve": nc.vector, "pool": nc.gpsimd, "sp": nc.sync}
        self.sems = {}
        self.cnt = {}
        for e in ["pe", "act", "dve", "pool", "sp"]:
            self.sems[e] = self.es.enter_context(nc.semaphore("prog_" + e))
            self.cnt[e] = 0
        self.known = {e: {} for e in self.eng}
        self.free_dma = []
        for i in range(n_dma_sems):
            k = "dma%d" % i
            self.sems[k] = self.es.enter_context(nc.semaphore(k))
            self.cnt[k] = 0
            self.free_dma.append(k)
        self.ninst = 0
        self.phase_sems = []
        self.free_sw = []
        self.free_cc = []
        for i in range(12):
            k = "cc%d" % i
            self.sems[k] = self.es.enter_context(nc.semaphore(k))
            self.cnt[k] = 0
            self.free_cc.append(k)
        for i in range(48):
            k = "swd%d" % i
            self.sems[k] = self.es.enter_context(nc.semaphore(k))
            self.cnt[k] = 0
            self.free_sw.append(k)

    def dma_sem(self):
        return self.free_dma.pop(0)

    def release_dma_sem(self, k):
        self.free_dma.append(k)

    def _wait(self, e, deps):
        eng = self.eng[e]
        kn = self.known[e]
        best = {}
        for (s, v) in deps:
            if s == "pe" and e == "pe":
                continue
            if kn.get(s, 0) < v and best.get(s, 0) < v:
                best[s] = v
        for s, v in best.items():
            eng.wait_ge(self.sems[s], v)
            kn[s] = v

    def op(self, e, fn, reads=(), writes=(), dma=None, signal=True, coll=False):
        deps = []
        for b in reads:
            if b.w is not None:
                deps.append(b.w)
        for b in writes:
            if b.w is not None:
                deps.append(b.w)
            deps.extend(b.r.items())
        self._wait(e, deps)
        inst = fn(self.eng[e])
        if coll:
            k = self.free_cc.pop(0)
            self.cnt[k] += 1
            inst.then_inc(self.sems[k], 1)
            tok = (k, self.cnt[k])
        elif dma is not None:
            if e == "pool":
                if dma.sem_sw is None:
                    dma.sem_sw = self.free_sw.pop(0)
                    self.phase_sems.append(dma)
                dma = dma.sem_sw
            else:
                if dma.sem is None:
                    dma.sem = self.free_dma.pop(0)
                    self.phase_sems.append(dma)
                dma = dma.sem
            self.cnt[dma] += 16
            inst.then_inc(self.sems[dma], 16)
            tok = (dma, self.cnt[dma])
        elif signal:
            self.cnt[e] += 1
            inst.then_inc(self.sems[e], 1)
            tok = (e, self.cnt[e])
        else:
            tok = (e, self.cnt[e] + 1)
        for b in reads:
            if not b.const:
                if b.r.get(tok[0], 0) < tok[1]:
                    b.r[tok[0]] = tok[1]
        for b in writes:
            b.w = tok
            b.r = {}
        self.ninst += 1
        return tok

    def barrier(self):
        allk = [(k, v) for k, v in self.cnt.items() if v > 0]
        for e in self.eng:
            self._wait(e, [(k, v) for (k, v) in allk if not (k == e)])

    def end_phase(self):
        self.barrier()
        for b in self.phase_sems:
            if b.sem is not None:
                self.free_dma.append(b.sem)
                b.sem = None
            if b.sem_sw is not None:
                self.free_sw.append(b.sem_sw)
                b.sem_sw = None
        self.phase_sems = []

    def keep_sems(self):
        self.phase_sems = []

    def final_wait(self):
        allk = [(k, v) for k, v in self.cnt.items() if v > 0]
        self._wait("sp", [(k, v) for (k, v) in allk if k != "sp"])


class Ring:
    def __init__(self, cx, es, name, shape, dtype, n):
        self.cx = cx
        self.slots = []
        for i in range(n):
            t = es.enter_context(cx.nc.sbuf_tensor(uniq("%s%d" % (name, i)), shape, dtype))
            bf = Buf("%s%d" % (name, i))
            self.slots.append((t, bf, bf))
        self.i = 0

    def next(self):
        s = self.slots[self.i % len(self.slots)]
        self.i += 1
        return s


class DramT:
    def __init__(self, nc, name, shape, dtype, kind="Internal", gran=512):
        self.t = nc.dram_tensor(name, shape, dtype, kind=kind).ap()
        self.gran = gran
        self.name = name
        self.bufs = {}

    def b(self, t0, t1):
        out = []
        for g in range(t0 // self.gran, (t1 - 1) // self.gran + 1):
            if g not in self.bufs:
                self.bufs[g] = Buf("%s_%d" % (self.name, g))
            out.append(self.bufs[g])
        return out


def fm_layout(W):
    K, N = W.shape
    return np.ascontiguousarray(W.reshape(K // 128, 128, N // 128, 128).transpose(2, 1, 0, 3))


def mv_layout(W):
    K, N = W.shape
    return np.ascontiguousarray(W.reshape(K // 128, 128, N).transpose(1, 0, 2))


class Builder:
    def __init__(self, T, L, debug=False):
        self.T = T
        self.L = L
        self.debug = debug
        nc = self.nc = bass.Bass("TRN2", target_bir_lowering=False)
        cx = self.cx = Ctx(nc)
        es = self.es = cx.es
        self.xin = DramT(nc, "xT", [8, 128, T], F32, kind="ExternalInput")
        self.out = DramT(nc, "outT", [8, 128, T], F32, kind="ExternalOutput")
        self.XS = DramT(nc, "XS", [8, 128, T], F32)
        self.win = {}
        self.wbf = {}
        self.wbuf = {}

        def wdecl(name, shape, cast=True):
            self.win[name] = nc.dram_tensor(name, shape, F32, kind="ExternalInput").ap()
            if cast:
                self.wbf[name] = nc.dram_tensor(name + "_bf", shape, BF16, kind="Internal").ap()
                if name.startswith("ffn"):
                    self.wbuf[name] = [Buf("%s_l%d" % (name, i)) for i in range(shape[0])]
                else:
                    if not hasattr(self, "_restbuf"):
                        self._restbuf = [Buf("wrest_l%d" % i) for i in range(shape[0])]
                    self.wbuf[name] = self._restbuf
        self.wdecl = wdecl
        wdecl("ffn_wi", [L, 2, NJ, 128, 8 * 256])
        wdecl("ffn_wo", [L, 2, 8, 128, NJ * 128])
        wdecl("pvec", [L, 128, NPV], cast=False)
        wdecl("consts", [128, 1024], cast=False)
        self.pvec = es.enter_context(nc.sbuf_tensor("sb_pvec", [128, L, NPV], F32))
        self.pvb = Buf("pvec", const=True)
        self.consts = es.enter_context(nc.sbuf_tensor("sb_consts", [128, 1024], F32))
        self.cb = Buf("consts", const=True)
        cx.op("sp", lambda e: e.dma_start(out=self.pvec[:], in_=self.win["pvec"].rearrange("l p n -> p l n")),
              writes=[self.pvb], dma=self.pvb)
        cx.op("sp", lambda e: e.dma_start(out=self.consts[:], in_=self.win["consts"]), writes=[self.cb], dma=self.cb)
        cx.keep_sems()
        self.ones = self.consts[:, 0:128]
        self.ident = self.consts[:, 128:256]
        self.blk64 = self.consts[:, 256:384]
        self.eps_rms = self.consts[:, 384:385]
        self.eps_q = self.consts[:, 385:386]
        self.eps_mq = self.consts[:, 386:387]
        self.eps_gn = self.consts[:, 387:388]
        self.cmask = self.consts[:, 512:1024]
        self.ps = []
        self.psb = []
        for i in range(8):
            self.ps.append(es.enter_context(nc.psum_tensor("ps%d" % i, [128, 512], F32)))
            self.psb.append(Buf("ps%d" % i))

    def pv(self, l, name, i=0, n=1):
        c = PV[name] + i
        return self.pvec[:, l, c:c + n]

    def cast_weights(self):
        cx = self.cx
        for l in range(self.L):
            for name, dst in self.wbf.items():
                src = self.win[name]
                wb = self.wbuf[name][l]
                cx.op("pool", lambda e: e.dma_start(out=dst[l], in_=src[l]), writes=[wb], dma=wb)
                wb.const = True
        cx.keep_sems()

    def rmsnorm_fm(self, xT, xb, cs, gains, xn, xnb, tmp, nft=8, w=512, dim=D):
        cx, ps, psb = self.cx, self.ps, self.psb
        sq, sqb, ss, ssb, rstd, rsb = tmp
        cx.op("act", lambda e: e.activation(out=sq[:, 0:nft, 0:w], in_=xT[:, 0:nft, cs], func=AF.Square),
              reads=[xb], writes=[sqb])
        cx.op("dve", lambda e: e.tensor_reduce(out=ss[:, 0:w], in_=sq[:, 0:nft, 0:w].rearrange("p f t -> p t f"),
                                               axis=AX.X, op=ALU.add), reads=[sqb], writes=[ssb])
        cx.op("pe", lambda e: e.matmul(out=ps[6][:, 0:w], lhsT=self.ones, rhs=ss[:, 0:w], start=True, stop=True),
              reads=[ssb, self.cb], writes=[psb[6]])
        cx.op("act", lambda e: e.activation(out=rstd[:, 0:w], in_=ps[6][:, 0:w], func=AF.Ln,
                                            scale=1.0 / dim, bias=self.eps_rms), reads=[psb[6], self.cb], writes=[rsb])
        cx.op("act", lambda e: e.activation(out=rstd[:, 0:w], in_=rstd[:, 0:w], func=AF.Exp, scale=-0.5), reads=[rsb], writes=[rsb])
        for ft in range(nft):
            cx.op("dve", lambda e: e.scalar_tensor_tensor(
                out=xn[:, ft, cs], in0=xT[:, ft, cs], scalar=gains(ft), in1=rstd[:, 0:w],
                op0=ALU.mult, op1=ALU.mult), reads=[xb, rsb, self.pvb], writes=[xnb])

    def norm_tmp(self, es, pfx):
        nc = self.nc
        sq = es.enter_context(nc.sbuf_tensor(uniq(pfx + "_sq"), [128, 8, 512], F32))
        ss = es.enter_context(nc.sbuf_tensor(uniq(pfx + "_ss"), [128, 512], F32))
        rstd = es.enter_context(nc.sbuf_tensor(uniq(pfx + "_rstd"), [128, 512], F32))
        return (sq, Buf(pfx + "_sq"), ss, Buf(pfx + "_ss"), rstd, Buf(pfx + "_rstd"))

    def phase_ffn(self, l, f, src, dst, gname):
        cx, nc, T = self.cx, self.nc, self.T
        TB = min(1024, T)
        NTC = TB // 512
        ps, psb = self.ps, self.psb
        with contextlib.ExitStack() as es:
            xT = es.enter_context(nc.sbuf_tensor(uniq("f_xT"), [128, 8, TB], F32))
            xn = es.enter_context(nc.sbuf_tensor(uniq("f_xn"), [128, 8, TB], BF16))
            hT = es.enter_context(nc.sbuf_tensor(uniq("f_hT"), [128, NJ, TB], BF16))
            ntmp = self.norm_tmp(es, "f")
            sg = [es.enter_context(nc.sbuf_tensor(uniq("f_sg%d" % i), [128, 512], F32)) for i in range(2)]
            xb = [Buf("f_xT%d" % i) for i in range(NTC)]
            xnb = [Buf("f_xn%d" % i) for i in range(NTC)]
            hb = [Buf("f_hT%d" % i) for i in range(NTC)]
            sgb = [Buf("f_sg0"), Buf("f_sg1")]
            w1r = Ring(cx, es, "f_w1", [128, 8, 256], BF16, 5)
            wor = Ring(cx, es, "f_wo", [128, NJ, 128], BF16, 3)
            w1d = self.wbf["ffn_wi"]
            wod = self.wbf["ffn_wo"]
            it = 0
            for blk in range(T // TB):
                t0 = blk * TB
                for tc in range(NTC):
                    a, b = t0 + tc * 512, t0 + (tc + 1) * 512
                    cs = slice(tc * 512, (tc + 1) * 512)
                    cx.op("sp", lambda e: e.dma_start(out=xT[:, :, cs], in_=src.t[:, :, a:b].rearrange("f p t -> p f t")),
                          reads=src.b(a, b), writes=[xb[tc]], dma=xb[tc])
                    self.rmsnorm_fm(xT, xb[tc], cs, lambda ft: self.pv(l, gname, ft), xn, xnb[tc], ntmp)
                for j in range(NJ):
                    w1, w1b, w1k = w1r.next()
                    cx.op("sp", lambda e: e.dma_start(out=w1[:], in_=w1d[l, f, j].rearrange("p (k c) -> p k c", c=256)),
                          reads=[self.wbuf["ffn_wi"][l]], writes=[w1b], dma=w1k)
                    for tc in range(NTC):
                        cs = slice(tc * 512, (tc + 1) * 512)
                        pg, pu = it % 2, 2 + it % 2
                        for kt in range(8):
                            cx.op("pe", lambda e: e.matmul(out=ps[pg][:], lhsT=w1[:, kt, 0:128], rhs=xn[:, kt, cs],
                                                           start=(kt == 0), stop=(kt == 7)),
                                  reads=[w1b, xnb[tc]], writes=[psb[pg]], signal=(kt == 7))
                        for kt in range(8):
                            cx.op("pe", lambda e: e.matmul(out=ps[pu][:], lhsT=w1[:, kt, 128:256], rhs=xn[:, kt, cs],
                                                           start=(kt == 0), stop=(kt == 7)),
                                  reads=[w1b, xnb[tc]], writes=[psb[pu]], signal=(kt == 7))
                        s = it % 2
                        cx.op("act", lambda e: e.activation(out=sg[s][:], in_=ps[pg][:], func=AF.Silu),
                              reads=[psb[pg]], writes=[sgb[s]])
                        cx.op("dve", lambda e: e.tensor_tensor(out=hT[:, j, cs], in0=sg[s][:], in1=ps[pu][:], op=ALU.mult),
                              reads=[sgb[s], psb[pu]], writes=[hb[tc]])
                        it += 1
                for o in range(8):
                    wo, wob, wok = wor.next()
                    cx.op("sp", lambda e: e.dma_start(out=wo[:], in_=wod[l, f, o].rearrange("p (k c) -> p k c", c=128)),
                          reads=[self.wbuf["ffn_wo"][l]], writes=[wob], dma=wok)
                    for tc in range(NTC):
                        cs = slice(tc * 512, (tc + 1) * 512)
                        po = 4 + it % 2
                        it += 1
                        for kt in range(NJ):
                            cx.op("pe", lambda e: e.matmul(out=ps[po][:], lhsT=wo[:, kt, :], rhs=hT[:, kt, cs],
                                                           start=(kt == 0), stop=(kt == NJ - 1)),
                                  reads=[wob, hb[tc]], writes=[psb[po]], signal=(kt == NJ - 1))
                        cx.op("dve", lambda e: e.scalar_tensor_tensor(out=xT[:, o, cs], in0=ps[po][:], scalar=0.5,
                                                                      in1=xT[:, o, cs], op0=ALU.mult, op1=ALU.add),
                              reads=[psb[po], xb[tc]], writes=[xb[tc]])
                for tc in range(NTC):
                    a, b = t0 + tc * 512, t0 + (tc + 1) * 512
                    cs = slice(tc * 512, (tc + 1) * 512)
                    cx.op("pool", lambda e: e.dma_start(out=dst.t[:, :, a:b].rearrange("f p t -> p f t"), in_=xT[:, :, cs]),
                          reads=[xb[tc]], writes=dst.b(a, b), dma=xb[tc])
            cx.end_phase()


def make_consts():
    c = np.zeros((128, 1024), np.float32)
    c[:, 0:128] = 1.0
    c[:, 128:256] = np.eye(128, dtype=np.float32)
    blk = np.zeros((128, 128), np.float32)
    blk[:64, :64] = 1.0
    blk[64:, 64:] = 1.0
    c[:, 256:384] = blk
    c[:, 384] = RMS_EPS
    c[:, 385] = 64 * RMS_EPS
    c[:, 386] = 128 * RMS_EPS
    c[:, 387] = GN_EPS
    c[:, 512:1024] = 1.0
    c[:, 512:1024:64] = 0.0
    return c


def prep_shared(inp, L):
    g = {k: np.asarray(v, np.float32) for k, v in inp.items() if k not in ("x", "mem")}
    sh = {}
    wi = np.zeros((L, 2, NJ, 128, 8 * 256), np.float32)
    wo = np.zeros((L, 2, 8, 128, NJ * 128), np.float32)
    for l in range(L):
        for f, (a, b) in enumerate([("ffn1_w_in", "ffn1_w_out"), ("ffn2_w_in", "ffn2_w_out")]):
            W = g[a][l]
            G = fm_layout(W[:, :DFF])
            U = fm_layout(W[:, DFF:])
            wi[l, f] = np.concatenate([G, U], axis=3).reshape(NJ, 128, 8 * 256)
            wo[l, f] = fm_layout(g[b][l]).reshape(8, 128, NJ * 128)
    sh["ffn_wi"] = wi
    sh["ffn_wo"] = wo
    pv = np.zeros((L, 128, NPV), np.float32)

    def put(l, name, vec):
        n = vec.size // 128
        pv[l, :, PV[name]:PV[name] + n] = vec.reshape(n, 128).T
    for l in range(L):
        put(l, "norm_ffn1", g["norm_ffn1"][l])
        put(l, "norm_mix", g["norm_mix"][l])
        put(l, "norm_ffn2", g["norm_ffn2"][l])
        put(l, "shift_mu", g["shift_mu"][l])
        put(l, "decay_w0", g["decay_w0"][l])
        put(l, "iclr_a0", g["iclr_a0"][l])
        put(l, "k_k", g["rwkv_k_k"][l])
        put(l, "k_a", g["rwkv_k_a"][l])
        put(l, "r_k", g["rwkv_r_k"][l].reshape(-1))
        put(l, "gn_g", g["rwkv_gn_g"][l])
        put(l, "gn_b", g["rwkv_gn_b"][l])
        if l > 0:
            put(l, "vres_v0", g["vres_v0"][l - 1])
        put(l, "att_q_norm", np.tile(g["att_q_norm"][l], 2))
        put(l, "att_k_norm", np.tile(g["att_k_norm"][l], 2))
        put(l, "mem_q_norm", g["mem_q_norm"][l])
        put(l, "mem_k_norm", g["mem_k_norm"][l])
        put(l, "norm_mem", g["norm_mem"][l])
        put(l, "b_gate", g["b_gate"][l])
    sh["pvec"] = pv
    sh["consts"] = make_consts()
    return sh


def to_fm(x):
    T = x.shape[0]
    return np.ascontiguousarray(x.T.reshape(x.shape[1] // 128, 128, T))


def from_fm(y):
    n, p, T = y.shape
    return np.ascontiguousarray(y.reshape(n * p, T).T)


def _decl_mix(self):
    L, T, nc = self.L, self.T, self.nc
    self.wdecl("wmix", [L, 26, 128, 8 * 128])
    self.wdecl("wvres", [L, 128, 8 * 32])
    self.wdecl("wv", [L, 128, 8 * 512])
    self.wdecl("wmk", [L, 4, 128, 8 * 128])
    self.wdecl("wmv", [L, 128, 8 * 512])
    self.memT = DramT(nc, "memT", [8, 128, MEMT], F32, kind="ExternalInput")
    self.HN = DramT(nc, "HN", [8, 128, T], BF16)
    self.PR = DramT(nc, "PR", [15, 128, T], F32)
    self.QT = DramT(nc, "QT", [4, 128, T], BF16)
    self.KT = DramT(nc, "KT", [4, 128, T], BF16)
    self.V1 = DramT(nc, "V1", [T // 128, 128, 520], BF16)
    self.YM = DramT(nc, "YM", [4, 128, T], BF16)
    self.YA = DramT(nc, "YA", [4, 128, T], BF16)
    self.YR = DramT(nc, "YR", [4, 128, T], BF16)
    self.VF = DramT(nc, "VF", [4, 128, T], F32)


def _phase_proj(self, l, src, hook=None):
    cx, nc, T = self.cx, self.nc, self.T
    TB = min(1024, T)
    NTC = TB // 512
    ps, psb = self.ps, self.psb
    with contextlib.ExitStack() as es:
        sb = lambda n, sh, dt: es.enter_context(nc.sbuf_tensor(uniq("p_" + n), sh, dt))
        xT = sb("xT", [128, 8, TB], F32)
        hn = sb("hn", [128, 8, TB], BF16)
        xb = [Buf("p_xT%d" % i) for i in range(NTC)]
        hb = [Buf("p_hn%d" % i) for i in range(NTC)]
        ntmp = self.norm_tmp(es, "p")
        wr = Ring(cx, es, "p_w", [128, 8, 128], BF16, 3)
        stf = Ring(cx, es, "p_stf", [128, 512], F32, 3)
        stb = Ring(cx, es, "p_stb", [128, 512], BF16, 3)
        stv = Ring(cx, es, "p_stv", [128, 8, 65], BF16, 2)
        wv = sb("wv", [128, 8, 512], BF16)
        wvb = Buf("p_wv")
        wvr = sb("wvr", [128, 8, 32], BF16)
        wvrb = Buf("p_wvr")
        sq2 = [sb("sq2_%d" % i, [128, 512], F32) for i in range(2)]
        sq2b = [Buf("p_sq2_%d" % i) for i in range(2)]
        rs2 = [sb("rs2_%d" % i, [128, 512], F32) for i in range(2)]
        rs2b = [Buf("p_rs2_%d" % i) for i in range(2)]
        mqn = sb("mqn", [128, 512], BF16)
        mqnb = Buf("p_mqn")
        esc = sb("esc", [128, 2, 512], BF16)
        escb = Buf("p_esc")
        rden = sb("rden", [128, 512], F32)
        rdenb = Buf("p_rden")
        onesb = sb("onesb", [128, 128], BF16)
        onesbb = Buf("p_onesb")
        memx = sb("memx", [128, 8, MEMT], F32)
        memn = sb("memn", [128, 8, MEMT], BF16)
        memxb, memnb = Buf("p_memx"), Buf("p_memn")
        mkT = sb("mkT", [128, 4, MEMT], BF16)
        mv = sb("mv", [128, 2, 512], BF16)
        mkb, mvb = Buf("p_mkT"), Buf("p_mv")
        wmv = sb("wmv", [128, 8, 512], BF16)
        wmvb = Buf("p_wmv")
        cx.op("dve", lambda e: e.tensor_copy(out=onesb[:], in_=self.ones), reads=[self.cb], writes=[onesbb])
        for s_ in stv.slots:
            cx.op("dve", lambda e: e.memset(s_[0][:, :, 64:65], 1.0), writes=[s_[1]])
        cx.op("sp", lambda e: e.dma_start(out=wv[:], in_=self.wbf["wv"][l].rearrange("p (k c) -> p k c", c=512)),
              reads=[self.wbuf["wv"][l]], writes=[wvb], dma=wvb)
        cx.op("sp", lambda e: e.dma_start(out=wvr[:], in_=self.wbf["wvres"][l].rearrange("p (k c) -> p k c", c=32)),
              reads=[self.wbuf["wvres"][l]], writes=[wvrb], dma=wvrb)
        cx.op("sp", lambda e: e.dma_start(out=wmv[:], in_=self.wbf["wmv"][l].rearrange("p (k c) -> p k c", c=512)),
              reads=[self.wbuf["wmv"][l]], writes=[wmvb], dma=wmvb)
        cx.op("sp", lambda e: e.dma_start(out=memx[:], in_=self.memT.t.rearrange("f p t -> p f t")),
              writes=[memxb], dma=memxb)
        self.rmsnorm_fm(memx, memxb, slice(0, MEMT), lambda ft: self.pv(l, "norm_mem", ft), memn, memnb, ntmp, w=MEMT)
        itp = 0
        for h in range(4):
            w, wb, wk = wr.next()
            cx.op("sp", lambda e: e.dma_start(out=w[:], in_=self.wbf["wmk"][l, h].rearrange("p (k c) -> p k c", c=128)),
                  reads=[self.wbuf["wmk"][l]], writes=[wb], dma=wk)
            for kt in range(8):
                cx.op("pe", lambda e: e.matmul(out=ps[0][:, 0:MEMT], lhsT=w[:, kt, :], rhs=memn[:, kt, :],
                                               start=(kt == 0), stop=(kt == 7)), reads=[wb, memnb], writes=[psb[0]],
                      signal=(kt == 7))
            cx.op("act", lambda e: e.activation(out=sq2[0][:, 0:MEMT], in_=ps[0][:, 0:MEMT], func=AF.Square),
                  reads=[psb[0]], writes=[sq2b[0]])
            cx.op("pe", lambda e: e.matmul(out=ps[2][:, 0:MEMT], lhsT=self.ones, rhs=sq2[0][:, 0:MEMT], start=True, stop=True),
                  reads=[sq2b[0], self.cb], writes=[psb[2]])
            cx.op("act", lambda e: e.activation(out=rs2[0][:, 0:MEMT], in_=ps[2][:, 0:MEMT], func=AF.Sqrt,
                                                scale=1.0 / 128, bias=self.eps_rms), reads=[psb[2], self.cb], writes=[rs2b[0]])
            cx.op("dve", lambda e: e.reciprocal(out=rs2[0][:, 0:MEMT], in_=rs2[0][:, 0:MEMT]), reads=[rs2b[0]], writes=[rs2b[0]])
            cx.op("dve", lambda e: e.scalar_tensor_tensor(out=mkT[:, h, :], in0=ps[0][:, 0:MEMT], scalar=self.pv(l, "mem_k_norm"),
                                                          in1=rs2[0][:, 0:MEMT], op0=ALU.mult, op1=ALU.mult),
                  reads=[psb[0], rs2b[0], self.pvb], writes=[mkb])
        for mt in range(2):
            for kt in range(8):
                cx.op("pe", lambda e: e.matmul(out=ps[1][:], lhsT=memn[:, kt, mt * 128:(mt + 1) * 128], rhs=wmv[:, kt, :],
                                               start=(kt == 0), stop=(kt == 7)), reads=[wmvb, memnb], writes=[psb[1]],
                      signal=(kt == 7))
            cx.op("act", lambda e: e.activation(out=mv[:, mt, :], in_=ps[1][:], func=AF.Copy), reads=[psb[1]], writes=[mvb])
        order = list(range(T // TB))
        if hook is not None:
            order = order[::-1]
        for bidx, blk in enumerate(order):
            if bidx == 1 and hook is not None:
                hook()
            t0 = blk * TB
            for tc in range(NTC):
                a, b = t0 + tc * 512, t0 + (tc + 1) * 512
                cs = slice(tc * 512, (tc + 1) * 512)
                cx.op("sp", lambda e: e.dma_start(out=xT[:, :, cs], in_=src.t[:, :, a:b].rearrange("f p t -> p f t")),
                      reads=src.b(a, b), writes=[xb[tc]], dma=xb[tc])
                self.rmsnorm_fm(xT, xb[tc], cs, lambda ft: self.pv(l, "norm_mix", ft), hn, hb[tc], ntmp)
                cx.op("pool", lambda e: e.dma_start(out=self.HN.t[:, :, a:b].rearrange("f p t -> p f t"), in_=hn[:, :, cs]),
                      reads=[hb[tc]], writes=self.HN.b(a, b), dma=hb[tc])
            for j in range(26):
                w, wb, wk = wr.next()
                cx.op("sp", lambda e: e.dma_start(out=w[:], in_=self.wbf["wmix"][l, j].rearrange("p (k c) -> p k c", c=128)),
                      reads=[self.wbuf["wmix"][l]], writes=[wb], dma=wk)
                for tc in range(NTC):
                    a, b = t0 + tc * 512, t0 + (tc + 1) * 512
                    cs = slice(tc * 512, (tc + 1) * 512)
                    pb = itp % 2
                    itp += 1
                    for kt in range(8):
                        cx.op("pe", lambda e: e.matmul(out=ps[pb][:], lhsT=w[:, kt, :], rhs=hn[:, kt, cs],
                                                       start=(kt == 0), stop=(kt == 7)), reads=[wb, hb[tc]], writes=[psb[pb]],
                              signal=(kt == 7))
                    if j < 14:
                        st, stbuf, stk = stf.next()
                        cx.op("act", lambda e: e.activation(out=st[:], in_=ps[pb][:], func=AF.Copy), reads=[psb[pb]], writes=[stbuf])
                        if getattr(self, "seg", False) and b == T:
                            cx.op("act", lambda e: e.activation(out=self.shcol[:, j:j + 1], in_=st[:, 511:512], func=AF.Copy),
                                  reads=[stbuf], writes=[self.shcolb])
                        cx.op("pool", lambda e: e.dma_start(out=self.PR.t[j, :, a:b], in_=st[:]), reads=[stbuf],
                              writes=self.PR.b(a, b), dma=stk)
                        continue
                    i2 = pb
                    cx.op("act", lambda e: e.activation(out=sq2[i2][:], in_=ps[pb][:], func=AF.Square), reads=[psb[pb]], writes=[sq2b[i2]])
                    red = self.ones if j >= 22 else self.blk64
                    cx.op("pe", lambda e: e.matmul(out=ps[2 + i2][:], lhsT=red, rhs=sq2[i2][:], start=True, stop=True),
                          reads=[sq2b[i2], self.cb], writes=[psb[2 + i2]])
                    if j < 18:
                        sc_, bi_, gn_ = 1.0, self.eps_q, "att_q_norm"
                    elif j < 22:
                        sc_, bi_, gn_ = 1.0 / 64, self.eps_rms, "att_k_norm"
                    else:
                        sc_, bi_, gn_ = 1.0, self.eps_mq, "mem_q_norm"
                    cx.op("act", lambda e: e.activation(out=rs2[i2][:], in_=ps[2 + i2][:], func=AF.Ln, scale=sc_, bias=bi_),
                          reads=[psb[2 + i2], self.cb], writes=[rs2b[i2]])
                    cx.op("act", lambda e: e.activation(out=rs2[i2][:], in_=rs2[i2][:], func=AF.Exp, scale=-0.5), reads=[rs2b[i2]], writes=[rs2b[i2]])
                    if j < 22:
                        st, stbuf, stk = stb.next()
                        cx.op("dve", lambda e: e.scalar_tensor_tensor(out=st[:], in0=ps[pb][:], scalar=self.pv(l, gn_), in1=rs2[i2][:],
                                                                      op0=ALU.mult, op1=ALU.mult),
                              reads=[psb[pb], rs2b[i2], self.pvb], writes=[stbuf])
                        dstT = self.QT if j < 18 else self.KT
                        jj = (j - 14) % 4
                        cx.op("pool", lambda e: e.dma_start(out=dstT.t[jj, :, a:b], in_=st[:]), reads=[stbuf],
                              writes=dstT.b(a, b), dma=stk)
                        continue
                    h = j - 22
                    cx.op("dve", lambda e: e.scalar_tensor_tensor(out=mqn[:], in0=ps[pb][:], scalar=self.pv(l, gn_), in1=rs2[i2][:],
                                                                  op0=ALU.mult, op1=ALU.mult),
                          reads=[psb[pb], rs2b[i2], self.pvb], writes=[mqnb])
                    for mt in range(2):
                        cx.op("pe", lambda e: e.matmul(out=ps[4 + mt][:], lhsT=mkT[:, h, mt * 128:(mt + 1) * 128], rhs=mqn[:],
                                                       start=True, stop=True), reads=[mkb, mqnb], writes=[psb[4 + mt]])
                        cx.op("act", lambda e: e.activation(out=esc[:, mt, :], in_=ps[4 + mt][:], func=AF.Exp),
                              reads=[psb[4 + mt]], writes=[escb])
                    for mt in range(2):
                        cx.op("pe", lambda e: e.matmul(out=ps[7][:], lhsT=onesb[:], rhs=esc[:, mt, :], start=(mt == 0), stop=(mt == 1)),
                              reads=[onesbb, escb], writes=[psb[7]], signal=(mt == 1))
                    for mt in range(2):
                        cx.op("pe", lambda e: e.matmul(out=ps[6][:], lhsT=mv[:, mt, h * 128:(h + 1) * 128], rhs=esc[:, mt, :],
                                                       start=(mt == 0), stop=(mt == 1)), reads=[mvb, escb], writes=[psb[6]],
                              signal=(mt == 1))
                    cx.op("dve", lambda e: e.reciprocal(out=rden[:], in_=ps[7][:]), reads=[psb[7]], writes=[rdenb])
                    st, stbuf, stk = stb.next()
                    cx.op("dve", lambda e: e.tensor_tensor(out=st[:], in0=ps[6][:], in1=rden[:], op=ALU.mult),
                          reads=[psb[6], rdenb], writes=[stbuf])
                    cx.op("pool", lambda e: e.dma_start(out=self.YM.t[h, :, a:b], in_=st[:]), reads=[stbuf],
                          writes=self.YM.b(a, b), dma=stk)
            if l > 0:
                for tc in range(NTC):
                    a, b = t0 + tc * 512, t0 + (tc + 1) * 512
                    cs = slice(tc * 512, (tc + 1) * 512)
                    pb = itp % 2
                    itp += 1
                    for kt in range(8):
                        cx.op("pe", lambda e: e.matmul(out=ps[pb][0:32, :], lhsT=wvr[:, kt, :], rhs=hn[:, kt, cs],
                                                       start=(kt == 0), stop=(kt == 7)), reads=[wvrb, hb[tc]], writes=[psb[pb]],
                              signal=(kt == 7))
                    st, stbuf, stk = stf.next()
                    cx.op("act", lambda e: e.activation(out=st[0:32, :], in_=ps[pb][0:32, :], func=AF.Copy), reads=[psb[pb]], writes=[stbuf])
                    cx.op("pool", lambda e: e.dma_start(out=self.PR.t[14, 0:32, a:b], in_=st[0:32, :]), reads=[stbuf],
                          writes=self.PR.b(a, b), dma=stk)
            for tt in range(TB // 128):
                tc = tt // 4
                pb = itp % 2
                itp += 1
                for kt in range(8):
                    cx.op("pe", lambda e: e.matmul(out=ps[pb][:], lhsT=hn[:, kt, tt * 128:(tt + 1) * 128], rhs=wv[:, kt, :],
                                                   start=(kt == 0), stop=(kt == 7)), reads=[wvb, hb[tc]], writes=[psb[pb]],
                          signal=(kt == 7))
                st, stbuf, stk = stv.next()
                cx.op("act", lambda e: e.activation(out=st[:, :, 0:64], in_=ps[pb][:].rearrange("p (h d) -> p h d", d=64), func=AF.Copy),
                      reads=[psb[pb]], writes=[stbuf])
                ta = t0 + tt * 128
                cx.op("pool", lambda e: e.dma_start(out=self.V1.t[ta // 128], in_=st[:].rearrange("p h d -> p (h d)")),
                      reads=[stbuf], writes=self.V1.b(ta, ta + 128), dma=stk)
        if hook is not None and len(order) == 1:
            hook()
        cx.end_phase()


Builder.decl_mix = _decl_mix
Builder.phase_proj = _phase_proj


def prep_mix(inp, L, sh):
    g = {k: np.asarray(v, np.float32) for k, v in inp.items() if k not in ("x", "mem")}
    wmix = np.zeros((L, 26, 128, 8 * 128), np.float32)
    wvres = np.zeros((L, 128, 8 * 32), np.float32)
    wv = np.zeros((L, 128, 8 * 512), np.float32)
    wmk = np.zeros((L, 4, 128, 8 * 128), np.float32)
    wmv = np.zeros((L, 128, 8 * 512), np.float32)
    for l in range(L):
        W = g["w_in"][l]
        cols = np.concatenate([np.arange(0, 1792), np.arange(1792, 1792 + 1024), np.arange(1792 + 1536, 3840)])
        wmix[l] = fm_layout(W[:, cols]).reshape(26, 128, 1024)
        wv[l] = mv_layout(W[:, 1792 + 1024:1792 + 1536]).reshape(128, 8 * 512)
        if l > 0:
            wvres[l] = mv_layout(g["vres_lora_a"][l - 1]).reshape(128, 8 * 32)
        wmk[l] = fm_layout(g["mem_w_kv"][l][:, :512]).reshape(4, 128, 1024)
        wmv[l] = mv_layout(g["mem_w_kv"][l][:, 512:]).reshape(128, 8 * 512)
    sh.update(wmix=wmix, wvres=wvres, wv=wv, wmk=wmk, wmv=wmv)


def _decl_att(self):
    self.wdecl("relb", [self.L, 128, 5 * 8 * 128], cast=False)
    self.wdecl("amask", [128, 5 * 128], cast=False)


def _phase_att(self, l):
    cx, nc, T = self.cx, self.nc, self.T
    ps, psb = self.ps, self.psb
    QB = 512
    with contextlib.ExitStack() as es:
        sb = lambda n, sh, dt: es.enter_context(nc.sbuf_tensor(uniq("a_" + n), sh, dt))
        M = sb("M", [128, 5, 8, 128], F32)
        Mb = Buf("a_M")
        am = sb("am", [128, 5, 128], F32)
        amb = Buf("a_am")
        cx.op("sp", lambda e: e.dma_start(out=M[:].rearrange("p i h q -> p (i h q)"), in_=self.win["relb"][l]), writes=[Mb], dma=Mb)
        cx.op("sp", lambda e: e.dma_start(out=am[:].rearrange("p i q -> p (i q)"), in_=self.win["amask"]), writes=[amb], dma=amb)
        cx.op("act", lambda e: e.activation(out=M[:].rearrange("p i h q -> p (i h q)"), in_=M[:].rearrange("p i h q -> p (i h q)"), func=AF.Exp),
              reads=[Mb], writes=[Mb])
        for i in range(5):
            cx.op("dve", lambda e: e.tensor_tensor(out=M[:, i], in0=M[:, i], in1=am[:, i:i + 1, :].broadcast_to([128, 8, 128]), op=ALU.mult),
                  reads=[Mb, amb], writes=[Mb])
        Mb.const = True
        qr = Ring(cx, es, "a_q", [128, 4, QB], BF16, 2)
        kr = Ring(cx, es, "a_k", [128, 4, 2 * QB], BF16, 2)
        vr = Ring(cx, es, "a_v", [128, 8, 520], BF16, 2)
        yst = Ring(cx, es, "a_yst", [128, 4, QB], BF16, 2)
        etmp = [sb("etmp%d" % i, [128, 512], F32) for i in range(2)]
        etb = [Buf("a_etmp%d" % i) for i in range(2)]
        expS = sb("expS", [128, 5, 8, 128], BF16)
        expSb = [Buf("a_expS%d" % i) for i in range(5)]
        rd = sb("rd", [128, 8], F32)
        rdb = Buf("a_rd")
        y = sb("y", [128, 8, 64], F32)
        yb = Buf("a_y")
        it = 0
        for blk in range(T // QB):
            a = blk * QB
            q, qb, qk = qr.next()
            k, kb, kk_ = kr.next()
            v, vb, vk = vr.next()
            ys, ysb, ysk = yst.next()
            cx.op("sp", lambda e: e.dma_start(out=q[:], in_=self.QT.t[:, :, a:a + QB].rearrange("f p t -> p f t")),
                  reads=self.QT.b(a, a + QB), writes=[qb], dma=qk)
            k0 = max(0, a - QB)
            off = k0 - (a - QB)
            if a == 0 and getattr(self, "seg", False):
                cx.op("sp", lambda e: e.dma_start(out=k[:, :, 0:QB], in_=self.KH.t.rearrange("f p t -> p f t")),
                      reads=self.KH.b(0, 512), writes=[kb], dma=kk_)
                cx.op("sp", lambda e: e.dma_start(out=v[:, 0:4, :], in_=self.VH.t.rearrange("n p c -> p n c")),
                      reads=self.VH.b(0, 512), writes=[vb], dma=vk)
            cx.op("sp", lambda e: e.dma_start(out=k[:, :, off:2 * QB], in_=self.KT.t[:, :, k0:a + QB].rearrange("f p t -> p f t")),
                  reads=self.KT.b(k0, a + QB), writes=[kb], dma=kk_)
            cx.op("sp", lambda e: e.dma_start(out=v[:, off // 128:8, :], in_=self.V1.t[k0 // 128:(a + QB) // 128].rearrange("n p c -> p n c")),
                  reads=self.V1.b(k0, a + QB), writes=[vb], dma=vk)
            for qp in range(QB // 128):
                if getattr(self, "att_dbg", 9) < 1:
                    break
                a1 = a + qp * 128
                valid = [i for i in range(5) if a1 - 512 + i * 128 >= 0 or getattr(self, "seg", False)]
                qs = slice(qp * 128, (qp + 1) * 128)
                for i in valid:
                    kc = slice(qp * 128 + i * 128, qp * 128 + (i + 1) * 128)
                    for par in range(2):
                        bk = it % 4
                        et = it % 2
                        it += 1
                        hp = par * 64
                        for hh in range(4):
                            jt = hh
                            cx.op("pe", lambda e: e.matmul(out=ps[bk][:, hh * 128:(hh + 1) * 128], lhsT=k[hp:hp + 64, jt, kc],
                                                           rhs=q[hp:hp + 64, jt, qs], start=True, stop=True),
                                  reads=[kb, qb], writes=[psb[bk]], signal=(hh == 3))
                        cx.op("act", lambda e: e.activation(out=etmp[et][:], in_=ps[bk][:], func=AF.Exp), reads=[psb[bk]], writes=[etb[et]])
                        cx.op("dve", lambda e: e.tensor_tensor(out=expS[:, i, par:8:2, :], in0=etmp[et][:].rearrange("p (h q) -> p h q", q=128),
                                                               in1=M[:, i, par:8:2, :], op=ALU.mult),
                              reads=[etb[et], Mb], writes=[expSb[i]])
                if getattr(self, "att_dbg", 9) < 2:
                    continue
                for g in range(2):
                    for hh in range(4):
                        h = 4 * g + hh
                        for i in valid:
                            cx.op("pe", lambda e: e.matmul(out=ps[4 + g][:, hh * 65:(hh + 1) * 65], lhsT=expS[:, i, h, :],
                                                           rhs=v[:, qp + i, h * 65:(h + 1) * 65], start=(i == valid[0]), stop=(i == valid[-1])),
                                  reads=[expSb[i], vb], writes=[psb[4 + g]], signal=(hh == 3 and i == valid[-1]))
                    if getattr(self, "att_dbg", 9) < 3:
                        continue
                    pv_ = ps[4 + g][:, 0:260].rearrange("p (h e) -> p h e", e=65)
                    cx.op("dve", lambda e: e.reciprocal(out=rd[:, 4 * g:4 * g + 4].rearrange("p (h o) -> p h o", o=1), in_=pv_[:, :, 64:65]),
                          reads=[psb[4 + g]], writes=[rdb])
                    cx.op("dve", lambda e: e.tensor_tensor(out=y[:, 4 * g:4 * g + 4, :], in0=pv_[:, :, 0:64],
                                                           in1=rd[:, 4 * g:4 * g + 4].rearrange("p (h o) -> p h o", o=1).broadcast_to([128, 4, 64]),
                                                           op=ALU.mult), reads=[psb[4 + g], rdb], writes=[yb])
                if getattr(self, "att_dbg", 9) < 4:
                    continue
                for jt in range(4):
                    cx.op("pe", lambda e: e.transpose(out=ps[6][:, jt * 128:(jt + 1) * 128],
                                                      in_=y[:, 2 * jt:2 * jt + 2, :].rearrange("p h d -> p (h d)"), identity=self.ident),
                          reads=[yb, self.cb], writes=[psb[6]], signal=(jt == 3))
                cx.op("act", lambda e: e.activation(out=ys[:, :, qs], in_=ps[6][:].rearrange("p (j q) -> p j q", q=128), func=AF.Copy),
                      reads=[psb[6]], writes=[ysb])
            cx.op("pool", lambda e: e.dma_start(out=self.YA.t[:, :, a:a + QB].rearrange("f p t -> p f t"), in_=ys[:]),
                  reads=[ysb], writes=self.YA.b(a, a + QB), dma=ysk)
        cx.end_phase()


Builder.decl_att = _decl_att
Builder.phase_att = _phase_att


def prep_att(inp, L, sh):
    rel = np.asarray(inp["att_rel_bias"], np.float32)
    p = np.arange(128)[:, None, None]
    i = np.arange(5)[None, :, None]
    q = np.arange(128)[None, None, :]
    kpos = i * 128 + p
    dist = q - kpos + 512
    idx = np.clip(dist, -63, 128) + 63
    cq = q // 64
    ck = kpos // 64
    mask = ((ck >= cq) & (ck <= cq + 8)).astype(np.float32)
    relb = np.zeros((L, 128, 5, 8, 128), np.float32)
    for l in range(L):
        for h in range(8):
            relb[l, :, :, h, :] = rel[l, h][idx]
    sh["relb"] = relb.reshape(L, 128, 5 * 8 * 128)
    sh["amask"] = np.ascontiguousarray(np.broadcast_to(mask, (128, 5, 128))).reshape(128, 640)


def _decl_merge(self):
    L = self.L
    self.wdecl("wgate", [L, 24, 128, 8 * 128])
    self.wdecl("wbr", [L, 3, 8, 128, 4 * 128])
    self.wdecl("wout", [L, 8, 128, 8 * 128])


def _phase_merge(self, l, src, dst):
    cx, nc, T = self.cx, self.nc, self.T
    TB = min(1024, T)
    NTC = TB // 512
    ps, psb = self.ps, self.psb
    with contextlib.ExitStack() as es:
        sb = lambda n, sh, dt: es.enter_context(nc.sbuf_tensor(uniq("m_" + n), sh, dt))
        xT = sb("xT", [128, 8, TB], F32)
        hn = sb("hn", [128, 8, TB], BF16)
        yb3 = [sb("y%d" % i, [128, 4, TB], BF16) for i in range(3)]
        mg = sb("mg", [128, 8, TB], BF16)
        macc = [sb("macc%d" % i, [128, 512], F32) for i in range(NTC)]
        gt = [sb("gt%d" % i, [128, 512], F32) for i in range(2)]
        xb = [Buf("m_xT%d" % i) for i in range(NTC)]
        hb = [Buf("m_hn%d" % i) for i in range(NTC)]
        ybb = [[Buf("m_y%d_%d" % (i, t)) for t in range(NTC)] for i in range(3)]
        mgb = [Buf("m_mg%d" % i) for i in range(NTC)]
        maccb = [Buf("m_macc%d" % i) for i in range(NTC)]
        gtb = [Buf("m_gt%d" % i) for i in range(2)]
        wgr = Ring(cx, es, "m_wg", [128, 8, 128], BF16, 5)
        wbrr = Ring(cx, es, "m_wb", [128, 4, 128], BF16, 5)
        ysrc = [self.YR, self.YA, self.YM]
        it = 0
        for blk in range(T // TB):
            t0 = blk * TB
            for tc in range(NTC):
                a, b = t0 + tc * 512, t0 + (tc + 1) * 512
                cs = slice(tc * 512, (tc + 1) * 512)
                cx.op("sp", lambda e: e.dma_start(out=xT[:, :, cs], in_=src.t[:, :, a:b].rearrange("f p t -> p f t")),
                      reads=src.b(a, b), writes=[xb[tc]], dma=xb[tc])
                cx.op("sp", lambda e: e.dma_start(out=hn[:, :, cs], in_=self.HN.t[:, :, a:b].rearrange("f p t -> p f t")),
                      reads=self.HN.b(a, b), writes=[hb[tc]], dma=hb[tc])
                for i in range(3):
                    cx.op("sp", lambda e: e.dma_start(out=yb3[i][:, :, cs], in_=ysrc[i].t[:, :, a:b].rearrange("f p t -> p f t")),
                          reads=ysrc[i].b(a, b), writes=[ybb[i][tc]], dma=ybb[i][tc])
            for o in range(8):
                for br in range(3):
                    wg, wgb, _ = wgr.next()
                    wb_, wbb, _ = wbrr.next()
                    cx.op("sp", lambda e: e.dma_start(out=wg[:], in_=self.wbf["wgate"][l, br * 8 + o].rearrange("p (k c) -> p k c", c=128)),
                          reads=[self.wbuf["wgate"][l]], writes=[wgb], dma=wgb)
                    cx.op("sp", lambda e: e.dma_start(out=wb_[:], in_=self.wbf["wbr"][l, br, o].rearrange("p (k c) -> p k c", c=128)),
                          reads=[self.wbuf["wbr"][l]], writes=[wbb], dma=wbb)
                    for tc in range(NTC):
                        cs = slice(tc * 512, (tc + 1) * 512)
                        pg, pb = it % 2, 2 + it % 2
                        gi = it % 2
                        it += 1
                        for kt in range(8):
                            cx.op("pe", lambda e: e.matmul(out=ps[pg][:], lhsT=wg[:, kt, :], rhs=hn[:, kt, cs], start=(kt == 0), stop=(kt == 7)),
                                  reads=[wgb, hb[tc]], writes=[psb[pg]], signal=(kt == 7))
                        for kt in range(4):
                            cx.op("pe", lambda e: e.matmul(out=ps[pb][:], lhsT=wb_[:, kt, :], rhs=yb3[br][:, kt, cs], start=(kt == 0), stop=(kt == 3)),
                                  reads=[wbb, ybb[br][tc]], writes=[psb[pb]], signal=(kt == 3))
                        cx.op("act", lambda e: e.activation(out=gt[gi][:], in_=ps[pg][:], func=AF.Sigmoid, bias=self.pv(l, "b_gate", br * 8 + o)),
                              reads=[psb[pg], self.pvb], writes=[gtb[gi]])
                        if br == 0:
                            cx.op("dve", lambda e: e.tensor_tensor(out=macc[tc][:], in0=gt[gi][:], in1=ps[pb][:], op=ALU.mult),
                                  reads=[gtb[gi], psb[pb]], writes=[maccb[tc]])
                        else:
                            cx.op("dve", lambda e: e.tensor_tensor(out=gt[gi][:], in0=gt[gi][:], in1=ps[pb][:], op=ALU.mult),
                                  reads=[gtb[gi], psb[pb]], writes=[gtb[gi]])
                            if br == 1:
                                cx.op("pool", lambda e: e.tensor_tensor(out=macc[tc][:], in0=macc[tc][:], in1=gt[gi][:], op=ALU.add),
                                      reads=[gtb[gi], maccb[tc]], writes=[maccb[tc]])
                            else:
                                cx.op("pool", lambda e: e.tensor_tensor(out=mg[:, o, cs], in0=macc[tc][:], in1=gt[gi][:], op=ALU.add),
                                      reads=[gtb[gi], maccb[tc]], writes=[mgb[tc]])
            for o in range(8):
                wg, wgb, _ = wgr.next()
                cx.op("sp", lambda e: e.dma_start(out=wg[:], in_=self.wbf["wout"][l, o].rearrange("p (k c) -> p k c", c=128)),
                      reads=[self.wbuf["wout"][l]], writes=[wgb], dma=wgb)
                for tc in range(NTC):
                    cs = slice(tc * 512, (tc + 1) * 512)
                    po = 4 + it % 2
                    it += 1
                    for kt in range(8):
                        cx.op("pe", lambda e: e.matmul(out=ps[po][:], lhsT=wg[:, kt, :], rhs=mg[:, kt, cs], start=(kt == 0), stop=(kt == 7)),
                              reads=[wgb, mgb[tc]], writes=[psb[po]], signal=(kt == 7))
                    cx.op("dve", lambda e: e.tensor_tensor(out=xT[:, o, cs], in0=ps[po][:], in1=xT[:, o, cs], op=ALU.add),
                          reads=[psb[po], xb[tc]], writes=[xb[tc]])
            for tc in range(NTC):
                a, b = t0 + tc * 512, t0 + (tc + 1) * 512
                cs = slice(tc * 512, (tc + 1) * 512)
                cx.op("pool", lambda e: e.dma_start(out=dst.t[:, :, a:b].rearrange("f p t -> p f t"), in_=xT[:, :, cs]),
                      reads=[xb[tc]], writes=dst.b(a, b), dma=xb[tc])
        cx.end_phase()


Builder.decl_merge = _decl_merge
Builder.phase_merge = _phase_merge


def prep_merge(inp, L, sh):
    g = {k: np.asarray(inp[k], np.float32) for k in ["w_gate", "w_branch_rwkv", "w_branch_att", "w_branch_mem", "w_out"]}
    wgate = np.zeros((L, 24, 128, 1024), np.float32)
    wbr = np.zeros((L, 3, 8, 128, 512), np.float32)
    wout = np.zeros((L, 8, 128, 1024), np.float32)
    for l in range(L):
        wgate[l] = fm_layout(g["w_gate"][l]).reshape(24, 128, 1024)
        for i, n in enumerate(["w_branch_rwkv", "w_branch_att", "w_branch_mem"]):
            wbr[l, i] = fm_layout(g[n][l]).reshape(8, 128, 512)
        wout[l] = fm_layout(g["w_out"][l]).reshape(8, 128, 1024)
    sh.update(wgate=wgate, wbr=wbr, wout=wout)


WC_ = 0.6065306597126334


def _decl_rwkv(self):
    L = self.L
    self.wdecl("lora", [L, 128, 3 * 512], cast=False)
    self.wdecl("rmask", [128, 5 * 128], cast=False)


def _phase_rwkv(self, l, mode="full"):
    emit_y = mode in ("full", "A")
    gn = mode == "full"
    seg = mode == "A"
    cx, nc, T = self.cx, self.nc, self.T
    ps, psb = self.ps, self.psb
    RB = 256
    NP = RB // 128
    NCH = RB // 64
    with contextlib.ExitStack() as es:
        def sb(n, sh, dt=F32):
            return es.enter_context(nc.sbuf_tensor(uniq("r_" + n), sh, dt)), Buf("r_" + n)
        PRb, PRbb = sb("PRb", [128, 15, RB + 1])
        Dt, Db = (None, None) if mode == "A" else sb("D", [128, 14, RB])
        S, Sb = sb("S", [128, 4, RB])
        A, Ab = sb("A", [128, 4, RB])
        G, Gb = sb("G", [128, 4, RB])
        KK, KKb = sb("KK", [128, 4, RB])
        CS, CSb = sb("CS", [128, 4, RB])
        E1, E1b = sb("E1", [128, 4, RB])
        E3, E3b = sb("E3", [128, 4, RB])
        Bh, Bhb = sb("Bh", [128, 4, RB])
        Bc, Bcb = sb("Bc", [128, 4, RB])
        Kc, Kcb = sb("Kc", [128, 4, RB])
        bonus, bonb = sb("bonus", [128, 4, RB])
        tmp4, tmp4b = sb("tmp4", [128, 4, RB])
        VFb, VFbb = sb("VFb", [128, 4, RB])
        tw, twb = sb("tw", [128, RB])
        sgx, sgxb = sb("sgx", [128, RB])
        lora, lorab = sb("lora", [128, 3, 512])
        rmask, rmb = sb("rmask", [128, 5, 128])
        TM = [sb("TM%d" % i, [128, NP, 512]) for i in range(4)]
        Amat, Amb = sb("Amat", [128, 8, 5, 128])
        if mode == "A":
            Dt = Amat[:].rearrange("p a b c -> p (a b c)")[:, 0:14 * RB].rearrange("p (a t) -> p a t", a=14)
            Db = Amb
        Pw = [sb("Pw%d_%d" % (g, i), [128, 4, 2, 128]) for g in range(2) for i in range(2)]
        Acc = [sb("Acc%d_%d" % (g, i), [128, 4, 128]) for g in range(2) for i in range(2)]
        P1s, P1b = sb("P1s", [128, 8, 64])
        P2T, P2b = sb("P2T", [128, 8, 128])
        Vt, Vtb = sb("Vt", [128, 8, 64])
        H = [sb("H%d" % i, [128, 4, 64]) for i in range(2)]
        Ht, Htb = sb("Ht", [128, 4, 64])
        if mode == "A":
            TnTq = [sb("TnTq%d" % i, [128, 2, 4, 128]) for i in range(2)]
            GnDq = [sb("GnDq%d" % i, [128, 2, 4, 64]) for i in range(2)]
            Y1Tq = [[sb("Y1Tq%d_%d" % (q_, i), [128, 4, 128]) for i in range(2)] for q_ in range(2)]
            Ycq = [sb("Ycq%d" % i, [64, 2, 512]) for i in range(2)]
            WCq = [sb("WCq%d" % i, [128, 4, 2]) for i in range(2)]
            TnT, TnTb = TnTq[0]
            GnD, GnDb = GnDq[0]
            Y1T = Y1Tq[0]
            yst = None
        else:
            TnT, TnTb = sb("TnT", [128, 2, 4, 128])
            GnD, GnDb = sb("GnD", [128, 2, 4, 64])
            Y1T = [sb("Y1T%d" % i, [128, 4, 128]) for i in range(2)]
            ysq, ysqb = sb("ysq", [64, 512])
            yn, ynb = sb("yn", [64, 512])
            st1, st1b = sb("st1", [64, 8])
            st2, st2b = sb("st2", [64, 8])
            st3, st3b = sb("st3", [64, 8])
            yo, yob = sb("yo", [128, 4, 128])
            yst = Ring(cx, es, "r_yst", [128, 4, RB], BF16, 2)
        prevc, prevcb = sb("prevc", [128, 14, 1])
        P_ = PRb[:, :, 1:RB + 1]
        r_, k_, v_ = P_[:, 0:4, :], P_[:, 4:8, :], P_[:, 8:12, :]

        cx.op("sp", lambda e: e.dma_start(out=lora[:].rearrange("p a c -> p (a c)"), in_=self.win["lora"][l]), writes=[lorab], dma=lorab)
        cx.op("sp", lambda e: e.dma_start(out=rmask[:].rearrange("p a c -> p (a c)"), in_=self.win["rmask"]), writes=[rmb], dma=rmb)
        lorab.const = True
        rmb.const = True
        if mode == "A":
            Ap = [sb("Ap%d" % i, [128, 4, 128]) for i in range(2)]
            ApT, ApTb = sb("ApT", [128, 4, 128])
            pay, payb = sb("pay", [128, 768])
            zst = Ring(cx, es, "r_zst", [128, 512], F32, 2)
            y0st = Ring(cx, es, "r_y0st", [64, 512], F32, 2)
        apcur = 0
        cx.op("dve", lambda e: e.memset(H[0][0][:], 0.0), writes=[H[0][1]])
        if mode == "A":
            cx.op("dve", lambda e: e.tensor_copy(out=Ap[0][0][:], in_=self.ident.rearrange("p (o c) -> p o c", o=1).broadcast_to([128, 4, 128])),
                  reads=[self.cb], writes=[Ap[0][1]])
        for (yt_, ytb_) in ([x_ for q_ in Y1Tq for x_ in q_] if mode == "A" else Y1T):
            cx.op("dve", lambda e: e.memset(yt_[:], 0.0), writes=[ytb_])
        deferred = []
        stt = {"h": 0, "a": 0, "pair": 0}

        def pump(n):
            for _ in range(n):
                if deferred:
                    deferred.pop(0)()
        pv = lambda name, j: self.pv(l, name, j)
        hcur = 0
        bi = [0]

        def bank():
            bi[0] += 1
            return bi[0] % 8

        for blk in range(T // RB):
            t0 = blk * RB
            ntile = 15 if l > 0 else 14
            cx.op("sp", lambda e: e.dma_start(out=PRb[:, 0:14, 1:RB + 1], in_=self.PR.t[0:14, :, t0:t0 + RB].rearrange("f p t -> p f t")),
                  reads=self.PR.b(t0, t0 + RB), writes=[PRbb], dma=PRbb)
            if l > 0:
                cx.op("sp", lambda e: e.dma_start(out=PRb[0:32, 14, 1:RB + 1], in_=self.PR.t[14, 0:32, t0:t0 + RB]),
                      reads=self.PR.b(t0, t0 + RB), writes=[PRbb], dma=PRbb)
            if t0 == 0 and seg:
                cx.op("sp", lambda e: e.dma_start(out=prevc[:].rearrange("p a o -> p (a o)"), in_=self.SH), reads=[self.SHb], writes=[prevcb], dma=prevcb)
                cx.op("pool", lambda e: e.tensor_copy(out=PRb[:, 0:14, 0:1], in_=prevc[:]), reads=[prevcb], writes=[PRbb])
            elif t0 == 0:
                cx.op("pool", lambda e: e.memset(PRb[:, 0:14, 0:1], 0.0), writes=[PRbb])
            else:
                cx.op("pool", lambda e: e.tensor_copy(out=PRb[:, 0:14, 0:1], in_=prevc[:]), reads=[prevcb], writes=[PRbb])
            cx.op("pool", lambda e: e.tensor_copy(out=prevc[:], in_=PRb[:, 0:14, RB:RB + 1]), reads=[PRbb], writes=[prevcb])
            if l > 0:
                cx.op("sp", lambda e: e.dma_start(out=VFb[:], in_=self.VF.t[:, :, t0:t0 + RB].rearrange("f p t -> p f t")),
                      reads=self.VF.b(t0, t0 + RB), writes=[VFbb], dma=VFbb)
            cx.op("pool", lambda e: e.tensor_tensor(out=Dt[:], in0=PRb[:, 0:14, 0:RB], in1=PRb[:, 0:14, 1:RB + 1], op=ALU.subtract),
                  reads=[PRbb], writes=[Db])
            for j in range(14):
                cx.op("dve", lambda e: e.scalar_tensor_tensor(out=P_[:, j, :], in0=Dt[:, j, :], scalar=pv("shift_mu", j), in1=P_[:, j, :],
                                                              op0=ALU.mult, op1=ALU.add), reads=[Db, PRbb, self.pvb], writes=[PRbb])
            cx.op("act", lambda e: e.activation(out=tw[0:64, :], in_=P_[0:64, 12, :], func=AF.Tanh), reads=[PRbb], writes=[twb])
            cx.op("act", lambda e: e.activation(out=sgx[:], in_=P_[:, 13, :], func=AF.Sigmoid), reads=[PRbb], writes=[sgxb])
            for jt in range(4):
                js = slice(jt * 128, (jt + 1) * 128)
                cx.op("pe", lambda e: e.matmul(out=ps[0][:, 0:RB], lhsT=lora[0:64, 0, js], rhs=tw[0:64, :], start=True, stop=True),
                      reads=[lorab, twb], writes=[psb[0]])
                cx.op("act", lambda e: e.activation(out=S[:, jt, :], in_=ps[0][:, 0:RB], func=AF.Sigmoid, bias=pv("decay_w0", jt)),
                      reads=[psb[0], self.pvb], writes=[Sb])
                cx.op("pe", lambda e: e.matmul(out=ps[1][:, 0:RB], lhsT=lora[64:128, 0, js], rhs=P_[64:128, 12, :], start=True, stop=True),
                      reads=[lorab, PRbb], writes=[psb[1]])
                cx.op("act", lambda e: e.activation(out=A[:, jt, :], in_=ps[1][:, 0:RB], func=AF.Sigmoid, bias=pv("iclr_a0", jt)),
                      reads=[psb[1], self.pvb], writes=[Ab])
                if emit_y:
                    cx.op("pe", lambda e: e.matmul(out=ps[2][:, 0:RB], lhsT=lora[:, 1, js], rhs=sgx[:], start=True, stop=True),
                          reads=[lorab, sgxb], writes=[psb[2]])
                    cx.op("act", lambda e: e.activation(out=G[:, jt, :], in_=ps[2][:, 0:RB], func=AF.Copy), reads=[psb[2]], writes=[Gb])
                if l > 0:
                    cx.op("pe", lambda e: e.matmul(out=ps[3][:, 0:RB], lhsT=lora[0:32, 2, js], rhs=P_[0:32, 14, :], start=True, stop=True),
                          reads=[lorab, PRbb], writes=[psb[3]])
                    cx.op("act", lambda e: e.activation(out=tmp4[:, jt, :], in_=ps[3][:, 0:RB], func=AF.Sigmoid, bias=pv("vres_v0", jt)),
                          reads=[psb[3], self.pvb], writes=[tmp4b])
            if l > 0:
                cx.op("dve", lambda e: e.tensor_tensor(out=VFb[:], in0=VFb[:], in1=v_, op=ALU.subtract), reads=[VFbb, PRbb], writes=[VFbb])
                cx.op("dve", lambda e: e.tensor_tensor(out=VFb[:], in0=VFb[:], in1=tmp4[:], op=ALU.mult), reads=[VFbb, tmp4b], writes=[VFbb])
                cx.op("dve", lambda e: e.tensor_tensor(out=v_, in0=v_, in1=VFb[:], op=ALU.add), reads=[VFbb, PRbb], writes=[PRbb])
            else:
                cx.op("pool", lambda e: e.dma_start(out=self.VF.t[:, :, t0:t0 + RB].rearrange("f p t -> p f t"), in_=v_),
                      reads=[PRbb], writes=self.VF.b(t0, t0 + RB), dma=PRbb)
            for jt in range(4):
                cx.op("dve", lambda e: e.tensor_scalar(out=KK[:, jt, :], in0=k_[:, jt, :], scalar1=pv("k_k", jt), scalar2=None, op0=ALU.mult),
                      reads=[PRbb, self.pvb], writes=[KKb])
            cx.op("act", lambda e: e.activation(out=tmp4[:], in_=KK[:], func=AF.Square), reads=[KKb], writes=[tmp4b])
            for hf in range(2):
                cx.op("pe", lambda e: e.matmul(out=ps[4 + hf][:], lhsT=self.blk64, rhs=tmp4[:, 2 * hf:2 * hf + 2, :].rearrange("p a t -> p (a t)"),
                                               start=True, stop=True), reads=[tmp4b, self.cb], writes=[psb[4 + hf]])
            for hf in range(2):
                cx.op("act", lambda e: e.activation(out=tmp4[:, 2 * hf:2 * hf + 2, :].rearrange("p a t -> p (a t)"), in_=ps[4 + hf][:], func=AF.Sqrt),
                      reads=[psb[4 + hf]], writes=[tmp4b])
            cx.op("dve", lambda e: e.tensor_scalar(out=tmp4[:], in0=tmp4[:], scalar1=1e-12, scalar2=None, op0=ALU.max), reads=[tmp4b], writes=[tmp4b])
            cx.op("dve", lambda e: e.reciprocal(out=tmp4[:], in_=tmp4[:]), reads=[tmp4b], writes=[tmp4b])
            cx.op("dve", lambda e: e.tensor_tensor(out=KK[:], in0=KK[:], in1=tmp4[:], op=ALU.mult), reads=[KKb, tmp4b], writes=[KKb])
            for jt in range(4):
                cx.op("dve", lambda e: e.tensor_scalar(out=tmp4[:, jt, :], in0=A[:, jt, :], scalar1=-1.0, scalar2=pv("k_a", jt), op0=ALU.add, op1=ALU.mult),
                      reads=[Ab, self.pvb], writes=[tmp4b])
            cx.op("dve", lambda e: e.scalar_tensor_tensor(out=k_, in0=tmp4[:], scalar=1.0, in1=k_, op0=ALU.add, op1=ALU.mult),
                  reads=[tmp4b, PRbb], writes=[PRbb])
            for jt in (range(4) if emit_y else []):
                cx.op("dve", lambda e: e.scalar_tensor_tensor(out=tmp4[:, jt, :], in0=r_[:, jt, :], scalar=pv("r_k", jt), in1=k_[:, jt, :],
                                                              op0=ALU.mult, op1=ALU.mult), reads=[PRbb, self.pvb], writes=[tmp4b])
            for hf in (range(2) if emit_y else []):
                cx.op("pe", lambda e: e.matmul(out=ps[6 + hf][:], lhsT=self.blk64, rhs=tmp4[:, 2 * hf:2 * hf + 2, :].rearrange("p a t -> p (a t)"),
                                               start=True, stop=True), reads=[tmp4b, self.cb], writes=[psb[6 + hf]])
            for hf in (range(2) if emit_y else []):
                cx.op("dve", lambda e: e.tensor_tensor(out=bonus[:, 2 * hf:2 * hf + 2, :].rearrange("p a t -> p (a t)"), in0=ps[6 + hf][:],
                                                       in1=v_[:, 2 * hf:2 * hf + 2, :], op=ALU.mult) if False else
                      e.tensor_tensor(out=bonus[:, 2 * hf:2 * hf + 2, :], in0=ps[6 + hf][:].rearrange("p (a t) -> p a t", t=RB),
                                      in1=v_[:, 2 * hf:2 * hf + 2, :], op=ALU.mult), reads=[psb[6 + hf], PRbb], writes=[bonb])
            for jt in range(4):
                cx.op("dve", lambda e: e.tensor_tensor_scan(out=CS[:, jt, :], data0=self.cmask[:, 0:RB], data1=S[:, jt, :], initial=0.0,
                                                            op0=ALU.mult, op1=ALU.add), reads=[Sb, self.cb], writes=[CSb])
            cx.op("pool", lambda e: e.tensor_tensor(out=E3[:], in0=CS[:], in1=S[:], op=ALU.subtract), reads=[CSb, Sb], writes=[E3b])
            cx.op("act", lambda e: e.activation(out=E3[:], in_=E3[:], func=AF.Exp, scale=-WC_), reads=[E3b], writes=[E3b])
            cx.op("act", lambda e: e.activation(out=E1[:], in_=CS[:], func=AF.Exp, scale=-WC_), reads=[CSb], writes=[E1b])
            cx.op("act", lambda e: e.activation(out=CS[:], in_=CS[:], func=AF.Exp, scale=WC_), reads=[CSb], writes=[CSb])
            cx.op("dve", lambda e: e.tensor_tensor(out=Bh[:], in0=KK[:], in1=A[:], op=ALU.mult), reads=[KKb, Ab], writes=[Bhb])
            cx.op("dve", lambda e: e.tensor_tensor(out=Bh[:], in0=Bh[:], in1=CS[:], op=ALU.mult), reads=[Bhb, CSb], writes=[Bhb])
            cx.op("pool", lambda e: e.tensor_tensor(out=k_, in0=k_, in1=CS[:], op=ALU.mult), reads=[PRbb, CSb], writes=[PRbb])
            cx.op("pool", lambda e: e.tensor_tensor(out=KK[:], in0=KK[:], in1=E3[:], op=ALU.mult), reads=[KKb, E3b], writes=[KKb])
            cx.op("dve", lambda e: e.tensor_tensor(out=r_, in0=r_, in1=E1[:], op=ALU.mult), reads=[PRbb, E1b], writes=[PRbb])
            wcb = E1[:].rearrange("p a (c t) -> p a c t", t=64)[:, :, :, 63:64].broadcast_to([128, 4, NCH, 64])
            cx.op("dve", lambda e: e.tensor_tensor(out=Bc[:].rearrange("p a (c t) -> p a c t", t=64), in0=Bh[:].rearrange("p a (c t) -> p a c t", t=64),
                                                   in1=wcb, op=ALU.mult), reads=[Bhb, E1b], writes=[Bcb])
            cx.op("pool", lambda e: e.tensor_tensor(out=Kc[:].rearrange("p a (c t) -> p a c t", t=64), in0=k_.rearrange("p a (c t) -> p a c t", t=64),
                                                    in1=wcb, op=ALU.mult), reads=[PRbb, E1b], writes=[Kcb])
            srcs = [(v_, PRbb, 1.0), (KK[:], KKb, 1.0), (Bc[:], Bcb, -1.0), (Kc[:], Kcb, 1.0)]
            for qi, (sap, sbuf_, sc_) in enumerate(srcs):
                for pr in range(NP):
                    bk = bank()
                    for jt in range(4):
                        cx.op("pe", lambda e: e.transpose(out=ps[bk][:, jt * 128:(jt + 1) * 128], in_=sap[:, jt, pr * 128:(pr + 1) * 128], identity=self.ident),
                              reads=[sbuf_, self.cb], writes=[psb[bk]], signal=(jt == 3))
                    cx.op("act", lambda e: e.activation(out=TM[qi][0][:, pr, :], in_=ps[bk][:], func=AF.Copy, scale=sc_), reads=[psb[bk]], writes=[TM[qi][1]])
            Vtm, KKtm, Bntm, Kctm = [t[0] for t in TM]
            Vtmb, KKtmb, Bntmb, Kctmb = [t[1] for t in TM]
            if yst is not None:
                ys, ysb, _ = yst.next()
            for pr in range(NP):
                pc = slice(pr * 128, (pr + 1) * 128)
                if mode == "A":
                    q = stt["pair"] % 2
                    stt["pair"] += 1
                    TnT, TnTb = TnTq[q]
                    GnD, GnDb = GnDq[q]
                    Y1T = Y1Tq[q]
                for h in range(8):
                    jt, hp = h // 2, (h % 2) * 64
                    hs = slice(hp, hp + 64)
                    bx = h % 2
                    by = 2 + h % 2
                    kkt, bh, kh, rt = KK[hs, jt, pc], Bh[hs, jt, pc], k_[hs, jt, pc], r_[hs, jt, pc]
                    for qi, (lt, rh, rb_) in enumerate([(kkt, bh, Bhb), (kkt, kh, PRbb), (bh, kkt, KKb), (bh, rt, PRbb)]):
                        cx.op("pe", lambda e: e.matmul(out=ps[bx][:, qi * 128:(qi + 1) * 128], lhsT=lt, rhs=rh, start=True, stop=True),
                              reads=[KKb, Bhb, PRbb], writes=[psb[bx]], signal=(qi == 3))
                    cx.op("pe", lambda e: e.matmul(out=ps[by][:, (h // 2) * 128:(h // 2 + 1) * 128], lhsT=kh, rhs=rt, start=True, stop=True),
                          reads=[PRbb], writes=[psb[by]], signal=(h >= 6))
                    cx.op("dve", lambda e: e.tensor_tensor(out=Amat[:, h, 0:4, :], in0=ps[bx][:].rearrange("p (a c) -> p a c", c=128), in1=rmask[:, 0:4, :],
                                                           op=ALU.mult), reads=[psb[bx], rmb], writes=[Amb])
                for par in range(2):
                    cx.op("dve", lambda e: e.tensor_tensor(out=Amat[:, par:8:2, 4, :], in0=ps[2 + par][:].rearrange("p (a c) -> p a c", c=128),
                                                           in1=rmask[:, 4:5, :].broadcast_to([128, 4, 128]), op=ALU.mult),
                          reads=[psb[2 + par], rmb], writes=[Amb])
                pump(2)
                cur = [None, None]
                for g in range(2):
                    a0_, a0b = Acc[2 * g]
                    cx.op("dve", lambda e: e.tensor_tensor(out=a0_[:], in0=Amat[:, 4 * g:4 * g + 4, 2, :],
                                                           in1=self.ident.rearrange("p (o c) -> p o c", o=1).broadcast_to([128, 4, 128]), op=ALU.add),
                          reads=[Amb, self.cb], writes=[a0b])
                    cur[g] = 0
                for lev in range(1, 6):
                    for g in range(2):
                        pwn, pwnb = Pw[2 * g + lev % 2]
                        pwo, pwob = Pw[2 * g + (lev - 1) % 2]
                        for hh in range(4):
                            h = 4 * g + hh
                            if lev == 1:
                                Mo, No, rdb_ = Amat[:, h, 2, :], Amat[:, h, 0, :], Amb
                            else:
                                Mo, No, rdb_ = pwo[:, hh, 0, :], pwo[:, hh, 1, :], pwob
                            bp = 4 + 2 * g + hh // 2
                            c0 = (hh % 2) * 256
                            if lev < 5:
                                cx.op("pe", lambda e: e.matmul(out=ps[bp][:, c0:c0 + 128], lhsT=No, rhs=Mo, start=True, stop=True),
                                      reads=[rdb_], writes=[psb[bp]], signal=False)
                            cx.op("pe", lambda e: e.matmul(out=ps[bp][:, c0 + 128:c0 + 256], lhsT=Mo, rhs=No, start=True, stop=True),
                                  reads=[rdb_], writes=[psb[bp]], signal=(hh % 2 == 1))
                        for half in range(2):
                            bp = 4 + 2 * g + half
                            cx.op("act", lambda e: e.activation(out=pwn[:, 2 * half:2 * half + 2, :, :], in_=ps[bp][:].rearrange("p (a b c) -> p a b c", b=2, c=128),
                                                                func=AF.Copy), reads=[psb[bp]], writes=[pwnb])
                        ao, aob = Acc[2 * g + cur[g]]
                        an, anb = Acc[2 * g + 1 - cur[g]]
                        bc_ = 2 + g
                        for hh in range(4):
                            cx.op("pe", lambda e: e.matmul(out=ps[bc_][:, hh * 128:(hh + 1) * 128], lhsT=pwn[:, hh, 1, :], rhs=ao[:, hh, :], start=True, stop=True),
                                  reads=[pwnb, aob], writes=[psb[bc_]], signal=(hh == 3))
                        cx.op("dve", lambda e: e.tensor_tensor(out=an[:], in0=ao[:], in1=ps[bc_][:].rearrange("p (a c) -> p a c", c=128), op=ALU.add),
                              reads=[aob, psb[bc_]], writes=[anb])
                        cur[g] = 1 - cur[g]
                    pump(1)
                XT = lambda h: Acc[2 * (h // 4) + cur[h // 4]][0][:, h % 4, :]
                XTb = lambda h: Acc[2 * (h // 4) + cur[h // 4]][1]
                b1 = bank()
                for h in range(8):
                    cx.op("pe", lambda e: e.matmul(out=ps[b1][:, h * 64:(h + 1) * 64], lhsT=XT(h), rhs=KKtm[:, pr, h * 64:(h + 1) * 64], start=True, stop=True),
                          reads=[XTb(h), KKtmb], writes=[psb[b1]], signal=(h == 7))
                cx.op("act", lambda e: e.activation(out=P1s[:].rearrange("p a c -> p (a c)"), in_=ps[b1][:], func=AF.Copy), reads=[psb[b1]], writes=[P1b])
                for g in range(2):
                    b2 = bank()
                    for hh in range(4):
                        h = 4 * g + hh
                        cx.op("pe", lambda e: e.matmul(out=ps[b2][:, hh * 128:(hh + 1) * 128], lhsT=Amat[:, h, 1, :], rhs=XT(h), start=True, stop=True),
                              reads=[Amb, XTb(h)], writes=[psb[b2]], signal=(hh == 3))
                    cx.op("act", lambda e: e.activation(out=P2T[:, 4 * g:4 * g + 4, :].rearrange("p a c -> p (a c)"), in_=ps[b2][:], func=AF.Copy),
                          reads=[psb[b2]], writes=[P2b])
                pump(1)
                b3 = bank()
                for h in range(8):
                    cx.op("pe", lambda e: e.matmul(out=ps[b3][:, h * 64:(h + 1) * 64], lhsT=P2T[:, h, :], rhs=Vtm[:, pr, h * 64:(h + 1) * 64], start=True, stop=True),
                          reads=[P2b, Vtmb], writes=[psb[b3]], signal=(h == 7))
                cx.op("act", lambda e: e.activation(out=Vt[:].rearrange("p a c -> p (a c)"), in_=ps[b3][:], func=AF.Copy), reads=[psb[b3]], writes=[Vtb])
                for c in range(2):
                    rs = slice(c * 64, (c + 1) * 64)
                    bt = bank()
                    while bt % 2 != c:
                        bt = bank()
                    for jt in range(4):
                        js = slice(jt * 128, (jt + 1) * 128)
                        cx.op("pe", lambda e: e.matmul(out=ps[bt][:, js], lhsT=P1s[rs, 2 * jt:2 * jt + 2, :].rearrange("p a c -> p (a c)"), rhs=Bntm[rs, pr, js],
                                                       start=True, stop=True), reads=[P1b, Bntmb], writes=[psb[bt]], signal=(jt == 3))
                    cx.op("dve", lambda e: e.tensor_tensor(out=TnT[:, c, :, :], in0=ps[bt][:].rearrange("p (a c) -> p a c", c=128),
                                                           in1=self.blk64.rearrange("p (o c) -> p o c", o=1).broadcast_to([128, 4, 128]), op=ALU.mult),
                          reads=[psb[bt], self.cb], writes=[TnTb])
                    bg = bank()
                    while bg % 2 != c:
                        bg = bank()
                    for jt in range(4):
                        js = slice(jt * 128, (jt + 1) * 128)
                        cx.op("pe", lambda e: e.matmul(out=ps[bg][:, js], lhsT=Kctm[rs, pr, js], rhs=Vtm[rs, pr, js], start=True, stop=False),
                              reads=[Kctmb, Vtmb], writes=[psb[bg]], signal=False)
                        cx.op("pe", lambda e: e.matmul(out=ps[bg][:, js], lhsT=Bntm[rs, pr, js], rhs=Vt[rs, 2 * jt:2 * jt + 2, :].rearrange("p a c -> p (a c)"),
                                                       start=False, stop=True), reads=[Bntmb, Vtb], writes=[psb[bg]], signal=(jt == 3))
                    for par in range(2):
                        hs = slice(par * 64, (par + 1) * 64)
                        cx.op("act", lambda e: e.activation(out=GnD[hs, c, :, :], in_=ps[bg][hs, :].rearrange("p (a c) -> p a c", c=128)[:, :, par * 64:(par + 1) * 64],
                                                            func=AF.Copy), reads=[psb[bg]], writes=[GnDb])
                for par in (range(2) if emit_y else []):
                    b4 = bank()
                    for hh in range(4):
                        h = 2 * hh + par
                        cx.op("pe", lambda e: e.matmul(out=ps[b4][:, hh * 128:(hh + 1) * 128], lhsT=P1s[:, 2 * hh:2 * hh + 2, :].rearrange("p a c -> p (a c)"),
                                                       rhs=Amat[:, h, 3, :], start=True, stop=True), reads=[P1b, Amb], writes=[psb[b4]], signal=(hh == 3))
                    hs = slice(par * 64, (par + 1) * 64)
                    cx.op("dve", lambda e: e.tensor_tensor(out=Y1T[par][0][hs, :, :], in0=ps[b4][hs, :].rearrange("p (a c) -> p a c", c=128), in1=r_[hs, :, pc], op=ALU.add),
                          reads=[psb[b4], PRbb], writes=[Y1T[par][1]])
                if mode == "A":
                    Yc, Ycb = Ycq[q]
                    WC, WCb = WCq[q]
                    for c in range(2):
                        cc = slice(c * 64, (c + 1) * 64)
                        byc = bank()
                        for h in range(8):
                            o_ = ps[byc][0:64, h * 64:(h + 1) * 64]
                            cx.op("pe", lambda e: e.matmul(out=o_, lhsT=Amat[:, h, 4, cc], rhs=Vtm[:, pr, h * 64:(h + 1) * 64], start=True, stop=False),
                                  reads=[Amb, Vtmb], writes=[psb[byc]], signal=False)
                            cx.op("pe", lambda e: e.matmul(out=o_, lhsT=Amat[:, h, 3, cc], rhs=Vt[:, h, :], start=False, stop=True),
                                  reads=[Amb, Vtb], writes=[psb[byc]], signal=(h == 7))
                        cx.op("act", lambda e: e.activation(out=Yc[:, c, :], in_=ps[byc][0:64, :], func=AF.Copy), reads=[psb[byc]], writes=[Ycb])
                    cx.op("pool", lambda e: e.tensor_copy(out=WC[:], in_=E1[:, :, pr * 128 + 63:pr * 128 + 128:64]), reads=[E1b], writes=[WCb])

                    def mk_units(c, Y1T=Y1T, TnT=TnT, TnTb=TnTb, GnD=GnD, GnDb=GnDb, Yc=Yc, Ycb=Ycb, WC=WC, WCb=WCb, nchunk=(t0 // 64) + pr * 2):
                        cc = slice(c * 64, (c + 1) * 64)
                        nck = nchunk + c

                        def u_y0():
                            Hc, Hcb = H[stt["h"]]
                            byc = bank()
                            for h in range(8):
                                jt, par = h // 2, h % 2
                                cx.op("pe", lambda e: e.matmul(out=ps[byc][0:64, h * 64:(h + 1) * 64], lhsT=Y1T[par][0][:, jt, cc], rhs=Hc[:, jt, :], start=True, stop=True),
                                      reads=[Y1T[par][1], Hcb], writes=[psb[byc]], signal=(h == 7))
                            y0t, y0b, _ = y0st.next()
                            cx.op("dve", lambda e: e.tensor_tensor(out=y0t[:], in0=ps[byc][0:64, :], in1=Yc[:, c, :], op=ALU.add), reads=[psb[byc], Ycb], writes=[y0b])
                            cx.op("pool", lambda e: e.dma_start(out=self.Y0.t[nck], in_=y0t[:]), reads=[y0b], writes=self.Y0.b(nck * 64, nck * 64 + 64), dma=y0b)

                        def u_zt():
                            Apc, Apcb = Ap[stt["a"]]
                            bz_ = bank()
                            for h in range(8):
                                jt, par = h // 2, h % 2
                                cx.op("pe", lambda e: e.matmul(out=ps[bz_][:, h * 64:(h + 1) * 64], lhsT=Apc[:, jt, :], rhs=Y1T[par][0][:, jt, cc], start=True, stop=True),
                                      reads=[Apcb, Y1T[par][1]], writes=[psb[bz_]], signal=(h == 7))
                            zt_, ztb, _ = zst.next()
                            cx.op("act", lambda e: e.activation(out=zt_[:], in_=ps[bz_][:], func=AF.Copy), reads=[psb[bz_]], writes=[ztb])
                            cx.op("pool", lambda e: e.dma_start(out=self.ZS.t[nck], in_=zt_[:]), reads=[ztb], writes=self.ZS.b(nck * 64, nck * 64 + 64), dma=ztb)

                        def u_h():
                            Hc, Hcb = H[stt["h"]]
                            Hn, Hnb = H[1 - stt["h"]]
                            bh_ = bank()
                            for jt in range(4):
                                cx.op("pe", lambda e: e.matmul(out=ps[bh_][:, jt * 64:(jt + 1) * 64], lhsT=TnT[:, c, jt, :], rhs=Hc[:, jt, :], start=True, stop=True),
                                      reads=[TnTb, Hcb], writes=[psb[bh_]], signal=(jt == 3))
                            cx.op("pool", lambda e: e.tensor_tensor(out=Ht[:], in0=Hc[:], in1=WC[:, :, c:c + 1].broadcast_to([128, 4, 64]), op=ALU.mult),
                                  reads=[Hcb, WCb], writes=[Htb])
                            cx.op("pool", lambda e: e.tensor_tensor(out=Ht[:], in0=Ht[:], in1=GnD[:, c, :, :], op=ALU.add), reads=[Htb, GnDb], writes=[Htb])
                            cx.op("dve", lambda e: e.tensor_tensor(out=Hn[:], in0=Ht[:], in1=ps[bh_][:, 0:256].rearrange("p (a c) -> p a c", c=64), op=ALU.add),
                                  reads=[Htb, psb[bh_]], writes=[Hnb])
                            stt["h"] = 1 - stt["h"]

                        def u_ap():
                            Apc, Apcb = Ap[stt["a"]]
                            Apn, Apnb = Ap[1 - stt["a"]]
                            ba_ = bank()
                            for jt in range(4):
                                cx.op("pe", lambda e: e.matmul(out=ps[ba_][:, jt * 128:(jt + 1) * 128], lhsT=TnT[:, c, jt, :], rhs=Apc[:, jt, :], start=True, stop=True),
                                      reads=[TnTb, Apcb], writes=[psb[ba_]], signal=(jt == 3))
                            cx.op("pool", lambda e: e.tensor_tensor(out=ApT[:], in0=Apc[:], in1=WC[:, :, c:c + 1].broadcast_to([128, 4, 128]), op=ALU.mult),
                                  reads=[Apcb, WCb], writes=[ApTb])
                            cx.op("dve", lambda e: e.tensor_tensor(out=Apn[:], in0=ApT[:], in1=ps[ba_][:].rearrange("p (a c) -> p a c", c=128), op=ALU.add),
                                  reads=[ApTb, psb[ba_]], writes=[Apnb])
                            stt["a"] = 1 - stt["a"]
                        return [u_y0, u_zt, u_h, u_ap]
                    pump(len(deferred))
                    deferred.extend(mk_units(0) + mk_units(1))
                    continue
                by_ = bank()
                for c in range(2):
                    cc = slice(c * 64, (c + 1) * 64)
                    Hc, Hcb = H[hcur]
                    Hn, Hnb = H[1 - hcur]
                    byc = bank()
                    for h in (range(8) if emit_y else []):
                        jt, par = h // 2, h % 2
                        o_ = ps[byc][0:64, h * 64:(h + 1) * 64]
                        cx.op("pe", lambda e: e.matmul(out=o_, lhsT=Y1T[par][0][:, jt, cc], rhs=Hc[:, jt, :], start=True, stop=False),
                              reads=[Y1T[par][1], Hcb], writes=[psb[byc]], signal=False)
                        cx.op("pe", lambda e: e.matmul(out=o_, lhsT=Amat[:, h, 4, cc], rhs=Vtm[:, pr, h * 64:(h + 1) * 64], start=False, stop=False),
                              reads=[Amb, Vtmb], writes=[psb[byc]], signal=False)
                        cx.op("pe", lambda e: e.matmul(out=o_, lhsT=Amat[:, h, 3, cc], rhs=Vt[:, h, :], start=False, stop=True),
                              reads=[Amb, Vtb], writes=[psb[byc]], signal=(h == 7))
                    bh_ = bank()
                    for jt in range(4):
                        cx.op("pe", lambda e: e.matmul(out=ps[bh_][:, jt * 64:(jt + 1) * 64], lhsT=TnT[:, c, jt, :], rhs=Hc[:, jt, :], start=True, stop=True),
                              reads=[TnTb, Hcb], writes=[psb[bh_]], signal=(jt == 3))
                    ci = pr * 2 + c
                    wc1 = E1[:, :, ci * 64 + 63:ci * 64 + 64].broadcast_to([128, 4, 64])
                    cx.op("pool", lambda e: e.tensor_tensor(out=Ht[:], in0=Hc[:], in1=wc1, op=ALU.mult), reads=[Hcb, E1b], writes=[Htb])
                    cx.op("pool", lambda e: e.tensor_tensor(out=Ht[:], in0=Ht[:], in1=GnD[:, c, :, :], op=ALU.add), reads=[Htb, GnDb], writes=[Htb])
                    cx.op("dve", lambda e: e.tensor_tensor(out=Hn[:], in0=Ht[:], in1=ps[bh_][:, 0:256].rearrange("p (a c) -> p a c", c=64), op=ALU.add),
                          reads=[Htb, psb[bh_]], writes=[Hnb])
                    hcur = 1 - hcur
                    if mode == "A":
                        Apc, Apcb = Ap[apcur]
                        Apn, Apnb = Ap[1 - apcur]
                        y0t, y0b, _ = y0st.next()
                        cx.op("act", lambda e: e.activation(out=y0t[:], in_=ps[byc][0:64, :], func=AF.Copy), reads=[psb[byc]], writes=[y0b])
                        nchunk = (t0 // 64) + pr * 2 + c
                        cx.op("pool", lambda e: e.dma_start(out=self.Y0.t[nchunk], in_=y0t[:]), reads=[y0b], writes=self.Y0.b(nchunk * 64, nchunk * 64 + 64), dma=y0b)
                        bz_ = bank()
                        for h in range(8):
                            jt, par = h // 2, h % 2
                            cx.op("pe", lambda e: e.matmul(out=ps[bz_][:, h * 64:(h + 1) * 64], lhsT=Apc[:, jt, :], rhs=Y1T[par][0][:, jt, cc], start=True, stop=True),
                                  reads=[Apcb, Y1T[par][1]], writes=[psb[bz_]], signal=(h == 7))
                        zt_, ztb, _ = zst.next()
                        cx.op("act", lambda e: e.activation(out=zt_[:], in_=ps[bz_][:], func=AF.Copy), reads=[psb[bz_]], writes=[ztb])
                        cx.op("pool", lambda e: e.dma_start(out=self.ZS.t[nchunk], in_=zt_[:]), reads=[ztb], writes=self.ZS.b(nchunk * 64, nchunk * 64 + 64), dma=ztb)
                        ba_ = bank()
                        for jt in range(4):
                            cx.op("pe", lambda e: e.matmul(out=ps[ba_][:, jt * 128:(jt + 1) * 128], lhsT=TnT[:, c, jt, :], rhs=Apc[:, jt, :], start=True, stop=True),
                                  reads=[TnTb, Apcb], writes=[psb[ba_]], signal=(jt == 3))
                        wc2 = E1[:, :, ci * 64 + 63:ci * 64 + 64].broadcast_to([128, 4, 128])
                        cx.op("pool", lambda e: e.tensor_tensor(out=ApT[:], in0=Apc[:], in1=wc2, op=ALU.mult), reads=[Apcb, E1b], writes=[ApTb])
                        cx.op("dve", lambda e: e.tensor_tensor(out=Apn[:], in0=ApT[:], in1=ps[ba_][:].rearrange("p (a c) -> p a c", c=128), op=ALU.add),
                              reads=[ApTb, psb[ba_]], writes=[Apnb])
                        apcur = 1 - apcur
                    if not gn:
                        continue
                    y3 = ps[byc][0:64, :].rearrange("p (h d) -> p h d", d=64)
                    cx.op("dve", lambda e: e.tensor_reduce(out=st1[:], in_=y3, axis=AX.X, op=ALU.add), reads=[psb[byc]], writes=[st1b])
                    cx.op("act", lambda e: e.activation(out=ysq[:], in_=ps[byc][0:64, :], func=AF.Square), reads=[psb[byc]], writes=[ysqb])
                    cx.op("dve", lambda e: e.tensor_reduce(out=st2[:], in_=ysq[:].rearrange("p (h d) -> p h d", d=64), axis=AX.X, op=ALU.add),
                          reads=[ysqb], writes=[st2b])
                    cx.op("dve", lambda e: e.tensor_scalar(out=st1[:], in0=st1[:], scalar1=1.0 / 64, scalar2=None, op0=ALU.mult), reads=[st1b], writes=[st1b])
                    cx.op("dve", lambda e: e.tensor_tensor(out=st3[:], in0=st1[:], in1=st1[:], op=ALU.mult), reads=[st1b], writes=[st3b])
                    cx.op("dve", lambda e: e.scalar_tensor_tensor(out=st2[:], in0=st2[:], scalar=1.0 / 64, in1=st3[:], op0=ALU.mult, op1=ALU.subtract),
                          reads=[st2b, st3b], writes=[st2b])
                    cx.op("act", lambda e: e.activation(out=st2[:], in_=st2[:], func=AF.Sqrt, bias=self.eps_gn[0:64, :]), reads=[st2b, self.cb], writes=[st2b])
                    cx.op("dve", lambda e: e.reciprocal(out=st2[:], in_=st2[:]), reads=[st2b], writes=[st2b])
                    cx.op("dve", lambda e: e.tensor_tensor(out=yn[:].rearrange("p (h d) -> p h d", d=64), in0=y3,
                                                           in1=st1[:].rearrange("p (h o) -> p h o", o=1).broadcast_to([64, 8, 64]), op=ALU.subtract),
                          reads=[psb[byc], st1b], writes=[ynb])
                    cx.op("dve", lambda e: e.tensor_tensor(out=yn[:].rearrange("p (h d) -> p h d", d=64), in0=yn[:].rearrange("p (h d) -> p h d", d=64),
                                                           in1=st2[:].rearrange("p (h o) -> p h o", o=1).broadcast_to([64, 8, 64]), op=ALU.mult),
                          reads=[ynb, st2b], writes=[ynb])
                    for jt in range(4):
                        cx.op("pe", lambda e: e.transpose(out=ps[by_][:, jt * 128 + c * 64:jt * 128 + (c + 1) * 64], in_=yn[:, jt * 128:(jt + 1) * 128],
                                                          identity=self.ident[0:64, 0:64]), reads=[ynb, self.cb], writes=[psb[by_]], signal=(jt == 3))
                if not gn:
                    continue
                for jt in range(4):
                    cx.op("act", lambda e: e.activation(out=yo[:, jt, :], in_=ps[by_][:, jt * 128:(jt + 1) * 128], func=AF.Identity,
                                                        scale=pv("gn_g", jt), bias=pv("gn_b", jt)), reads=[psb[by_], self.pvb], writes=[yob])
                cx.op("dve", lambda e: e.tensor_tensor(out=yo[:], in0=yo[:], in1=bonus[:, :, pc], op=ALU.add), reads=[yob, bonb], writes=[yob])
                cx.op("dve", lambda e: e.tensor_tensor(out=ys[:, :, pc], in0=yo[:], in1=G[:, :, pc], op=ALU.mult), reads=[yob, Gb], writes=[ysb])
            if mode == "A":
                cx.op("pool", lambda e: e.dma_start(out=self.GS.t[:, :, t0:t0 + RB].rearrange("f p t -> p f t"), in_=G[:]),
                      reads=[Gb], writes=self.GS.b(t0, t0 + RB), dma=Gb)
                cx.op("pool", lambda e: e.dma_start(out=self.BS.t[:, :, t0:t0 + RB].rearrange("f p t -> p f t"), in_=bonus[:]),
                      reads=[bonb], writes=self.BS.b(t0, t0 + RB), dma=bonb)
            if gn:
                cx.op("pool", lambda e: e.dma_start(out=self.YR.t[:, :, t0:t0 + RB].rearrange("f p t -> p f t"), in_=ys[:]),
                      reads=[ysb], writes=self.YR.b(t0, t0 + RB), dma=ysb)
        if mode == "A":
            pump(len(deferred))
            Apc, Apcb = Ap[stt["a"]]
            Hc, Hcb = H[stt["h"]]
            bq = bank()
            for jt in range(4):
                cx.op("pe", lambda e: e.transpose(out=ps[bq][:, jt * 128:(jt + 1) * 128], in_=Apc[:, jt, :], identity=self.ident),
                      reads=[Apcb, self.cb], writes=[psb[bq]], signal=(jt == 3))
            cx.op("act", lambda e: e.activation(out=pay[:, 0:512], in_=ps[bq][:], func=AF.Copy), reads=[psb[bq]], writes=[payb])
            cx.op("dve", lambda e: e.tensor_copy(out=pay[:, 512:768], in_=Hc[:].rearrange("p a c -> p (a c)")), reads=[Hcb], writes=[payb])
            cx.op("pool", lambda e: e.dma_start(out=self.EX2s, in_=pay[:]), reads=[payb], writes=[self.EX2sb], dma=payb)
            cx.op("pool", lambda e: e.collective_compute("AllGather", ALU.bypass, replica_groups=GROUPS, ins=[self.EX2s], outs=[self.EX2d]),
                  reads=[self.EX2sb], writes=[self.EX2db], coll=True)
        cx.end_phase()


def _phase_rwkv_out(self, l):
    cx, nc, T = self.cx, self.nc, self.T
    ps, psb = self.ps, self.psb
    RB = 256
    pv = lambda name, j: self.pv(l, name, j)
    with contextlib.ExitStack() as es:
        def sb(n, sh, dt=F32):
            return es.enter_context(nc.sbuf_tensor(uniq("o_" + n), sh, dt)), Buf("o_" + n)
        g2, g2b = sb("g2", [128, 4, 768])
        H = [sb("H%d" % i, [128, 4, 64]) for i in range(2)]
        Ht, Htb = sb("Ht", [128, 4, 64])
        zr = Ring(cx, es, "o_z", [128, 512], F32, 3)
        y0r = Ring(cx, es, "o_y0", [64, 512], F32, 3)
        gr = Ring(cx, es, "o_g", [128, 4, RB], F32, 2)
        br = Ring(cx, es, "o_b", [128, 4, RB], F32, 2)
        yst = Ring(cx, es, "o_yst", [128, 4, RB], BF16, 2)
        ysum = [sb("ysum%d" % i, [64, 512]) for i in range(2)]
        ysq = [sb("ysq%d" % i, [64, 512]) for i in range(2)]
        yn = [sb("yn%d" % i, [64, 512]) for i in range(2)]
        st1 = [sb("st1_%d" % i, [64, 8]) for i in range(2)]
        st2 = [sb("st2_%d" % i, [64, 8]) for i in range(2)]
        st3 = [sb("st3_%d" % i, [64, 8]) for i in range(2)]
        yo = [sb("yo%d" % i, [128, 4, 128]) for i in range(2)]
        cx.op("dve", lambda e: e.memset(H[0][0][:], 0.0), writes=[H[0][1]])
        cx.op("sp", lambda e: e.dma_start(out=g2[:], in_=self.EX2d.rearrange("(r p) c -> p r c", r=4)), reads=[self.EX2db], writes=[g2b], dma=g2b)
        hc_ = 0
        for r in range(4):
            Hc, Hcb = H[hc_]
            Hn, Hnb = H[1 - hc_]
            bz = 7
            for jt in range(4):
                cx.op("pe", lambda e: e.matmul(out=ps[bz][:, jt * 64:(jt + 1) * 64], lhsT=g2[:, r, jt * 128:(jt + 1) * 128], rhs=Hc[:, jt, :], start=True, stop=True),
                      reads=[g2b, Hcb], writes=[psb[bz]], signal=(jt == 3))
            cx.op("dve", lambda e: e.tensor_tensor(out=Ht[:].rearrange("p a c -> p (a c)"), in0=ps[bz][:, 0:256], in1=g2[:, r, 512:768], op=ALU.add),
                  reads=[psb[bz], g2b], writes=[Htb])
            cx.op("dve", lambda e: e.tensor_tensor(out=Ht[:], in0=Ht[:], in1=Hc[:], op=ALU.subtract), reads=[Htb, Hcb], writes=[Htb])
            cx.op("dve", lambda e: e.scalar_tensor_tensor(out=Hn[:], in0=Ht[:], scalar=self.selsb[:, 4 + r:5 + r], in1=Hc[:], op0=ALU.mult, op1=ALU.add),
                  reads=[Htb, Hcb, self.selb], writes=[Hnb])
            hc_ = 1 - hc_
        Hs, Hsb = H[hc_]
        it = 0
        for blk in range(T // RB):
            t0 = blk * RB
            G, Gb, _ = gr.next()
            bonus, bonb, _ = br.next()
            ys, ysb, _ = yst.next()
            cx.op("sp", lambda e: e.dma_start(out=G[:], in_=self.GS.t[:, :, t0:t0 + RB].rearrange("f p t -> p f t")),
                  reads=self.GS.b(t0, t0 + RB), writes=[Gb], dma=Gb)
            cx.op("sp", lambda e: e.dma_start(out=bonus[:], in_=self.BS.t[:, :, t0:t0 + RB].rearrange("f p t -> p f t")),
                  reads=self.BS.b(t0, t0 + RB), writes=[bonb], dma=bonb)
            for pr in range(RB // 128):
                pc = slice(pr * 128, (pr + 1) * 128)
                by_ = 4 + (it // 2) % 2
                for c in range(2):
                    n = t0 // 64 + pr * 2 + c
                    d = it % 2
                    it += 1
                    zt, ztb, _ = zr.next()
                    y0, y0b, _ = y0r.next()
                    cx.op("sp", lambda e: e.dma_start(out=zt[:], in_=self.ZS.t[n]), reads=self.ZS.b(n * 64, n * 64 + 64), writes=[ztb], dma=ztb)
                    cx.op("sp", lambda e: e.dma_start(out=y0[:], in_=self.Y0.t[n]), reads=self.Y0.b(n * 64, n * 64 + 64), writes=[y0b], dma=y0b)
                    byc = d
                    for h in range(8):
                        cx.op("pe", lambda e: e.matmul(out=ps[byc][0:64, h * 64:(h + 1) * 64], lhsT=zt[:, h * 64:(h + 1) * 64], rhs=Hs[:, h // 2, :], start=True, stop=True),
                              reads=[ztb, Hsb], writes=[psb[byc]], signal=(h == 7))
                    ysm, ysmb = ysum[d]
                    cx.op("dve", lambda e: e.tensor_tensor(out=ysm[:], in0=ps[byc][0:64, :], in1=y0[:], op=ALU.add), reads=[psb[byc], y0b], writes=[ysmb])
                    y3 = ysm[:].rearrange("p (h d) -> p h d", d=64)
                    s1, s1b = st1[d]
                    s2, s2b = st2[d]
                    s3, s3b = st3[d]
                    yq, yqb = ysq[d]
                    ynn, ynb = yn[d]
                    cx.op("dve", lambda e: e.tensor_reduce(out=s1[:], in_=y3, axis=AX.X, op=ALU.add), reads=[ysmb], writes=[s1b])
                    cx.op("act", lambda e: e.activation(out=yq[:], in_=ysm[:], func=AF.Square), reads=[ysmb], writes=[yqb])
                    cx.op("dve", lambda e: e.tensor_reduce(out=s2[:], in_=yq[:].rearrange("p (h d) -> p h d", d=64), axis=AX.X, op=ALU.add),
                          reads=[yqb], writes=[s2b])
                    cx.op("dve", lambda e: e.tensor_scalar(out=s1[:], in0=s1[:], scalar1=1.0 / 64, scalar2=None, op0=ALU.mult), reads=[s1b], writes=[s1b])
                    cx.op("dve", lambda e: e.tensor_tensor(out=s3[:], in0=s1[:], in1=s1[:], op=ALU.mult), reads=[s1b], writes=[s3b])
                    cx.op("dve", lambda e: e.scalar_tensor_tensor(out=s2[:], in0=s2[:], scalar=1.0 / 64, in1=s3[:], op0=ALU.mult, op1=ALU.subtract),
                          reads=[s2b, s3b], writes=[s2b])
                    cx.op("act", lambda e: e.activation(out=s2[:], in_=s2[:], func=AF.Sqrt, bias=self.eps_gn[0:64, :]), reads=[s2b, self.cb], writes=[s2b])
                    cx.op("dve", lambda e: e.reciprocal(out=s2[:], in_=s2[:]), reads=[s2b], writes=[s2b])
                    cx.op("pool", lambda e: e.tensor_tensor(out=ynn[:].rearrange("p (h d) -> p h d", d=64), in0=y3,
                                                            in1=s1[:].rearrange("p (h o) -> p h o", o=1).broadcast_to([64, 8, 64]), op=ALU.subtract),
                          reads=[ysmb, s1b], writes=[ynb])
                    cx.op("dve", lambda e: e.tensor_tensor(out=ynn[:].rearrange("p (h d) -> p h d", d=64), in0=ynn[:].rearrange("p (h d) -> p h d", d=64),
                                                           in1=s2[:].rearrange("p (h o) -> p h o", o=1).broadcast_to([64, 8, 64]), op=ALU.mult),
                          reads=[ynb, s2b], writes=[ynb])
                    for jt in range(4):
                        cx.op("pe", lambda e: e.transpose(out=ps[by_][:, jt * 128 + c * 64:jt * 128 + (c + 1) * 64], in_=ynn[:, jt * 128:(jt + 1) * 128],
                                                          identity=self.ident[0:64, 0:64]), reads=[ynb, self.cb], writes=[psb[by_]], signal=(jt == 3))
                yo_, yob = yo[(it // 2) % 2]
                for jt in range(4):
                    cx.op("act", lambda e: e.activation(out=yo_[:, jt, :], in_=ps[by_][:, jt * 128:(jt + 1) * 128], func=AF.Identity,
                                                        scale=pv("gn_g", jt), bias=pv("gn_b", jt)), reads=[psb[by_], self.pvb], writes=[yob])
                cx.op("pool", lambda e: e.tensor_tensor(out=yo_[:], in0=yo_[:], in1=bonus[:, :, pc], op=ALU.add), reads=[yob, bonb], writes=[yob])
                cx.op("dve", lambda e: e.tensor_tensor(out=ys[:, :, pc], in0=yo_[:], in1=G[:, :, pc], op=ALU.mult), reads=[yob, Gb], writes=[ysb])
            cx.op("pool", lambda e: e.dma_start(out=self.YR.t[:, :, t0:t0 + RB].rearrange("f p t -> p f t"), in_=ys[:]),
                  reads=[ysb], writes=self.YR.b(t0, t0 + RB), dma=ysb)
        cx.end_phase()


Builder.phase_rwkv_out = _phase_rwkv_out


Builder.decl_rwkv = _decl_rwkv
Builder.phase_rwkv = _phase_rwkv


def prep_rwkv(inp, L, sh):
    lora = np.zeros((L, 128, 3, 512), np.float32)
    for l in range(L):
        lora[l, 0:64, 0] = inp["decay_lora_b"][l]
        lora[l, 64:128, 0] = inp["iclr_lora_b"][l]
        lora[l, :, 1] = inp["gate_lora_b"][l]
        if l > 0:
            lora[l, 0:32, 2] = inp["vres_lora_b"][l - 1]
    sh["lora"] = lora.reshape(L, 128, 3 * 512)
    i = np.arange(128)[:, None]
    j = np.arange(128)[None, :]
    same = (i // 64) == (j // 64)
    SL = (same & (j < i)).astype(np.float32)
    SU = (same & (j > i)).astype(np.float32)
    UI = (same & (j >= i)).astype(np.float32)
    rm = np.stack([-SL, SL, -SU, -UI, UI], axis=1)
    sh["rmask"] = np.ascontiguousarray(rm).reshape(128, 640)


def build_full(T, L, seg=False):
    b = Builder(T, L)
    b.decl_mix()
    b.decl_att()
    b.decl_merge()
    b.decl_rwkv()
    if seg:
        b.decl_seg()
    b.cast_weights()
    for l in range(L):
        src = b.xin if l == 0 else b.XS
        b.phase_ffn(l, 0, src, b.XS, "norm_ffn1")
        if seg:
            b.phase_proj(l, b.XS, hook=lambda: b.phase_ex1(l))
            b.phase_ex1_select(l)
            b.phase_rwkv(l, "A")
            b.phase_att(l)
            b.phase_rwkv_out(l)
        else:
            b.phase_proj(l, b.XS)
            b.phase_rwkv(l)
            b.phase_att(l)
        b.phase_merge(l, b.XS, b.XS)
        b.phase_ffn(l, 1, b.XS, b.out if l == L - 1 else b.XS, "norm_ffn2")
    b.cx.final_wait()
    return b


def seg_selectors(s):
    sel = np.zeros((128, 4), np.float32)
    pre = np.zeros((128, 4), np.float32)
    if s > 0:
        sel[:, s - 1] = 1.0
    pre[:, :s] = 1.0
    return sel, pre


def prep_all(inp, L):
    sh = prep_shared(inp, L)
    prep_mix(inp, L, sh)
    prep_att(inp, L, sh)
    prep_merge(inp, L, sh)
    prep_rwkv(inp, L, sh)
    return sh


def kernel(**inputs):
    x = np.asarray(inputs["x"], np.float32)
    mem = np.asarray(inputs["mem"], np.float32)
    B, S, _ = x.shape
    L = int(np.asarray(inputs["norm_ffn1"]).shape[0])
    NSEG = 4
    T = S // NSEG
    b = build_full(T, L, seg=True)
    sh = prep_all(inputs, L)
    in_maps = []
    for c in range(B * NSEG):
        bi, si = c // NSEG, c % NSEG
        m = {k: sh[k] for k in b.win if k in sh}
        m["sel"], m["pre"] = seg_selectors(si)
        m["xT"] = to_fm(x[bi, si * T:(si + 1) * T])
        m["memT"] = to_fm(mem[bi])
        in_maps.append(m)
    res = run_bass_kernel_spmd(b.nc, in_maps, core_ids=list(range(B * NSEG)))
    out = np.zeros((B, S, D), np.float32)
    for c, r in enumerate(res.results):
        bi, si = c // NSEG, c % NSEG
        out[bi, si * T:(si + 1) * T] = from_fm(np.asarray(r["outT"], np.float32))
    return out


GROUPS = [[0, 1, 2, 3], [4, 5, 6, 7]]
EXA = 2048
EXB = 2112


def _decl_seg(self):
    nc, es = self.nc, self.es
    self.seg = True
    self.wdecl("sel", [128, 4], cast=False)
    self.wdecl("pre", [128, 4], cast=False)
    self.EX1s = nc.dram_tensor("EX1s", [128, EXA], BF16, kind="Internal").ap()
    self.EX1d = nc.dram_tensor("EX1d", [4 * 128, EXA], BF16, kind="Internal").ap()
    self.EX1bs = nc.dram_tensor("EX1bs", [128, EXB], BF16, kind="Internal").ap()
    self.EX1bd = nc.dram_tensor("EX1bd", [4 * 128, EXB], BF16, kind="Internal").ap()
    self.EX1bsb, self.EX1bdb = Buf("EX1bs"), Buf("EX1bd")
    self.EX2s = nc.dram_tensor("EX2s", [128, 768], F32, kind="Internal").ap()
    self.EX2d = nc.dram_tensor("EX2d", [4 * 128, 768], F32, kind="Internal").ap()
    self.EX1sb, self.EX1db, self.EX2sb, self.EX2db = Buf("EX1s"), Buf("EX1d"), Buf("EX2s"), Buf("EX2d")
    self.KH = DramT(nc, "KH", [4, 128, 512], BF16)
    self.VH = DramT(nc, "VH", [4, 128, 520], BF16)
    T = self.T
    self.Y0 = DramT(nc, "Y0", [T // 64, 64, 512], F32, gran=64)
    self.ZS = DramT(nc, "ZS", [T // 64, 128, 512], F32, gran=64)
    self.GS = DramT(nc, "GS", [4, 128, T], F32)
    self.BS = DramT(nc, "BS", [4, 128, T], F32)
    self.SH = nc.dram_tensor("SH", [128, 14], F32, kind="Internal").ap()
    self.SHb = Buf("SH")
    self.selsb = es.enter_context(nc.sbuf_tensor("sb_sel", [128, 8], F32))
    self.selb = Buf("sel", const=True)
    self.shcol = es.enter_context(nc.sbuf_tensor("sb_shcol", [128, 14], F32))
    self.shcolb = Buf("shcol")
    cx = self.cx
    cx.op("sp", lambda e: e.dma_start(out=self.selsb[:, 0:4], in_=self.win["sel"]), writes=[self.selb], dma=self.selb)
    cx.op("sp", lambda e: e.dma_start(out=self.selsb[:, 4:8], in_=self.win["pre"]), writes=[self.selb], dma=self.selb)
    cx.keep_sems()


def _phase_ex1(self, l):
    cx, nc, T = self.cx, self.nc, self.T
    cx.op("sp", lambda e: e.dma_start(out=self.EX1s[:, 0:2048].rearrange("p (f t) -> p f t", f=4),
                                      in_=self.KT.t[:, :, T - 512:T].rearrange("f p t -> p f t")),
          reads=self.KT.b(T - 512, T), writes=[self.EX1sb], dma=self.EX1sb)
    cx.op("sp", lambda e: e.dma_start(out=self.EX1bs[:, 0:2080].rearrange("p (n c) -> p n c", n=4),
                                      in_=self.V1.t[T // 128 - 4:T // 128].rearrange("n p c -> p n c")),
          reads=self.V1.b(T - 512, T), writes=[self.EX1bsb], dma=self.EX1bsb)
    cx.op("sp", lambda e: e.dma_start(out=self.EX1bs[:, 2080:2108].bitcast(F32), in_=self.shcol[:]),
          reads=[self.shcolb], writes=[self.EX1bsb], dma=self.EX1bsb)
    cx.op("pool", lambda e: e.collective_compute("AllGather", ALU.bypass, replica_groups=GROUPS, ins=[self.EX1s], outs=[self.EX1d]),
          reads=[self.EX1sb], writes=[self.EX1db], coll=True)
    cx.op("pool", lambda e: e.collective_compute("AllGather", ALU.bypass, replica_groups=GROUPS, ins=[self.EX1bs], outs=[self.EX1bd]),
          reads=[self.EX1bsb], writes=[self.EX1bdb], coll=True)


def _phase_ex1_select(self, l):
    cx, nc, T = self.cx, self.nc, self.T
    with contextlib.ExitStack() as es:
        g = es.enter_context(nc.sbuf_tensor(uniq("x1_g"), [128, 4, EXA], BF16))
        g2_ = es.enter_context(nc.sbuf_tensor(uniq("x1_g2"), [128, 4, EXB], BF16))
        acc = es.enter_context(nc.sbuf_tensor(uniq("x1_acc"), [128, 4128], BF16))
        accf = es.enter_context(nc.sbuf_tensor(uniq("x1_accf"), [128, 14], F32))
        gb, g2b_, accb, accb2, accfb = Buf("x1_g"), Buf("x1_g2"), Buf("x1_acc"), Buf("x1_acc2"), Buf("x1_accf")
        cx.op("sp", lambda e: e.dma_start(out=g[:], in_=self.EX1d.rearrange("(r p) c -> p r c", r=4)), reads=[self.EX1db], writes=[gb], dma=gb)
        cx.op("sp", lambda e: e.dma_start(out=g2_[:], in_=self.EX1bd.rearrange("(r p) c -> p r c", r=4)), reads=[self.EX1bdb], writes=[g2b_], dma=g2b_)
        for r in range(4):
            sc = self.selsb[:, r:r + 1]
            gf = g2_[:, r, 2080:2108].bitcast(F32)
            if r == 0:
                cx.op("dve", lambda e: e.tensor_scalar(out=acc[:, 0:2048], in0=g[:, r, :], scalar1=sc, scalar2=None, op0=ALU.mult),
                      reads=[gb, self.selb], writes=[accb])
                cx.op("dve", lambda e: e.tensor_scalar(out=acc[:, 2048:4128], in0=g2_[:, r, 0:2080], scalar1=sc, scalar2=None, op0=ALU.mult),
                      reads=[g2b_, self.selb], writes=[accb2])
                cx.op("pool", lambda e: e.tensor_scalar(out=accf[:], in0=gf, scalar1=sc, scalar2=None, op0=ALU.mult),
                      reads=[g2b_, self.selb], writes=[accfb])
            else:
                cx.op("dve", lambda e: e.scalar_tensor_tensor(out=acc[:, 0:2048], in0=g[:, r, :], scalar=sc, in1=acc[:, 0:2048], op0=ALU.mult, op1=ALU.add),
                      reads=[gb, self.selb, accb], writes=[accb])
                cx.op("dve", lambda e: e.scalar_tensor_tensor(out=acc[:, 2048:4128], in0=g2_[:, r, 0:2080], scalar=sc, in1=acc[:, 2048:4128], op0=ALU.mult, op1=ALU.add),
                      reads=[g2b_, self.selb, accb2], writes=[accb2])
                cx.op("dve", lambda e: e.scalar_tensor_tensor(out=accf[:], in0=gf, scalar=sc, in1=accf[:], op0=ALU.mult, op1=ALU.add),
                      reads=[g2b_, self.selb, accfb], writes=[accfb])
        cx.op("pool", lambda e: e.dma_start(out=self.KH.t.rearrange("f p t -> p f t"), in_=acc[:, 0:2048].rearrange("p (f t) -> p f t", f=4)),
              reads=[accb], writes=self.KH.b(0, 512), dma=accb)
        cx.op("pool", lambda e: e.dma_start(out=self.VH.t.rearrange("n p c -> p n c"), in_=acc[:, 2048:4128].rearrange("p (n c) -> p n c", n=4)),
              reads=[accb2], writes=self.VH.b(0, 512), dma=accb2)
        cx.op("pool", lambda e: e.dma_start(out=self.SH, in_=accf[:]), reads=[accfb], writes=[self.SHb], dma=accfb)
        cx.end_phase()


Builder.decl_seg = _decl_seg
Builder.phase_ex1 = _phase_ex1
Builder.phase_ex1_select = _phase_ex1_select
```

```python
import contextlib
import numpy as np
import concourse.bass as bass
import concourse.mybir as mybir
from concourse.bass_utils import run_bass_kernel_spmd

F32 = mybir.dt.float32
BF16 = mybir.dt.bfloat16
AF = mybir.ActivationFunctionType
ALU = mybir.AluOpType
AX = mybir.AxisListType

D = 1024
DFF = 2816
NJ = DFF // 128
RWKV_IN = 1792
D_IN = 3840
NH = 8
HD = 64
CH = 64
MEMT = 256
RMS_EPS = 1e-6
GN_EPS = 64e-5

PV = {}
_off = 0
for _n, _w in [("norm_ffn1", 8), ("norm_mix", 8), ("norm_ffn2", 8), ("shift_mu", 14), ("decay_w0", 4),
               ("iclr_a0", 4), ("k_k", 4), ("k_a", 4), ("r_k", 4), ("gn_g", 4), ("gn_b", 4), ("vres_v0", 4),
               ("att_q_norm", 1), ("att_k_norm", 1), ("mem_q_norm", 1), ("mem_k_norm", 1), ("norm_mem", 8),
               ("b_gate", 24)]:
    PV[_n] = _off
    _off += _w
NPV = _off


_UID = [0]


def uniq(name):
    _UID[0] += 1
    return "%s_u%d" % (name, _UID[0])


class Buf:
    __slots__ = ("name", "w", "r", "const", "sem", "sem_sw")

    def __init__(self, name, const=False):
        self.name = name
        self.w = None
        self.r = {}
        self.const = const
        self.sem = None
        self.sem_sw = None


class Ctx:
    def __init__(self, nc, n_dma_sems=36):
        self.nc = nc
        self.es = contextlib.ExitStack()
        self.eng = {"pe": nc.tensor, "act": nc.scalar, "dve": nc.vector, "pool": nc.gpsimd, "sp": nc.sync}
        self.sems = {}
        self.cnt = {}
        for e in ["pe", "act", "dve", "pool", "sp"]:
            self.sems[e] = self.es.enter_context(nc.semaphore("prog_" + e))
            self.cnt[e] = 0
        self.known = {e: {} for e in self.eng}
        self.free_dma = []
        for i in range(n_dma_sems):
            k = "dma%d" % i
            self.sems[k] = self.es.enter_context(nc.semaphore(k))
            self.cnt[k] = 0
            self.free_dma.append(k)
        self.ninst = 0
        self.phase_sems = []
        self.free_sw = []
        self.free_cc = []
        for i in range(12):
            k = "cc%d" % i
            self.sems[k] = self.es.enter_context(nc.semaphore(k))
            self.cnt[k] = 0
            self.free_cc.append(k)
        for i in range(48):
            k = "swd%d" % i
            self.sems[k] = self.es.enter_context(nc.semaphore(k))
            self.cnt[k] = 0
            self.free_sw.append(k)

    def dma_sem(self):
        return self.free_dma.pop(0)

    def release_dma_sem(self, k):
        self.free_dma.append(k)

    def _wait(self, e, deps):
        eng = self.eng[e]
        kn = self.known[e]
        best = {}
        for (s, v) in deps:
            if s == "pe" and e == "pe":
                continue
            if kn.get(s, 0) < v and best.get(s, 0) < v:
                best[s] = v
        for s, v in best.items():
            eng.wait_ge(self.sems[s], v)
            kn[s] = v

    def op(self, e, fn, reads=(), writes=(), dma=None, signal=True, coll=False):
        deps = []
        for b in reads:
            if b.w is not None:
                deps.append(b.w)
        for b in writes:
            if b.w is not None:
                deps.append(b.w)
            deps.extend(b.r.items())
        self._wait(e, deps)
        inst = fn(self.eng[e])
        if coll:
            k = self.free_cc.pop(0)
            self.cnt[k] += 1
            inst.then_inc(self.sems[k], 1)
            tok = (k, self.cnt[k])
        elif dma is not None:
            if e == "pool":
                if dma.sem_sw is None:
                    dma.sem_sw = self.free_sw.pop(0)
                    self.phase_sems.append(dma)
                dma = dma.sem_sw
            else:
                if dma.sem is None:
                    dma.sem = self.free_dma.pop(0)
                    self.phase_sems.append(dma)
                dma = dma.sem
            self.cnt[dma] += 16
            inst.then_inc(self.sems[dma], 16)
            tok = (dma, self.cnt[dma])
        elif signal:
            self.cnt[e] += 1
            inst.then_inc(self.sems[e], 1)
            tok = (e, self.cnt[e])
        else:
            tok = (e, self.cnt[e] + 1)
        for b in reads:
            if not b.const:
                if b.r.get(tok[0], 0) < tok[1]:
                    b.r[tok[0]] = tok[1]
        for b in writes:
            b.w = tok
            b.r = {}
        self.ninst += 1
        return tok

    def barrier(self):
        allk = [(k, v) for k, v in self.cnt.items() if v > 0]
        for e in self.eng:
            self._wait(e, [(k, v) for (k, v) in allk if not (k == e)])

    def end_phase(self):
        self.barrier()
        for b in self.phase_sems:
            if b.sem is not None:
                self.free_dma.append(b.sem)
                b.sem = None
            if b.sem_sw is not None:
                self.free_sw.append(b.sem_sw)
                b.sem_sw = None
        self.phase_sems = []

    def keep_sems(self):
        self.phase_sems = []

    def final_wait(self):
        allk = [(k, v) for k, v in self.cnt.items() if v > 0]
        self._wait("sp", [(k, v) for (k, v) in allk if k != "sp"])


class Ring:
    def __init__(self, cx, es, name, shape, dtype, n):
        self.cx = cx
        self.slots = []
        for i in range(n):
            t = es.enter_context(cx.nc.sbuf_tensor(uniq("%s%d" % (name, i)), shape, dtype))
            bf = Buf("%s%d" % (name, i))
            self.slots.append((t, bf, bf))
        self.i = 0

    def next(self):
        s = self.slots[self.i % len(self.slots)]
        self.i += 1
        return s


class DramT:
    def __init__(self, nc, name, shape, dtype, kind="Internal", gran=512):
        self.t = nc.dram_tensor(name, shape, dtype, kind=kind).ap()
        self.gran = gran
        self.name = name
        self.bufs = {}

    def b(self, t0, t1):
        out = []
        for g in range(t0 // self.gran, (t1 - 1) // self.gran + 1):
            if g not in self.bufs:
                self.bufs[g] = Buf("%s_%d" % (self.name, g))
            out.append(self.bufs[g])
        return out


def fm_layout(W):
    K, N = W.shape
    return np.ascontiguousarray(W.reshape(K // 128, 128, N // 128, 128).transpose(2, 1, 0, 3))


def mv_layout(W):
    K, N = W.shape
    return np.ascontiguousarray(W.reshape(K // 128, 128, N).transpose(1, 0, 2))


class Builder:
    def __init__(self, T, L, debug=False):
        self.T = T
        self.L = L
        self.debug = debug
        nc = self.nc = bass.Bass("TRN2", target_bir_lowering=False)
        cx = self.cx = Ctx(nc)
        es = self.es = cx.es
        self.xin = DramT(nc, "xT", [8, 128, T], F32, kind="ExternalInput")
        self.out = DramT(nc, "outT", [8, 128, T], F32, kind="ExternalOutput")
        self.XS = DramT(nc, "XS", [8, 128, T], F32)
        self.win = {}
        self.wbf = {}
        self.wbuf = {}

        def wdecl(name, shape, cast=True):
            self.win[name] = nc.dram_tensor(name, shape, F32, kind="ExternalInput").ap()
            if cast:
                self.wbf[name] = nc.dram_tensor(name + "_bf", shape, BF16, kind="Internal").ap()
                if name.startswith("ffn"):
                    self.wbuf[name] = [Buf("%s_l%d" % (name, i)) for i in range(shape[0])]
                else:
                    if not hasattr(self, "_restbuf"):
                        self._restbuf = [Buf("wrest_l%d" % i) for i in range(shape[0])]
                    self.wbuf[name] = self._restbuf
        self.wdecl = wdecl
        wdecl("ffn_wi", [L, 2, NJ, 128, 8 * 256])
        wdecl("ffn_wo", [L, 2, 8, 128, NJ * 128])
        wdecl("pvec", [L, 128, NPV], cast=False)
        wdecl("consts", [128, 1024], cast=False)
        self.pvec = es.enter_context(nc.sbuf_tensor("sb_pvec", [128, L, NPV], F32))
        self.pvb = Buf("pvec", const=True)
        self.consts = es.enter_context(nc.sbuf_tensor("sb_consts", [128, 1024], F32))
        self.cb = Buf("consts", const=True)
        cx.op("sp", lambda e: e.dma_start(out=self.pvec[:], in_=self.win["pvec"].rearrange("l p n -> p l n")),
              writes=[self.pvb], dma=self.pvb)
        cx.op("sp", lambda e: e.dma_start(out=self.consts[:], in_=self.win["consts"]), writes=[self.cb], dma=self.cb)
        cx.keep_sems()
        self.ones = self.consts[:, 0:128]
        self.ident = self.consts[:, 128:256]
        self.blk64 = self.consts[:, 256:384]
        self.eps_rms = self.consts[:, 384:385]
        self.eps_q = self.consts[:, 385:386]
        self.eps_mq = self.consts[:, 386:387]
        self.eps_gn = self.consts[:, 387:388]
        self.cmask = self.consts[:, 512:1024]
        self.ps = []
        self.psb = []
        for i in range(8):
            self.ps.append(es.enter_context(nc.psum_tensor("ps%d" % i, [128, 512], F32)))
            self.psb.append(Buf("ps%d" % i))

    def pv(self, l, name, i=0, n=1):
        c = PV[name] + i
        return self.pvec[:, l, c:c + n]

    def cast_weights(self):
        cx = self.cx
        for l in range(self.L):
            for name, dst in self.wbf.items():
                src = self.win[name]
                wb = self.wbuf[name][l]
                cx.op("pool", lambda e: e.dma_start(out=dst[l], in_=src[l]), writes=[wb], dma=wb)
                wb.const = True
        cx.keep_sems()

    def rmsnorm_fm(self, xT, xb, cs, gains, xn, xnb, tmp, nft=8, w=512, dim=D):
        cx, ps, psb = self.cx, self.ps, self.psb
        sq, sqb, ss, ssb, rstd, rsb = tmp
        cx.op("act", lambda e: e.activation(out=sq[:, 0:nft, 0:w], in_=xT[:, 0:nft, cs], func=AF.Square),
              reads=[xb], writes=[sqb])
        cx.op("dve", lambda e: e.tensor_reduce(out=ss[:, 0:w], in_=sq[:, 0:nft, 0:w].rearrange("p f t -> p t f"),
                                               axis=AX.X, op=ALU.add), reads=[sqb], writes=[ssb])
        cx.op("pe", lambda e: e.matmul(out=ps[6][:, 0:w], lhsT=self.ones, rhs=ss[:, 0:w], start=True, stop=True),
              reads=[ssb, self.cb], writes=[psb[6]])
        cx.op("act", lambda e: e.activation(out=rstd[:, 0:w], in_=ps[6][:, 0:w], func=AF.Ln,
                                            scale=1.0 / dim, bias=self.eps_rms), reads=[psb[6], self.cb], writes=[rsb])
        cx.op("act", lambda e: e.activation(out=rstd[:, 0:w], in_=rstd[:, 0:w], func=AF.Exp, scale=-0.5), reads=[rsb], writes=[rsb])
        for ft in range(nft):
            cx.op("dve", lambda e: e.scalar_tensor_tensor(
                out=xn[:, ft, cs], in0=xT[:, ft, cs], scalar=gains(ft), in1=rstd[:, 0:w],
                op0=ALU.mult, op1=ALU.mult), reads=[xb, rsb, self.pvb], writes=[xnb])

    def norm_tmp(self, es, pfx):
        nc = self.nc
        sq = es.enter_context(nc.sbuf_tensor(uniq(pfx + "_sq"), [128, 8, 512], F32))
        ss = es.enter_context(nc.sbuf_tensor(uniq(pfx + "_ss"), [128, 512], F32))
        rstd = es.enter_context(nc.sbuf_tensor(uniq(pfx + "_rstd"), [128, 512], F32))
        return (sq, Buf(pfx + "_sq"), ss, Buf(pfx + "_ss"), rstd, Buf(pfx + "_rstd"))

    def phase_ffn(self, l, f, src, dst, gname):
        cx, nc, T = self.cx, self.nc, self.T
        TB = min(1024, T)
        NTC = TB // 512
        ps, psb = self.ps, self.psb
        with contextlib.ExitStack() as es:
            xT = es.enter_context(nc.sbuf_tensor(uniq("f_xT"), [128, 8, TB], F32))
            xn = es.enter_context(nc.sbuf_tensor(uniq("f_xn"), [128, 8, TB], BF16))
            hT = es.enter_context(nc.sbuf_tensor(uniq("f_hT"), [128, NJ, TB], BF16))
            ntmp = self.norm_tmp(es, "f")
            sg = [es.enter_context(nc.sbuf_tensor(uniq("f_sg%d" % i), [128, 512], F32)) for i in range(2)]
            xb = [Buf("f_xT%d" % i) for i in range(NTC)]
            xnb = [Buf("f_xn%d" % i) for i in range(NTC)]
            hb = [Buf("f_hT%d" % i) for i in range(NTC)]
            sgb = [Buf("f_sg0"), Buf("f_sg1")]
            w1r = Ring(cx, es, "f_w1", [128, 8, 256], BF16, 5)
            wor = Ring(cx, es, "f_wo", [128, NJ, 128], BF16, 3)
            w1d = self.wbf["ffn_wi"]
            wod = self.wbf["ffn_wo"]
            it = 0
            for blk in range(T // TB):
                t0 = blk * TB
                for tc in range(NTC):
                    a, b = t0 + tc * 512, t0 + (tc + 1) * 512
                    cs = slice(tc * 512, (tc + 1) * 512)
                    cx.op("sp", lambda e: e.dma_start(out=xT[:, :, cs], in_=src.t[:, :, a:b].rearrange("f p t -> p f t")),
                          reads=src.b(a, b), writes=[xb[tc]], dma=xb[tc])
                    self.rmsnorm_fm(xT, xb[tc], cs, lambda ft: self.pv(l, gname, ft), xn, xnb[tc], ntmp)
                for j in range(NJ):
                    w1, w1b, w1k = w1r.next()
                    cx.op("sp", lambda e: e.dma_start(out=w1[:], in_=w1d[l, f, j].rearrange("p (k c) -> p k c", c=256)),
                          reads=[self.wbuf["ffn_wi"][l]], writes=[w1b], dma=w1k)
                    for tc in range(NTC):
                        cs = slice(tc * 512, (tc + 1) * 512)
                        pg, pu = it % 2, 2 + it % 2
                        for kt in range(8):
                            cx.op("pe", lambda e: e.matmul(out=ps[pg][:], lhsT=w1[:, kt, 0:128], rhs=xn[:, kt, cs],
                                                           start=(kt == 0), stop=(kt == 7)),
                                  reads=[w1b, xnb[tc]], writes=[psb[pg]], signal=(kt == 7))
                        for kt in range(8):
                            cx.op("pe", lambda e: e.matmul(out=ps[pu][:], lhsT=w1[:, kt, 128:256], rhs=xn[:, kt, cs],
                                                           start=(kt == 0), stop=(kt == 7)),
                                  reads=[w1b, xnb[tc]], writes=[psb[pu]], signal=(kt == 7))
                        s = it % 2
                        cx.op("act", lambda e: e.activation(out=sg[s][:], in_=ps[pg][:], func=AF.Silu),
                              reads=[psb[pg]], writes=[sgb[s]])
                        cx.op("dve", lambda e: e.tensor_tensor(out=hT[:, j, cs], in0=sg[s][:], in1=ps[pu][:], op=ALU.mult),
                              reads=[sgb[s], psb[pu]], writes=[hb[tc]])
                        it += 1
                for o in range(8):
                    wo, wob, wok = wor.next()
                    cx.op("sp", lambda e: e.dma_start(out=wo[:], in_=wod[l, f, o].rearrange("p (k c) -> p k c", c=128)),
                          reads=[self.wbuf["ffn_wo"][l]], writes=[wob], dma=wok)
                    for tc in range(NTC):
                        cs = slice(tc * 512, (tc + 1) * 512)
                        po = 4 + it % 2
                        it += 1
                        for kt in range(NJ):
                            cx.op("pe", lambda e: e.matmul(out=ps[po][:], lhsT=wo[:, kt, :], rhs=hT[:, kt, cs],
                                                           start=(kt == 0), stop=(kt == NJ - 1)),
                                  reads=[wob, hb[tc]], writes=[psb[po]], signal=(kt == NJ - 1))
                        cx.op("dve", lambda e: e.scalar_tensor_tensor(out=xT[:, o, cs], in0=ps[po][:], scalar=0.5,
                                                                      in1=xT[:, o, cs], op0=ALU.mult, op1=ALU.add),
                              reads=[psb[po], xb[tc]], writes=[xb[tc]])
                for tc in range(NTC):
                    a, b = t0 + tc * 512, t0 + (tc + 1) * 512
                    cs = slice(tc * 512, (tc + 1) * 512)
                    cx.op("pool", lambda e: e.dma_start(out=dst.t[:, :, a:b].rearrange("f p t -> p f t"), in_=xT[:, :, cs]),
                          reads=[xb[tc]], writes=dst.b(a, b), dma=xb[tc])
            cx.end_phase()


def make_consts():
    c = np.zeros((128, 1024), np.float32)
    c[:, 0:128] = 1.0
    c[:, 128:256] = np.eye(128, dtype=np.float32)
    blk = np.zeros((128, 128), np.float32)
    blk[:64, :64] = 1.0
    blk[64:, 64:] = 1.0
    c[:, 256:384] = blk
    c[:, 384] = RMS_EPS
    c[:, 385] = 64 * RMS_EPS
    c[:, 386] = 128 * RMS_EPS
    c[:, 387] = GN_EPS
    c[:, 512:1024] = 1.0
    c[:, 512:1024:64] = 0.0
    return c


def prep_shared(inp, L):
    g = {k: np.asarray(v, np.float32) for k, v in inp.items() if k not in ("x", "mem")}
    sh = {}
    wi = np.zeros((L, 2, NJ, 128, 8 * 256), np.float32)
    wo = np.zeros((L, 2, 8, 128, NJ * 128), np.float32)
    for l in range(L):
        for f, (a, b) in enumerate([("ffn1_w_in", "ffn1_w_out"), ("ffn2_w_in", "ffn2_w_out")]):
            W = g[a][l]
            G = fm_layout(W[:, :DFF])
            U = fm_layout(W[:, DFF:])
            wi[l, f] = np.concatenate([G, U], axis=3).reshape(NJ, 128, 8 * 256)
            wo[l, f] = fm_layout(g[b][l]).reshape(8, 128, NJ * 128)
    sh["ffn_wi"] = wi
    sh["ffn_wo"] = wo
    pv = np.zeros((L, 128, NPV), np.float32)

    def put(l, name, vec):
        n = vec.size // 128
        pv[l, :, PV[name]:PV[name] + n] = vec.reshape(n, 128).T
    for l in range(L):
        put(l, "norm_ffn1", g["norm_ffn1"][l])
        put(l, "norm_mix", g["norm_mix"][l])
        put(l, "norm_ffn2", g["norm_ffn2"][l])
        put(l, "shift_mu", g["shift_mu"][l])
        put(l, "decay_w0", g["decay_w0"][l])
        put(l, "iclr_a0", g["iclr_a0"][l])
        put(l, "k_k", g["rwkv_k_k"][l])
        put(l, "k_a", g["rwkv_k_a"][l])
        put(l, "r_k", g["rwkv_r_k"][l].reshape(-1))
        put(l, "gn_g", g["rwkv_gn_g"][l])
        put(l, "gn_b", g["rwkv_gn_b"][l])
        if l > 0:
            put(l, "vres_v0", g["vres_v0"][l - 1])
        put(l, "att_q_norm", np.tile(g["att_q_norm"][l], 2))
        put(l, "att_k_norm", np.tile(g["att_k_norm"][l], 2))
        put(l, "mem_q_norm", g["mem_q_norm"][l])
        put(l, "mem_k_norm", g["mem_k_norm"][l])
        put(l, "norm_mem", g["norm_mem"][l])
        put(l, "b_gate", g["b_gate"][l])
    sh["pvec"] = pv
    sh["consts"] = make_consts()
    return sh


def to_fm(x):
    T = x.shape[0]
    return np.ascontiguousarray(x.T.reshape(x.shape[1] // 128, 128, T))


def from_fm(y):
    n, p, T = y.shape
    return np.ascontiguousarray(y.reshape(n * p, T).T)


def _decl_mix(self):
    L, T, nc = self.L, self.T, self.nc
    self.wdecl("wmix", [L, 26, 128, 8 * 128])
    self.wdecl("wvres", [L, 128, 8 * 32])
    self.wdecl("wv", [L, 128, 8 * 512])
    self.wdecl("wmk", [L, 4, 128, 8 * 128])
    self.wdecl("wmv", [L, 128, 8 * 512])
    self.memT = DramT(nc, "memT", [8, 128, MEMT], F32, kind="ExternalInput")
    self.HN = DramT(nc, "HN", [8, 128, T], BF16)
    self.PR = DramT(nc, "PR", [15, 128, T], F32)
    self.QT = DramT(nc, "QT", [4, 128, T], BF16)
    self.KT = DramT(nc, "KT", [4, 128, T], BF16)
    self.V1 = DramT(nc, "V1", [T // 128, 128, 520], BF16)
    self.YM = DramT(nc, "YM", [4, 128, T], BF16)
    self.YA = DramT(nc, "YA", [4, 128, T], BF16)
    self.YR = DramT(nc, "YR", [4, 128, T], BF16)
    self.VF = DramT(nc, "VF", [4, 128, T], F32)


def _phase_proj(self, l, src, hook=None):
    cx, nc, T = self.cx, self.nc, self.T
    TB = min(1024, T)
    NTC = TB // 512
    ps, psb = self.ps, self.psb
    with contextlib.ExitStack() as es:
        sb = lambda n, sh, dt: es.enter_context(nc.sbuf_tensor(uniq("p_" + n), sh, dt))
        xT = sb("xT", [128, 8, TB], F32)
        hn = sb("hn", [128, 8, TB], BF16)
        xb = [Buf("p_xT%d" % i) for i in range(NTC)]
        hb = [Buf("p_hn%d" % i) for i in range(NTC)]
        ntmp = self.norm_tmp(es, "p")
        wr = Ring(cx, es, "p_w", [128, 8, 128], BF16, 5)
        stf = Ring(cx, es, "p_stf", [128, 512], F32, 6)
        stb = Ring(cx, es, "p_stb", [128, 512], BF16, 6)
        stv = Ring(cx, es, "p_stv", [128, 8, 65], BF16, 4)
        wv = sb("wv", [128, 8, 512], BF16)
        wvb = Buf("p_wv")
        wvr = sb("wvr", [128, 8, 32], BF16)
        wvrb = Buf("p_wvr")
        sq2 = [sb("sq2_%d" % i, [128, 512], F32) for i in range(2)]
        sq2b = [Buf("p_sq2_%d" % i) for i in range(2)]
        rs2 = [sb("rs2_%d" % i, [128, 512], F32) for i in range(2)]
        rs2b = [Buf("p_rs2_%d" % i) for i in range(2)]
        mqn = sb("mqn", [128, 512], BF16)
        mqnb = Buf("p_mqn")
        esc = sb("esc", [128, 2, 512], BF16)
        escb = Buf("p_esc")
        rden = sb("rden", [128, 512], F32)
        rdenb = Buf("p_rden")
        onesb = sb("onesb", [128, 128], BF16)
        onesbb = Buf("p_onesb")
        memx = sb("memx", [128, 8, MEMT], F32)
        memn = sb("memn", [128, 8, MEMT], BF16)
        memxb, memnb = Buf("p_memx"), Buf("p_memn")
        mkT = sb("mkT", [128, 4, MEMT], BF16)
        mv = sb("mv", [128, 2, 512], BF16)
        mkb, mvb = Buf("p_mkT"), Buf("p_mv")
        wmv = sb("wmv", [128, 8, 512], BF16)
        wmvb = Buf("p_wmv")
        cx.op("dve", lambda e: e.tensor_copy(out=onesb[:], in_=self.ones), reads=[self.cb], writes=[onesbb])
        for s_ in stv.slots:
            cx.op("dve", lambda e: e.memset(s_[0][:, :, 64:65], 1.0), writes=[s_[1]])
        cx.op("sp", lambda e: e.dma_start(out=wv[:], in_=self.wbf["wv"][l].rearrange("p (k c) -> p k c", c=512)),
              reads=[self.wbuf["wv"][l]], writes=[wvb], dma=wvb)
        cx.op("sp", lambda e: e.dma_start(out=wvr[:], in_=self.wbf["wvres"][l].rearrange("p (k c) -> p k c", c=32)),
              reads=[self.wbuf["wvres"][l]], writes=[wvrb], dma=wvrb)
        cx.op("sp", lambda e: e.dma_start(out=wmv[:], in_=self.wbf["wmv"][l].rearrange("p (k c) -> p k c", c=512)),
              reads=[self.wbuf["wmv"][l]], writes=[wmvb], dma=wmvb)
        cx.op("sp", lambda e: e.dma_start(out=memx[:], in_=self.memT.t.rearrange("f p t -> p f t")),
              writes=[memxb], dma=memxb)
        self.rmsnorm_fm(memx, memxb, slice(0, MEMT), lambda ft: self.pv(l, "norm_mem", ft), memn, memnb, ntmp, w=MEMT)
        itp = 0
        for h in range(4):
            w, wb, wk = wr.next()
            cx.op("sp", lambda e: e.dma_start(out=w[:], in_=self.wbf["wmk"][l, h].rearrange("p (k c) -> p k c", c=128)),
                  reads=[self.wbuf["wmk"][l]], writes=[wb], dma=wk)
            for kt in range(8):
                cx.op("pe", lambda e: e.matmul(out=ps[0][:, 0:MEMT], lhsT=w[:, kt, :], rhs=memn[:, kt, :],
                                               start=(kt == 0), stop=(kt == 7)), reads=[wb, memnb], writes=[psb[0]],
                      signal=(kt == 7))
            cx.op("act", lambda e: e.activation(out=sq2[0][:, 0:MEMT], in_=ps[0][:, 0:MEMT], func=AF.Square),
                  reads=[psb[0]], writes=[sq2b[0]])
            cx.op("pe", lambda e: e.matmul(out=ps[2][:, 0:MEMT], lhsT=self.ones, rhs=sq2[0][:, 0:MEMT], start=True, stop=True),
                  reads=[sq2b[0], self.cb], writes=[psb[2]])
            cx.op("act", lambda e: e.activation(out=rs2[0][:, 0:MEMT], in_=ps[2][:, 0:MEMT], func=AF.Sqrt,
                                                scale=1.0 / 128, bias=self.eps_rms), reads=[psb[2], self.cb], writes=[rs2b[0]])
            cx.op("dve", lambda e: e.reciprocal(out=rs2[0][:, 0:MEMT], in_=rs2[0][:, 0:MEMT]), reads=[rs2b[0]], writes=[rs2b[0]])
            cx.op("dve", lambda e: e.scalar_tensor_tensor(out=mkT[:, h, :], in0=ps[0][:, 0:MEMT], scalar=self.pv(l, "mem_k_norm"),
                                                          in1=rs2[0][:, 0:MEMT], op0=ALU.mult, op1=ALU.mult),
                  reads=[psb[0], rs2b[0], self.pvb], writes=[mkb])
        for mt in range(2):
            for kt in range(8):
                cx.op("pe", lambda e: e.matmul(out=ps[1][:], lhsT=memn[:, kt, mt * 128:(mt + 1) * 128], rhs=wmv[:, kt, :],
                                               start=(kt == 0), stop=(kt == 7)), reads=[wmvb, memnb], writes=[psb[1]],
                      signal=(kt == 7))
            cx.op("act", lambda e: e.activation(out=mv[:, mt, :], in_=ps[1][:], func=AF.Copy), reads=[psb[1]], writes=[mvb])
        order = list(range(T // TB))
        if hook is not None:
            order = order[::-1]
        for bidx, blk in enumerate(order):
            if bidx == 1 and hook is not None:
                hook()
            t0 = blk * TB
            for tc in range(NTC):
                a, b = t0 + tc * 512, t0 + (tc + 1) * 512
                cs = slice(tc * 512, (tc + 1) * 512)
                cx.op("sp", lambda e: e.dma_start(out=xT[:, :, cs], in_=src.t[:, :, a:b].rearrange("f p t -> p f t")),
                      reads=src.b(a, b), writes=[xb[tc]], dma=xb[tc])
                self.rmsnorm_fm(xT, xb[tc], cs, lambda ft: self.pv(l, "norm_mix", ft), hn, hb[tc], ntmp)
                cx.op("pool", lambda e: e.dma_start(out=self.HN.t[:, :, a:b].rearrange("f p t -> p f t"), in_=hn[:, :, cs]),
                      reads=[hb[tc]], writes=self.HN.b(a, b), dma=hb[tc])
            for j in range(26):
                w, wb, wk = wr.next()
                cx.op("sp", lambda e: e.dma_start(out=w[:], in_=self.wbf["wmix"][l, j].rearrange("p (k c) -> p k c", c=128)),
                      reads=[self.wbuf["wmix"][l]], writes=[wb], dma=wk)
                for tc in range(NTC):
                    a, b = t0 + tc * 512, t0 + (tc + 1) * 512
                    cs = slice(tc * 512, (tc + 1) * 512)
                    pb = itp % 2
                    itp += 1
                    for kt in range(8):
                        cx.op("pe", lambda e: e.matmul(out=ps[pb][:], lhsT=w[:, kt, :], rhs=hn[:, kt, cs],
                                                       start=(kt == 0), stop=(kt == 7)), reads=[wb, hb[tc]], writes=[psb[pb]],
                              signal=(kt == 7))
                    if j < 14:
                        st, stbuf, stk = stf.next()
                        cx.op("act", lambda e: e.activation(out=st[:], in_=ps[pb][:], func=AF.Copy), reads=[psb[pb]], writes=[stbuf])
                        if getattr(self, "seg", False) and b == T:
                            cx.op("act", lambda e: e.activation(out=self.shcol[:, j:j + 1], in_=st[:, 511:512], func=AF.Copy),
                                  reads=[stbuf], writes=[self.shcolb])
                        cx.op("pool", lambda e: e.dma_start(out=self.PR.t[j, :, a:b], in_=st[:]), reads=[stbuf],
                              writes=self.PR.b(a, b), dma=stk)
                        continue
                    i2 = pb
                    cx.op("act", lambda e: e.activation(out=sq2[i2][:], in_=ps[pb][:], func=AF.Square), reads=[psb[pb]], writes=[sq2b[i2]])
                    red = self.ones if j >= 22 else self.blk64
                    cx.op("pe", lambda e: e.matmul(out=ps[2 + i2][:], lhsT=red, rhs=sq2[i2][:], start=True, stop=True),
                          reads=[sq2b[i2], self.cb], writes=[psb[2 + i2]])
                    if j < 18:
                        sc_, bi_, gn_ = 1.0, self.eps_q, "att_q_norm"
                    elif j < 22:
                        sc_, bi_, gn_ = 1.0 / 64, self.eps_rms, "att_k_norm"
                    else:
                        sc_, bi_, gn_ = 1.0, self.eps_mq, "mem_q_norm"
                    cx.op("act", lambda e: e.activation(out=rs2[i2][:], in_=ps[2 + i2][:], func=AF.Ln, scale=sc_, bias=bi_),
                          reads=[psb[2 + i2], self.cb], writes=[rs2b[i2]])
                    cx.op("act", lambda e: e.activation(out=rs2[i2][:], in_=rs2[i2][:], func=AF.Exp, scale=-0.5), reads=[rs2b[i2]], writes=[rs2b[i2]])
                    if j < 22:
                        st, stbuf, stk = stb.next()
                        cx.op("dve", lambda e: e.scalar_tensor_tensor(out=st[:], in0=ps[pb][:], scalar=self.pv(l, gn_), in1=rs2[i2][:],
                                                                      op0=ALU.mult, op1=ALU.mult),
                              reads=[psb[pb], rs2b[i2], self.pvb], writes=[stbuf])
                        dstT = self.QT if j < 18 else self.KT
                        jj = (j - 14) % 4
                        cx.op("pool", lambda e: e.dma_start(out=dstT.t[jj, :, a:b], in_=st[:]), reads=[stbuf],
                              writes=dstT.b(a, b), dma=stk)
                        continue
                    h = j - 22
                    cx.op("dve", lambda e: e.scalar_tensor_tensor(out=mqn[:], in0=ps[pb][:], scalar=self.pv(l, gn_), in1=rs2[i2][:],
                                                                  op0=ALU.mult, op1=ALU.mult),
                          reads=[psb[pb], rs2b[i2], self.pvb], writes=[mqnb])
                    for mt in range(2):
                        cx.op("pe", lambda e: e.matmul(out=ps[4 + mt][:], lhsT=mkT[:, h, mt * 128:(mt + 1) * 128], rhs=mqn[:],
                                                       start=True, stop=True), reads=[mkb, mqnb], writes=[psb[4 + mt]])
                        cx.op("act", lambda e: e.activation(out=esc[:, mt, :], in_=ps[4 + mt][:], func=AF.Exp),
                              reads=[psb[4 + mt]], writes=[escb])
                    for mt in range(2):
                        cx.op("pe", lambda e: e.matmul(out=ps[7][:], lhsT=onesb[:], rhs=esc[:, mt, :], start=(mt == 0), stop=(mt == 1)),
                              reads=[onesbb, escb], writes=[psb[7]], signal=(mt == 1))
                    for mt in range(2):
                        cx.op("pe", lambda e: e.matmul(out=ps[6][:], lhsT=mv[:, mt, h * 128:(h + 1) * 128], rhs=esc[:, mt, :],
                                                       start=(mt == 0), stop=(mt == 1)), reads=[mvb, escb], writes=[psb[6]],
                              signal=(mt == 1))
                    cx.op("dve", lambda e: e.reciprocal(out=rden[:], in_=ps[7][:]), reads=[psb[7]], writes=[rdenb])
                    st, stbuf, stk = stb.next()
                    cx.op("dve", lambda e: e.tensor_tensor(out=st[:], in0=ps[6][:], in1=rden[:], op=ALU.mult),
                          reads=[psb[6], rdenb], writes=[stbuf])
                    cx.op("pool", lambda e: e.dma_start(out=self.YM.t[h, :, a:b], in_=st[:]), reads=[stbuf],
                          writes=self.YM.b(a, b), dma=stk)
            if l > 0:
                for tc in range(NTC):
                    a, b = t0 + tc * 512, t0 + (tc + 1) * 512
                    cs = slice(tc * 512, (tc + 1) * 512)
                    pb = itp % 2
                    itp += 1
                    for kt in range(8):
                        cx.op("pe", lambda e: e.matmul(out=ps[pb][0:32, :], lhsT=wvr[:, kt, :], rhs=hn[:, kt, cs],
                                                       start=(kt == 0), stop=(kt == 7)), reads=[wvrb, hb[tc]], writes=[psb[pb]],
                              signal=(kt == 7))
                    st, stbuf, stk = stf.next()
                    cx.op("act", lambda e: e.activation(out=st[0:32, :], in_=ps[pb][0:32, :], func=AF.Copy), reads=[psb[pb]], writes=[stbuf])
                    cx.op("pool", lambda e: e.dma_start(out=self.PR.t[14, 0:32, a:b], in_=st[0:32, :]), reads=[stbuf],
                          writes=self.PR.b(a, b), dma=stk)
            for tt in range(TB // 128):
                tc = tt // 4
                pb = itp % 2
                itp += 1
                for kt in range(8):
                    cx.op("pe", lambda e: e.matmul(out=ps[pb][:], lhsT=hn[:, kt, tt * 128:(tt + 1) * 128], rhs=wv[:, kt, :],
                                                   start=(kt == 0), stop=(kt == 7)), reads=[wvb, hb[tc]], writes=[psb[pb]],
                          signal=(kt == 7))
                st, stbuf, stk = stv.next()
                cx.op("act", lambda e: e.activation(out=st[:, :, 0:64], in_=ps[pb][:].rearrange("p (h d) -> p h d", d=64), func=AF.Copy),
                      reads=[psb[pb]], writes=[stbuf])
                ta = t0 + tt * 128
                cx.op("pool", lambda e: e.dma_start(out=self.V1.t[ta // 128], in_=st[:].rearrange("p h d -> p (h d)")),
                      reads=[stbuf], writes=self.V1.b(ta, ta + 128), dma=stk)
        if hook is not None and len(order) == 1:
            hook()
        cx.end_phase()


Builder.decl_mix = _decl_mix
Builder.phase_proj = _phase_proj


def prep_mix(inp, L, sh):
    g = {k: np.asarray(v, np.float32) for k, v in inp.items() if k not in ("x", "mem")}
    wmix = np.zeros((L, 26, 128, 8 * 128), np.float32)
    wvres = np.zeros((L, 128, 8 * 32), np.float32)
    wv = np.zeros((L, 128, 8 * 512), np.float32)
    wmk = np.zeros((L, 4, 128, 8 * 128), np.float32)
    wmv = np.zeros((L, 128, 8 * 512), np.float32)
    for l in range(L):
        W = g["w_in"][l]
        cols = np.concatenate([np.arange(0, 1792), np.arange(1792, 1792 + 1024), np.arange(1792 + 1536, 3840)])
        wmix[l] = fm_layout(W[:, cols]).reshape(26, 128, 1024)
        wv[l] = mv_layout(W[:, 1792 + 1024:1792 + 1536]).reshape(128, 8 * 512)
        if l > 0:
            wvres[l] = mv_layout(g["vres_lora_a"][l - 1]).reshape(128, 8 * 32)
        wmk[l] = fm_layout(g["mem_w_kv"][l][:, :512]).reshape(4, 128, 1024)
        wmv[l] = mv_layout(g["mem_w_kv"][l][:, 512:]).reshape(128, 8 * 512)
    sh.update(wmix=wmix, wvres=wvres, wv=wv, wmk=wmk, wmv=wmv)


def _decl_att(self):
    self.wdecl("relb", [self.L, 128, 5 * 8 * 128], cast=False)
    self.wdecl("amask", [128, 5 * 128], cast=False)


def _phase_att(self, l):
    cx, nc, T = self.cx, self.nc, self.T
    ps, psb = self.ps, self.psb
    QB = 512
    with contextlib.ExitStack() as es:
        sb = lambda n, sh, dt: es.enter_context(nc.sbuf_tensor(uniq("a_" + n), sh, dt))
        M = sb("M", [128, 5, 8, 128], F32)
        Mb = Buf("a_M")
        am = sb("am", [128, 5, 128], F32)
        amb = Buf("a_am")
        cx.op("sp", lambda e: e.dma_start(out=M[:].rearrange("p i h q -> p (i h q)"), in_=self.win["relb"][l]), writes=[Mb], dma=Mb)
        cx.op("sp", lambda e: e.dma_start(out=am[:].rearrange("p i q -> p (i q)"), in_=self.win["amask"]), writes=[amb], dma=amb)
        cx.op("act", lambda e: e.activation(out=M[:].rearrange("p i h q -> p (i h q)"), in_=M[:].rearrange("p i h q -> p (i h q)"), func=AF.Exp),
              reads=[Mb], writes=[Mb])
        for i in range(5):
            cx.op("dve", lambda e: e.tensor_tensor(out=M[:, i], in0=M[:, i], in1=am[:, i:i + 1, :].broadcast_to([128, 8, 128]), op=ALU.mult),
                  reads=[Mb, amb], writes=[Mb])
        Mb.const = True
        qr = Ring(cx, es, "a_q", [128, 4, QB], BF16, 2)
        kr = Ring(cx, es, "a_k", [128, 4, 2 * QB], BF16, 2)
        vr = Ring(cx, es, "a_v", [128, 8, 520], BF16, 2)
        yst = Ring(cx, es, "a_yst", [128, 4, QB], BF16, 2)
        etmp = [sb("etmp%d" % i, [128, 512], F32) for i in range(2)]
        etb = [Buf("a_etmp%d" % i) for i in range(2)]
        expS = sb("expS", [128, 5, 8, 128], BF16)
        expSb = [Buf("a_expS%d" % i) for i in range(5)]
        rd = sb("rd", [128, 8], F32)
        rdb = Buf("a_rd")
        y = sb("y", [128, 8, 64], F32)
        yb = Buf("a_y")
        it = 0
        for blk in range(T // QB):
            a = blk * QB
            q, qb, qk = qr.next()
            k, kb, kk_ = kr.next()
            v, vb, vk = vr.next()
            ys, ysb, ysk = yst.next()
            cx.op("sp", lambda e: e.dma_start(out=q[:], in_=self.QT.t[:, :, a:a + QB].rearrange("f p t -> p f t")),
                  reads=self.QT.b(a, a + QB), writes=[qb], dma=qk)
            k0 = max(0, a - QB)
            off = k0 - (a - QB)
            if a == 0 and getattr(self, "seg", False):
                cx.op("sp", lambda e: e.dma_start(out=k[:, :, 0:QB], in_=self.KH.t.rearrange("f p t -> p f t")),
                      reads=self.KH.b(0, 512), writes=[kb], dma=kk_)
                cx.op("sp", lambda e: e.dma_start(out=v[:, 0:4, :], in_=self.VH.t.rearrange("n p c -> p n c")),
                      reads=self.VH.b(0, 512), writes=[vb], dma=vk)
            cx.op("sp", lambda e: e.dma_start(out=k[:, :, off:2 * QB], in_=self.KT.t[:, :, k0:a + QB].rearrange("f p t -> p f t")),
                  reads=self.KT.b(k0, a + QB), writes=[kb], dma=kk_)
            cx.op("sp", lambda e: e.dma_start(out=v[:, off // 128:8, :], in_=self.V1.t[k0 // 128:(a + QB) // 128].rearrange("n p c -> p n c")),
                  reads=self.V1.b(k0, a + QB), writes=[vb], dma=vk)
            for qp in range(QB // 128):
                if getattr(self, "att_dbg", 9) < 1:
                    break
                a1 = a + qp * 128
                valid = [i for i in range(5) if a1 - 512 + i * 128 >= 0 or getattr(self, "seg", False)]
                qs = slice(qp * 128, (qp + 1) * 128)
                for i in valid:
                    kc = slice(qp * 128 + i * 128, qp * 128 + (i + 1) * 128)
                    for par in range(2):
                        bk = it % 4
                        et = it % 2
                        it += 1
                        hp = par * 64
                        for hh in range(4):
                            jt = hh
                            cx.op("pe", lambda e: e.matmul(out=ps[bk][:, hh * 128:(hh + 1) * 128], lhsT=k[hp:hp + 64, jt, kc],
                                                           rhs=q[hp:hp + 64, jt, qs], start=True, stop=True),
                                  reads=[kb, qb], writes=[psb[bk]], signal=(hh == 3))
                        cx.op("act", lambda e: e.activation(out=etmp[et][:], in_=ps[bk][:], func=AF.Exp), reads=[psb[bk]], writes=[etb[et]])
                        cx.op("dve", lambda e: e.tensor_tensor(out=expS[:, i, par:8:2, :], in0=etmp[et][:].rearrange("p (h q) -> p h q", q=128),
                                                               in1=M[:, i, par:8:2, :], op=ALU.mult),
                              reads=[etb[et], Mb], writes=[expSb[i]])
                if getattr(self, "att_dbg", 9) < 2:
                    continue
                for g in range(2):
                    for hh in range(4):
                        h = 4 * g + hh
                        for i in valid:
                            cx.op("pe", lambda e: e.matmul(out=ps[4 + g][:, hh * 65:(hh + 1) * 65], lhsT=expS[:, i, h, :],
                                                           rhs=v[:, qp + i, h * 65:(h + 1) * 65], start=(i == valid[0]), stop=(i == valid[-1])),
                                  reads=[expSb[i], vb], writes=[psb[4 + g]], signal=(hh == 3 and i == valid[-1]))
                    if getattr(self, "att_dbg", 9) < 3:
                        continue
                    pv_ = ps[4 + g][:, 0:260].rearrange("p (h e) -> p h e", e=65)
                    cx.op("dve", lambda e: e.reciprocal(out=rd[:, 4 * g:4 * g + 4].rearrange("p (h o) -> p h o", o=1), in_=pv_[:, :, 64:65]),
                          reads=[psb[4 + g]], writes=[rdb])
                    cx.op("dve", lambda e: e.tensor_tensor(out=y[:, 4 * g:4 * g + 4, :], in0=pv_[:, :, 0:64],
                                                           in1=rd[:, 4 * g:4 * g + 4].rearrange("p (h o) -> p h o", o=1).broadcast_to([128, 4, 64]),
                                                           op=ALU.mult), reads=[psb[4 + g], rdb], writes=[yb])
                if getattr(self, "att_dbg", 9) < 4:
                    continue
                for jt in range(4):
                    cx.op("pe", lambda e: e.transpose(out=ps[6][:, jt * 128:(jt + 1) * 128],
                                                      in_=y[:, 2 * jt:2 * jt + 2, :].rearrange("p h d -> p (h d)"), identity=self.ident),
                          reads=[yb, self.cb], writes=[psb[6]], signal=(jt == 3))
                cx.op("act", lambda e: e.activation(out=ys[:, :, qs], in_=ps[6][:].rearrange("p (j q) -> p j q", q=128), func=AF.Copy),
                      reads=[psb[6]], writes=[ysb])
            cx.op("pool", lambda e: e.dma_start(out=self.YA.t[:, :, a:a + QB].rearrange("f p t -> p f t"), in_=ys[:]),
                  reads=[ysb], writes=self.YA.b(a, a + QB), dma=ysk)
        cx.end_phase()


Builder.decl_att = _decl_att
Builder.phase_att = _phase_att


def prep_att(inp, L, sh):
    rel = np.asarray(inp["att_rel_bias"], np.float32)
    p = np.arange(128)[:, None, None]
    i = np.arange(5)[None, :, None]
    q = np.arange(128)[None, None, :]
    kpos = i * 128 + p
    dist = q - kpos + 512
    idx = np.clip(dist, -63, 128) + 63
    cq = q // 64
    ck = kpos // 64
    mask = ((ck >= cq) & (ck <= cq + 8)).astype(np.float32)
    relb = np.zeros((L, 128, 5, 8, 128), np.float32)
    for l in range(L):
        for h in range(8):
            relb[l, :, :, h, :] = rel[l, h][idx]
    sh["relb"] = relb.reshape(L, 128, 5 * 8 * 128)
    sh["amask"] = np.ascontiguousarray(np.broadcast_to(mask, (128, 5, 128))).reshape(128, 640)


def _decl_merge(self):
    L = self.L
    self.wdecl("wgate", [L, 24, 128, 8 * 128])
    self.wdecl("wbr", [L, 3, 8, 128, 4 * 128])
    self.wdecl("wout", [L, 8, 128, 8 * 128])


def _phase_merge(self, l, src, dst):
    cx, nc, T = self.cx, self.nc, self.T
    TB = min(1024, T)
    NTC = TB // 512
    ps, psb = self.ps, self.psb
    with contextlib.ExitStack() as es:
        sb = lambda n, sh, dt: es.enter_context(nc.sbuf_tensor(uniq("m_" + n), sh, dt))
        xT = sb("xT", [128, 8, TB], F32)
        hn = sb("hn", [128, 8, TB], BF16)
        yb3 = [sb("y%d" % i, [128, 4, TB], BF16) for i in range(3)]
        mg = sb("mg", [128, 8, TB], BF16)
        macc = [sb("macc%d" % i, [128, 512], F32) for i in range(NTC)]
        gt = [sb("gt%d" % i, [128, 512], F32) for i in range(2)]
        xb = [Buf("m_xT%d" % i) for i in range(NTC)]
        hb = [Buf("m_hn%d" % i) for i in range(NTC)]
        ybb = [[Buf("m_y%d_%d" % (i, t)) for t in range(NTC)] for i in range(3)]
        mgb = [Buf("m_mg%d" % i) for i in range(NTC)]
        maccb = [Buf("m_macc%d" % i) for i in range(NTC)]
        gtb = [Buf("m_gt%d" % i) for i in range(2)]
        wgr = Ring(cx, es, "m_wg", [128, 8, 128], BF16, 5)
        wbrr = Ring(cx, es, "m_wb", [128, 4, 128], BF16, 5)
        ysrc = [self.YR, self.YA, self.YM]
        it = 0
        for blk in range(T // TB):
            t0 = blk * TB
            for tc in range(NTC):
                a, b = t0 + tc * 512, t0 + (tc + 1) * 512
                cs = slice(tc * 512, (tc + 1) * 512)
                cx.op("sp", lambda e: e.dma_start(out=xT[:, :, cs], in_=src.t[:, :, a:b].rearrange("f p t -> p f t")),
                      reads=src.b(a, b), writes=[xb[tc]], dma=xb[tc])
                cx.op("sp", lambda e: e.dma_start(out=hn[:, :, cs], in_=self.HN.t[:, :, a:b].rearrange("f p t -> p f t")),
                      reads=self.HN.b(a, b), writes=[hb[tc]], dma=hb[tc])
                for i in range(3):
                    cx.op("sp", lambda e: e.dma_start(out=yb3[i][:, :, cs], in_=ysrc[i].t[:, :, a:b].rearrange("f p t -> p f t")),
                          reads=ysrc[i].b(a, b), writes=[ybb[i][tc]], dma=ybb[i][tc])
            for o in range(8):
                for br in range(3):
                    wg, wgb, _ = wgr.next()
                    wb_, wbb, _ = wbrr.next()
                    cx.op("sp", lambda e: e.dma_start(out=wg[:], in_=self.wbf["wgate"][l, br * 8 + o].rearrange("p (k c) -> p k c", c=128)),
                          reads=[self.wbuf["wgate"][l]], writes=[wgb], dma=wgb)
                    cx.op("sp", lambda e: e.dma_start(out=wb_[:], in_=self.wbf["wbr"][l, br, o].rearrange("p (k c) -> p k c", c=128)),
                          reads=[self.wbuf["wbr"][l]], writes=[wbb], dma=wbb)
                    for tc in range(NTC):
                        cs = slice(tc * 512, (tc + 1) * 512)
                        pg, pb = it % 2, 2 + it % 2
                        gi = it % 2
                        it += 1
                        for kt in range(8):
                            cx.op("pe", lambda e: e.matmul(out=ps[pg][:], lhsT=wg[:, kt, :], rhs=hn[:, kt, cs], start=(kt == 0), stop=(kt == 7)),
                                  reads=[wgb, hb[tc]], writes=[psb[pg]], signal=(kt == 7))
                        for kt in range(4):
                            cx.op("pe", lambda e: e.matmul(out=ps[pb][:], lhsT=wb_[:, kt, :], rhs=yb3[br][:, kt, cs], start=(kt == 0), stop=(kt == 3)),
                                  reads=[wbb, ybb[br][tc]], writes=[psb[pb]], signal=(kt == 3))
                        cx.op("act", lambda e: e.activation(out=gt[gi][:], in_=ps[pg][:], func=AF.Sigmoid, bias=self.pv(l, "b_gate", br * 8 + o)),
                              reads=[psb[pg], self.pvb], writes=[gtb[gi]])
                        if br == 0:
                            cx.op("dve", lambda e: e.tensor_tensor(out=macc[tc][:], in0=gt[gi][:], in1=ps[pb][:], op=ALU.mult),
                                  reads=[gtb[gi], psb[pb]], writes=[maccb[tc]])
                        else:
                            cx.op("dve", lambda e: e.tensor_tensor(out=gt[gi][:], in0=gt[gi][:], in1=ps[pb][:], op=ALU.mult),
                                  reads=[gtb[gi], psb[pb]], writes=[gtb[gi]])
                            if br == 1:
                                cx.op("pool", lambda e: e.tensor_tensor(out=macc[tc][:], in0=macc[tc][:], in1=gt[gi][:], op=ALU.add),
                                      reads=[gtb[gi], maccb[tc]], writes=[maccb[tc]])
                            else:
                                cx.op("pool", lambda e: e.tensor_tensor(out=mg[:, o, cs], in0=macc[tc][:], in1=gt[gi][:], op=ALU.add),
                                      reads=[gtb[gi], maccb[tc]], writes=[mgb[tc]])
            for o in range(8):
                wg, wgb, _ = wgr.next()
                cx.op("sp", lambda e: e.dma_start(out=wg[:], in_=self.wbf["wout"][l, o].rearrange("p (k c) -> p k c", c=128)),
                      reads=[self.wbuf["wout"][l]], writes=[wgb], dma=wgb)
                for tc in range(NTC):
                    cs = slice(tc * 512, (tc + 1) * 512)
                    po = 4 + it % 2
                    it += 1
                    for kt in range(8):
                        cx.op("pe", lambda e: e.matmul(out=ps[po][:], lhsT=wg[:, kt, :], rhs=mg[:, kt, cs], start=(kt == 0), stop=(kt == 7)),
                              reads=[wgb, mgb[tc]], writes=[psb[po]], signal=(kt == 7))
                    cx.op("dve", lambda e: e.tensor_tensor(out=xT[:, o, cs], in0=ps[po][:], in1=xT[:, o, cs], op=ALU.add),
                          reads=[psb[po], xb[tc]], writes=[xb[tc]])
            for tc in range(NTC):
                a, b = t0 + tc * 512, t0 + (tc + 1) * 512
                cs = slice(tc * 512, (tc + 1) * 512)
                cx.op("pool", lambda e: e.dma_start(out=dst.t[:, :, a:b].rearrange("f p t -> p f t"), in_=xT[:, :, cs]),
                      reads=[xb[tc]], writes=dst.b(a, b), dma=xb[tc])
        cx.end_phase()


Builder.decl_merge = _decl_merge
Builder.phase_merge = _phase_merge


def prep_merge(inp, L, sh):
    g = {k: np.asarray(inp[k], np.float32) for k in ["w_gate", "w_branch_rwkv", "w_branch_att", "w_branch_mem", "w_out"]}
    wgate = np.zeros((L, 24, 128, 1024), np.float32)
    wbr = np.zeros((L, 3, 8, 128, 512), np.float32)
    wout = np.zeros((L, 8, 128, 1024), np.float32)
    for l in range(L):
        wgate[l] = fm_layout(g["w_gate"][l]).reshape(24, 128, 1024)
        for i, n in enumerate(["w_branch_rwkv", "w_branch_att", "w_branch_mem"]):
            wbr[l, i] = fm_layout(g[n][l]).reshape(8, 128, 512)
        wout[l] = fm_layout(g["w_out"][l]).reshape(8, 128, 1024)
    sh.update(wgate=wgate, wbr=wbr, wout=wout)


WC_ = 0.6065306597126334


def _decl_rwkv(self):
    L = self.L
    self.wdecl("lora", [L, 128, 3 * 512], cast=False)
    self.wdecl("rmask", [128, 5 * 128], cast=False)


def _phase_rwkv(self, l, mode="full"):
    emit_y = mode in ("full", "A")
    gn = mode == "full"
    seg = mode == "A"
    cx, nc, T = self.cx, self.nc, self.T
    ps, psb = self.ps, self.psb
    RB = 256
    NP = RB // 128
    NCH = RB // 64
    with contextlib.ExitStack() as es:
        def sb(n, sh, dt=F32):
            return es.enter_context(nc.sbuf_tensor(uniq("r_" + n), sh, dt)), Buf("r_" + n)
        PRb, PRbb = sb("PRb", [128, 15, RB + 1])
        Dt, Db = (None, None) if mode == "A" else sb("D", [128, 14, RB])
        S, Sb = sb("S", [128, 4, RB])
        A, Ab = sb("A", [128, 4, RB])
        G, Gb = sb("G", [128, 4, RB])
        KK, KKb = sb("KK", [128, 4, RB])
        CS, CSb = sb("CS", [128, 4, RB])
        E1, E1b = sb("E1", [128, 4, RB])
        E3, E3b = sb("E3", [128, 4, RB])
        Bh, Bhb = sb("Bh", [128, 4, RB])
        Bc, Bcb = sb("Bc", [128, 4, RB])
        Kc, Kcb = sb("Kc", [128, 4, RB])
        bonus, bonb = sb("bonus", [128, 4, RB])
        tmp4, tmp4b = sb("tmp4", [128, 4, RB])
        VFb, VFbb = sb("VFb", [128, 4, RB])
        tw, twb = sb("tw", [128, RB])
        sgx, sgxb = sb("sgx", [128, RB])
        lora, lorab = sb("lora", [128, 3, 512])
        rmask, rmb = sb("rmask", [128, 5, 128])
        TM = [sb("TM%d" % i, [128, NP, 512]) for i in range(4)]
        Amat, Amb = sb("Amat", [128, 8, 5, 128])
        if mode == "A":
            Dt = Amat[:].rearrange("p a b c -> p (a b c)")[:, 0:14 * RB].rearrange("p (a t) -> p a t", a=14)
            Db = Amb
        Pw = [sb("Pw%d_%d" % (g, i), [128, 4, 2, 128]) for g in range(2) for i in range(2)]
        Acc = [sb("Acc%d_%d" % (g, i), [128, 4, 128]) for g in range(2) for i in range(2)]
        P1s, P1b = sb("P1s", [128, 8, 64])
        P2T, P2b = sb("P2T", [128, 8, 128])
        Vt, Vtb = sb("Vt", [128, 8, 64])
        H = [sb("H%d" % i, [128, 4, 64]) for i in range(2)]
        Ht, Htb = sb("Ht", [128, 4, 64])
        if mode == "A":
            TnTq = [sb("TnTq%d" % i, [128, 2, 4, 128]) for i in range(2)]
            GnDq = [sb("GnDq%d" % i, [128, 2, 4, 64]) for i in range(2)]
            Y1Tq = [[sb("Y1Tq%d_%d" % (q_, i), [128, 4, 128]) for i in range(2)] for q_ in range(2)]
            Ycq = [sb("Ycq%d" % i, [64, 2, 512]) for i in range(2)]
            WCq = [sb("WCq%d" % i, [128, 4, 2]) for i in range(2)]
            TnT, TnTb = TnTq[0]
            GnD, GnDb = GnDq[0]
            Y1T = Y1Tq[0]
            yst = None
        else:
            TnT, TnTb = sb("TnT", [128, 2, 4, 128])
            GnD, GnDb = sb("GnD", [128, 2, 4, 64])
            Y1T = [sb("Y1T%d" % i, [128, 4, 128]) for i in range(2)]
            ysq, ysqb = sb("ysq", [64, 512])
            yn, ynb = sb("yn", [64, 512])
            st1, st1b = sb("st1", [64, 8])
            st2, st2b = sb("st2", [64, 8])
            st3, st3b = sb("st3", [64, 8])
            yo, yob = sb("yo", [128, 4, 128])
            yst = Ring(cx, es, "r_yst", [128, 4, RB], BF16, 2)
        prevc, prevcb = sb("prevc", [128, 14, 1])
        P_ = PRb[:, :, 1:RB + 1]
        r_, k_, v_ = P_[:, 0:4, :], P_[:, 4:8, :], P_[:, 8:12, :]

        cx.op("sp", lambda e: e.dma_start(out=lora[:].rearrange("p a c -> p (a c)"), in_=self.win["lora"][l]), writes=[lorab], dma=lorab)
        cx.op("sp", lambda e: e.dma_start(out=rmask[:].rearrange("p a c -> p (a c)"), in_=self.win["rmask"]), writes=[rmb], dma=rmb)
        lorab.const = True
        rmb.const = True
        if mode == "A":
            Ap = [sb("Ap%d" % i, [128, 4, 128]) for i in range(2)]
            ApT, ApTb = sb("ApT", [128, 4, 128])
            pay, payb = sb("pay", [128, 768])
            zst = Ring(cx, es, "r_zst", [128, 512], F32, 2)
            y0st = Ring(cx, es, "r_y0st", [64, 512], F32, 2)
        apcur = 0
        cx.op("dve", lambda e: e.memset(H[0][0][:], 0.0), writes=[H[0][1]])
        if mode == "A":
            cx.op("dve", lambda e: e.tensor_copy(out=Ap[0][0][:], in_=self.ident.rearrange("p (o c) -> p o c", o=1).broadcast_to([128, 4, 128])),
                  reads=[self.cb], writes=[Ap[0][1]])
        for (yt_, ytb_) in ([x_ for q_ in Y1Tq for x_ in q_] if mode == "A" else Y1T):
            cx.op("dve", lambda e: e.memset(yt_[:], 0.0), writes=[ytb_])
        deferred = []
        stt = {"h": 0, "a": 0, "pair": 0}

        def pump(n):
            for _ in range(n):
                if deferred:
                    deferred.pop(0)()
        pv = lambda name, j: self.pv(l, name, j)
        hcur = 0
        bi = [0]

        def bank():
            bi[0] += 1
            return bi[0] % 8

        for blk in range(T // RB):
            t0 = blk * RB
            ntile = 15 if l > 0 else 14
            cx.op("sp", lambda e: e.dma_start(out=PRb[:, 0:14, 1:RB + 1], in_=self.PR.t[0:14, :, t0:t0 + RB].rearrange("f p t -> p f t")),
                  reads=self.PR.b(t0, t0 + RB), writes=[PRbb], dma=PRbb)
            if l > 0:
                cx.op("sp", lambda e: e.dma_start(out=PRb[0:32, 14, 1:RB + 1], in_=self.PR.t[14, 0:32, t0:t0 + RB]),
                      reads=self.PR.b(t0, t0 + RB), writes=[PRbb], dma=PRbb)
            if t0 == 0 and seg:
                cx.op("sp", lambda e: e.dma_start(out=prevc[:].rearrange("p a o -> p (a o)"), in_=self.SH), reads=[self.SHb], writes=[prevcb], dma=prevcb)
                cx.op("pool", lambda e: e.tensor_copy(out=PRb[:, 0:14, 0:1], in_=prevc[:]), reads=[prevcb], writes=[PRbb])
            elif t0 == 0:
                cx.op("pool", lambda e: e.memset(PRb[:, 0:14, 0:1], 0.0), writes=[PRbb])
            else:
                cx.op("pool", lambda e: e.tensor_copy(out=PRb[:, 0:14, 0:1], in_=prevc[:]), reads=[prevcb], writes=[PRbb])
            cx.op("pool", lambda e: e.tensor_copy(out=prevc[:], in_=PRb[:, 0:14, RB:RB + 1]), reads=[PRbb], writes=[prevcb])
            if l > 0:
                cx.op("sp", lambda e: e.dma_start(out=VFb[:], in_=self.VF.t[:, :, t0:t0 + RB].rearrange("f p t -> p f t")),
                      reads=self.VF.b(t0, t0 + RB), writes=[VFbb], dma=VFbb)
            cx.op("pool", lambda e: e.tensor_tensor(out=Dt[:], in0=PRb[:, 0:14, 0:RB], in1=PRb[:, 0:14, 1:RB + 1], op=ALU.subtract),
                  reads=[PRbb], writes=[Db])
            for j in range(14):
                cx.op("dve", lambda e: e.scalar_tensor_tensor(out=P_[:, j, :], in0=Dt[:, j, :], scalar=pv("shift_mu", j), in1=P_[:, j, :],
                                                              op0=ALU.mult, op1=ALU.add), reads=[Db, PRbb, self.pvb], writes=[PRbb])
            cx.op("act", lambda e: e.activation(out=tw[0:64, :], in_=P_[0:64, 12, :], func=AF.Tanh), reads=[PRbb], writes=[twb])
            cx.op("act", lambda e: e.activation(out=sgx[:], in_=P_[:, 13, :], func=AF.Sigmoid), reads=[PRbb], writes=[sgxb])
            for jt in range(4):
                js = slice(jt * 128, (jt + 1) * 128)
                cx.op("pe", lambda e: e.matmul(out=ps[0][:, 0:RB], lhsT=lora[0:64, 0, js], rhs=tw[0:64, :], start=True, stop=True),
                      reads=[lorab, twb], writes=[psb[0]])
                cx.op("act", lambda e: e.activation(out=S[:, jt, :], in_=ps[0][:, 0:RB], func=AF.Sigmoid, bias=pv("decay_w0", jt)),
                      reads=[psb[0], self.pvb], writes=[Sb])
                cx.op("pe", lambda e: e.matmul(out=ps[1][:, 0:RB], lhsT=lora[64:128, 0, js], rhs=P_[64:128, 12, :], start=True, stop=True),
                      reads=[lorab, PRbb], writes=[psb[1]])
                cx.op("act", lambda e: e.activation(out=A[:, jt, :], in_=ps[1][:, 0:RB], func=AF.Sigmoid, bias=pv("iclr_a0", jt)),
                      reads=[psb[1], self.pvb], writes=[Ab])
                if emit_y:
                    cx.op("pe", lambda e: e.matmul(out=ps[2][:, 0:RB], lhsT=lora[:, 1, js], rhs=sgx[:], start=True, stop=True),
                          reads=[lorab, sgxb], writes=[psb[2]])
                    cx.op("act", lambda e: e.activation(out=G[:, jt, :], in_=ps[2][:, 0:RB], func=AF.Copy), reads=[psb[2]], writes=[Gb])
                if l > 0:
                    cx.op("pe", lambda e: e.matmul(out=ps[3][:, 0:RB], lhsT=lora[0:32, 2, js], rhs=P_[0:32, 14, :], start=True, stop=True),
                          reads=[lorab, PRbb], writes=[psb[3]])
                    cx.op("act", lambda e: e.activation(out=tmp4[:, jt, :], in_=ps[3][:, 0:RB], func=AF.Sigmoid, bias=pv("vres_v0", jt)),
                          reads=[psb[3], self.pvb], writes=[tmp4b])
            if l > 0:
                cx.op("dve", lambda e: e.tensor_tensor(out=VFb[:], in0=VFb[:], in1=v_, op=ALU.subtract), reads=[VFbb, PRbb], writes=[VFbb])
                cx.op("dve", lambda e: e.tensor_tensor(out=VFb[:], in0=VFb[:], in1=tmp4[:], op=ALU.mult), reads=[VFbb, tmp4b], writes=[VFbb])
                cx.op("dve", lambda e: e.tensor_tensor(out=v_, in0=v_, in1=VFb[:], op=ALU.add), reads=[VFbb, PRbb], writes=[PRbb])
            else:
                cx.op("pool", lambda e: e.dma_start(out=self.VF.t[:, :, t0:t0 + RB].rearrange("f p t -> p f t"), in_=v_),
                      reads=[PRbb], writes=self.VF.b(t0, t0 + RB), dma=PRbb)
            for jt in range(4):
                cx.op("dve", lambda e: e.tensor_scalar(out=KK[:, jt, :], in0=k_[:, jt, :], scalar1=pv("k_k", jt), scalar2=None, op0=ALU.mult),
                      reads=[PRbb, self.pvb], writes=[KKb])
            cx.op("act", lambda e: e.activation(out=tmp4[:], in_=KK[:], func=AF.Square), reads=[KKb], writes=[tmp4b])
            for hf in range(2):
                cx.op("pe", lambda e: e.matmul(out=ps[4 + hf][:], lhsT=self.blk64, rhs=tmp4[:, 2 * hf:2 * hf + 2, :].rearrange("p a t -> p (a t)"),
                                               start=True, stop=True), reads=[tmp4b, self.cb], writes=[psb[4 + hf]])
            for hf in range(2):
                cx.op("act", lambda e: e.activation(out=tmp4[:, 2 * hf:2 * hf + 2, :].rearrange("p a t -> p (a t)"), in_=ps[4 + hf][:], func=AF.Sqrt),
                      reads=[psb[4 + hf]], writes=[tmp4b])
            cx.op("dve", lambda e: e.tensor_scalar(out=tmp4[:], in0=tmp4[:], scalar1=1e-12, scalar2=None, op0=ALU.max), reads=[tmp4b], writes=[tmp4b])
            cx.op("dve", lambda e: e.reciprocal(out=tmp4[:], in_=tmp4[:]), reads=[tmp4b], writes=[tmp4b])
            cx.op("dve", lambda e: e.tensor_tensor(out=KK[:], in0=KK[:], in1=tmp4[:], op=ALU.mult), reads=[KKb, tmp4b], writes=[KKb])
            for jt in range(4):
                cx.op("dve", lambda e: e.tensor_scalar(out=tmp4[:, jt, :], in0=A[:, jt, :], scalar1=-1.0, scalar2=pv("k_a", jt), op0=ALU.add, op1=ALU.mult),
                      reads=[Ab, self.pvb], writes=[tmp4b])
            cx.op("dve", lambda e: e.scalar_tensor_tensor(out=k_, in0=tmp4[:], scalar=1.0, in1=k_, op0=ALU.add, op1=ALU.mult),
                  reads=[tmp4b, PRbb], writes=[PRbb])
            for jt in (range(4) if emit_y else []):
                cx.op("dve", lambda e: e.scalar_tensor_tensor(out=tmp4[:, jt, :], in0=r_[:, jt, :], scalar=pv("r_k", jt), in1=k_[:, jt, :],
                                                              op0=ALU.mult, op1=ALU.mult), reads=[PRbb, self.pvb], writes=[tmp4b])
            for hf in (range(2) if emit_y else []):
                cx.op("pe", lambda e: e.matmul(out=ps[6 + hf][:], lhsT=self.blk64, rhs=tmp4[:, 2 * hf:2 * hf + 2, :].rearrange("p a t -> p (a t)"),
                                               start=True, stop=True), reads=[tmp4b, self.cb], writes=[psb[6 + hf]])
            for hf in (range(2) if emit_y else []):
                cx.op("dve", lambda e: e.tensor_tensor(out=bonus[:, 2 * hf:2 * hf + 2, :].rearrange("p a t -> p (a t)"), in0=ps[6 + hf][:],
                                                       in1=v_[:, 2 * hf:2 * hf + 2, :], op=ALU.mult) if False else
                      e.tensor_tensor(out=bonus[:, 2 * hf:2 * hf + 2, :], in0=ps[6 + hf][:].rearrange("p (a t) -> p a t", t=RB),
                                      in1=v_[:, 2 * hf:2 * hf + 2, :], op=ALU.mult), reads=[psb[6 + hf], PRbb], writes=[bonb])
            for jt in range(4):
                cx.op("dve", lambda e: e.tensor_tensor_scan(out=CS[:, jt, :], data0=self.cmask[:, 0:RB], data1=S[:, jt, :], initial=0.0,
                                                            op0=ALU.mult, op1=ALU.add), reads=[Sb, self.cb], writes=[CSb])
            cx.op("pool", lambda e: e.tensor_tensor(out=E3[:], in0=CS[:], in1=S[:], op=ALU.subtract), reads=[CSb, Sb], writes=[E3b])
            cx.op("act", lambda e: e.activation(out=E3[:], in_=E3[:], func=AF.Exp, scale=-WC_), reads=[E3b], writes=[E3b])
            cx.op("act", lambda e: e.activation(out=E1[:], in_=CS[:], func=AF.Exp, scale=-WC_), reads=[CSb], writes=[E1b])
            cx.op("act", lambda e: e.activation(out=CS[:], in_=CS[:], func=AF.Exp, scale=WC_), reads=[CSb], writes=[CSb])
            cx.op("dve", lambda e: e.tensor_tensor(out=Bh[:], in0=KK[:], in1=A[:], op=ALU.mult), reads=[KKb, Ab], writes=[Bhb])
            cx.op("dve", lambda e: e.tensor_tensor(out=Bh[:], in0=Bh[:], in1=CS[:], op=ALU.mult), reads=[Bhb, CSb], writes=[Bhb])
            cx.op("pool", lambda e: e.tensor_tensor(out=k_, in0=k_, in1=CS[:], op=ALU.mult), reads=[PRbb, CSb], writes=[PRbb])
            cx.op("pool", lambda e: e.tensor_tensor(out=KK[:], in0=KK[:], in1=E3[:], op=ALU.mult), reads=[KKb, E3b], writes=[KKb])
            cx.op("dve", lambda e: e.tensor_tensor(out=r_, in0=r_, in1=E1[:], op=ALU.mult), reads=[PRbb, E1b], writes=[PRbb])
            wcb = E1[:].rearrange("p a (c t) -> p a c t", t=64)[:, :, :, 63:64].broadcast_to([128, 4, NCH, 64])
            cx.op("dve", lambda e: e.tensor_tensor(out=Bc[:].rearrange("p a (c t) -> p a c t", t=64), in0=Bh[:].rearrange("p a (c t) -> p a c t", t=64),
                                                   in1=wcb, op=ALU.mult), reads=[Bhb, E1b], writes=[Bcb])
            cx.op("pool", lambda e: e.tensor_tensor(out=Kc[:].rearrange("p a (c t) -> p a c t", t=64), in0=k_.rearrange("p a (c t) -> p a c t", t=64),
                                                    in1=wcb, op=ALU.mult), reads=[PRbb, E1b], writes=[Kcb])
            srcs = [(v_, PRbb, 1.0), (KK[:], KKb, 1.0), (Bc[:], Bcb, -1.0), (Kc[:], Kcb, 1.0)]
            for qi, (sap, sbuf_, sc_) in enumerate(srcs):
                for pr in range(NP):
                    bk = bank()
                    for jt in range(4):
                        cx.op("pe", lambda e: e.transpose(out=ps[bk][:, jt * 128:(jt + 1) * 128], in_=sap[:, jt, pr * 128:(pr + 1) * 128], identity=self.ident),
                              reads=[sbuf_, self.cb], writes=[psb[bk]], signal=(jt == 3))
                    cx.op("act", lambda e: e.activation(out=TM[qi][0][:, pr, :], in_=ps[bk][:], func=AF.Copy, scale=sc_), reads=[psb[bk]], writes=[TM[qi][1]])
            Vtm, KKtm, Bntm, Kctm = [t[0] for t in TM]
            Vtmb, KKtmb, Bntmb, Kctmb = [t[1] for t in TM]
            if yst is not None:
                ys, ysb, _ = yst.next()
            for pr in range(NP):
                pc = slice(pr * 128, (pr + 1) * 128)
                if mode == "A":
                    q = stt["pair"] % 2
                    stt["pair"] += 1
                    TnT, TnTb = TnTq[q]
                    GnD, GnDb = GnDq[q]
                    Y1T = Y1Tq[q]
                for h in range(8):
                    jt, hp = h // 2, (h % 2) * 64
                    hs = slice(hp, hp + 64)
                    bx = h % 2
                    by = 2 + h % 2
                    kkt, bh, kh, rt = KK[hs, jt, pc], Bh[hs, jt, pc], k_[hs, jt, pc], r_[hs, jt, pc]
                    for qi, (lt, rh, rb_) in enumerate([(kkt, bh, Bhb), (kkt, kh, PRbb), (bh, kkt, KKb), (bh, rt, PRbb)]):
                        cx.op("pe", lambda e: e.matmul(out=ps[bx][:, qi * 128:(qi + 1) * 128], lhsT=lt, rhs=rh, start=True, stop=True),
                              reads=[KKb, Bhb, PRbb], writes=[psb[bx]], signal=(qi == 3))
                    cx.op("pe", lambda e: e.matmul(out=ps[by][:, (h // 2) * 128:(h // 2 + 1) * 128], lhsT=kh, rhs=rt, start=True, stop=True),
                          reads=[PRbb], writes=[psb[by]], signal=(h >= 6))
                    cx.op("dve", lambda e: e.tensor_tensor(out=Amat[:, h, 0:4, :], in0=ps[bx][:].rearrange("p (a c) -> p a c", c=128), in1=rmask[:, 0:4, :],
                                                           op=ALU.mult), reads=[psb[bx], rmb], writes=[Amb])
                for par in range(2):
                    cx.op("dve", lambda e: e.tensor_tensor(out=Amat[:, par:8:2, 4, :], in0=ps[2 + par][:].rearrange("p (a c) -> p a c", c=128),
                                                           in1=rmask[:, 4:5, :].broadcast_to([128, 4, 128]), op=ALU.mult),
                          reads=[psb[2 + par], rmb], writes=[Amb])
                pump(2)
                cur = [None, None]
                for g in range(2):
                    a0_, a0b = Acc[2 * g]
                    cx.op("dve", lambda e: e.tensor_tensor(out=a0_[:], in0=Amat[:, 4 * g:4 * g + 4, 2, :],
                                                           in1=self.ident.rearrange("p (o c) -> p o c", o=1).broadcast_to([128, 4, 128]), op=ALU.add),
                          reads=[Amb, self.cb], writes=[a0b])
                    cur[g] = 0
                for lev in range(1, 6):
                    for g in range(2):
                        pwn, pwnb = Pw[2 * g + lev % 2]
                        pwo, pwob = Pw[2 * g + (lev - 1) % 2]
                        for hh in range(4):
                            h = 4 * g + hh
                            if lev == 1:
                                Mo, No, rdb_ = Amat[:, h, 2, :], Amat[:, h, 0, :], Amb
                            else:
                                Mo, No, rdb_ = pwo[:, hh, 0, :], pwo[:, hh, 1, :], pwob
                            bp = 4 + 2 * g + hh // 2
                            c0 = (hh % 2) * 256
                            if lev < 5:
                                cx.op("pe", lambda e: e.matmul(out=ps[bp][:, c0:c0 + 128], lhsT=No, rhs=Mo, start=True, stop=True),
                                      reads=[rdb_], writes=[psb[bp]], signal=False)
                            cx.op("pe", lambda e: e.matmul(out=ps[bp][:, c0 + 128:c0 + 256], lhsT=Mo, rhs=No, start=True, stop=True),
                                  reads=[rdb_], writes=[psb[bp]], signal=(hh % 2 == 1))
                        for half in range(2):
                            bp = 4 + 2 * g + half
                            cx.op("act", lambda e: e.activation(out=pwn[:, 2 * half:2 * half + 2, :, :], in_=ps[bp][:].rearrange("p (a b c) -> p a b c", b=2, c=128),
                                                                func=AF.Copy), reads=[psb[bp]], writes=[pwnb])
                        ao, aob = Acc[2 * g + cur[g]]
                        an, anb = Acc[2 * g + 1 - cur[g]]
                        bc_ = 2 + g
                        for hh in range(4):
                            cx.op("pe", lambda e: e.matmul(out=ps[bc_][:, hh * 128:(hh + 1) * 128], lhsT=pwn[:, hh, 1, :], rhs=ao[:, hh, :], start=True, stop=True),
                                  reads=[pwnb, aob], writes=[psb[bc_]], signal=(hh == 3))
                        cx.op("dve", lambda e: e.tensor_tensor(out=an[:], in0=ao[:], in1=ps[bc_][:].rearrange("p (a c) -> p a c", c=128), op=ALU.add),
                              reads=[aob, psb[bc_]], writes=[anb])
                        cur[g] = 1 - cur[g]
                    pump(1)
                XT = lambda h: Acc[2 * (h // 4) + cur[h // 4]][0][:, h % 4, :]
                XTb = lambda h: Acc[2 * (h // 4) + cur[h // 4]][1]
                b1 = bank()
                for h in range(8):
                    cx.op("pe", lambda e: e.matmul(out=ps[b1][:, h * 64:(h + 1) * 64], lhsT=XT(h), rhs=KKtm[:, pr, h * 64:(h + 1) * 64], start=True, stop=True),
                          reads=[XTb(h), KKtmb], writes=[psb[b1]], signal=(h == 7))
                cx.op("act", lambda e: e.activation(out=P1s[:].rearrange("p a c -> p (a c)"), in_=ps[b1][:], func=AF.Copy), reads=[psb[b1]], writes=[P1b])
                for g in range(2):
                    b2 = bank()
                    for hh in range(4):
                        h = 4 * g + hh
                        cx.op("pe", lambda e: e.matmul(out=ps[b2][:, hh * 128:(hh + 1) * 128], lhsT=Amat[:, h, 1, :], rhs=XT(h), start=True, stop=True),
                              reads=[Amb, XTb(h)], writes=[psb[b2]], signal=(hh == 3))
                    cx.op("act", lambda e: e.activation(out=P2T[:, 4 * g:4 * g + 4, :].rearrange("p a c -> p (a c)"), in_=ps[b2][:], func=AF.Copy),
                          reads=[psb[b2]], writes=[P2b])
                pump(1)
                b3 = bank()
                for h in range(8):
                    cx.op("pe", lambda e: e.matmul(out=ps[b3][:, h * 64:(h + 1) * 64], lhsT=P2T[:, h, :], rhs=Vtm[:, pr, h * 64:(h + 1) * 64], start=True, stop=True),
                          reads=[P2b, Vtmb], writes=[psb[b3]], signal=(h == 7))
                cx.op("act", lambda e: e.activation(out=Vt[:].rearrange("p a c -> p (a c)"), in_=ps[b3][:], func=AF.Copy), reads=[psb[b3]], writes=[Vtb])
                for c in range(2):
                    rs = slice(c * 64, (c + 1) * 64)
                    bt = bank()
                    while bt % 2 != c:
                        bt = bank()
                    for jt in range(4):
                        js = slice(jt * 128, (jt + 1) * 128)
                        cx.op("pe", lambda e: e.matmul(out=ps[bt][:, js], lhsT=P1s[rs, 2 * jt:2 * jt + 2, :].rearrange("p a c -> p (a c)"), rhs=Bntm[rs, pr, js],
                                                       start=True, stop=True), reads=[P1b, Bntmb], writes=[psb[bt]], signal=(jt == 3))
                    cx.op("dve", lambda e: e.tensor_tensor(out=TnT[:, c, :, :], in0=ps[bt][:].rearrange("p (a c) -> p a c", c=128),
                                                           in1=self.blk64.rearrange("p (o c) -> p o c", o=1).broadcast_to([128, 4, 128]), op=ALU.mult),
                          reads=[psb[bt], self.cb], writes=[TnTb])
                    bg = bank()
                    while bg % 2 != c:
                        bg = bank()
                    for jt in range(4):
                        js = slice(jt * 128, (jt + 1) * 128)
                        cx.op("pe", lambda e: e.matmul(out=ps[bg][:, js], lhsT=Kctm[rs, pr, js], rhs=Vtm[rs, pr, js], start=True, stop=False),
                              reads=[Kctmb, Vtmb], writes=[psb[bg]], signal=False)
                        cx.op("pe", lambda e: e.matmul(out=ps[bg][:, js], lhsT=Bntm[rs, pr, js], rhs=Vt[rs, 2 * jt:2 * jt + 2, :].rearrange("p a c -> p (a c)"),
                                                       start=False, stop=True), reads=[Bntmb, Vtb], writes=[psb[bg]], signal=(jt == 3))
                    for par in range(2):
                        hs = slice(par * 64, (par + 1) * 64)
                        cx.op("act", lambda e: e.activation(out=GnD[hs, c, :, :], in_=ps[bg][hs, :].rearrange("p (a c) -> p a c", c=128)[:, :, par * 64:(par + 1) * 64],
                                                            func=AF.Copy), reads=[psb[bg]], writes=[GnDb])
                for par in (range(2) if emit_y else []):
                    b4 = bank()
                    for hh in range(4):
                        h = 2 * hh + par
                        cx.op("pe", lambda e: e.matmul(out=ps[b4][:, hh * 128:(hh + 1) * 128], lhsT=P1s[:, 2 * hh:2 * hh + 2, :].rearrange("p a c -> p (a c)"),
                                                       rhs=Amat[:, h, 3, :], start=True, stop=True), reads=[P1b, Amb], writes=[psb[b4]], signal=(hh == 3))
                    hs = slice(par * 64, (par + 1) * 64)
                    cx.op("dve", lambda e: e.tensor_tensor(out=Y1T[par][0][hs, :, :], in0=ps[b4][hs, :].rearrange("p (a c) -> p a c", c=128), in1=r_[hs, :, pc], op=ALU.add),
                          reads=[psb[b4], PRbb], writes=[Y1T[par][1]])
                if mode == "A":
                    Yc, Ycb = Ycq[q]
                    WC, WCb = WCq[q]
                    for c in range(2):
                        cc = slice(c * 64, (c + 1) * 64)
                        byc = bank()
                        for h in range(8):
                            o_ = ps[byc][0:64, h * 64:(h + 1) * 64]
                            cx.op("pe", lambda e: e.matmul(out=o_, lhsT=Amat[:, h, 4, cc], rhs=Vtm[:, pr, h * 64:(h + 1) * 64], start=True, stop=False),
                                  reads=[Amb, Vtmb], writes=[psb[byc]], signal=False)
                            cx.op("pe", lambda e: e.matmul(out=o_, lhsT=Amat[:, h, 3, cc], rhs=Vt[:, h, :], start=False, stop=True),
                                  reads=[Amb, Vtb], writes=[psb[byc]], signal=(h == 7))
                        cx.op("act", lambda e: e.activation(out=Yc[:, c, :], in_=ps[byc][0:64, :], func=AF.Copy), reads=[psb[byc]], writes=[Ycb])
                    cx.op("pool", lambda e: e.tensor_copy(out=WC[:], in_=E1[:, :, pr * 128 + 63:pr * 128 + 128:64]), reads=[E1b], writes=[WCb])

                    def mk_units(c, Y1T=Y1T, TnT=TnT, TnTb=TnTb, GnD=GnD, GnDb=GnDb, Yc=Yc, Ycb=Ycb, WC=WC, WCb=WCb, nchunk=(t0 // 64) + pr * 2):
                        cc = slice(c * 64, (c + 1) * 64)
                        nck = nchunk + c

                        def u_y0():
                            Hc, Hcb = H[stt["h"]]
                            byc = bank()
                            for h in range(8):
                                jt, par = h // 2, h % 2
                                cx.op("pe", lambda e: e.matmul(out=ps[byc][0:64, h * 64:(h + 1) * 64], lhsT=Y1T[par][0][:, jt, cc], rhs=Hc[:, jt, :], start=True, stop=True),
                                      reads=[Y1T[par][1], Hcb], writes=[psb[byc]], signal=(h == 7))
                            y0t, y0b, _ = y0st.next()
                            cx.op("dve", lambda e: e.tensor_tensor(out=y0t[:], in0=ps[byc][0:64, :], in1=Yc[:, c, :], op=ALU.add), reads=[psb[byc], Ycb], writes=[y0b])
                            cx.op("pool", lambda e: e.dma_start(out=self.Y0.t[nck], in_=y0t[:]), reads=[y0b], writes=self.Y0.b(nck * 64, nck * 64 + 64), dma=y0b)

                        def u_zt():
                            Apc, Apcb = Ap[stt["a"]]
                            bz_ = bank()
                            for h in range(8):
                                jt, par = h // 2, h % 2
                                cx.op("pe", lambda e: e.matmul(out=ps[bz_][:, h * 64:(h + 1) * 64], lhsT=Apc[:, jt, :], rhs=Y1T[par][0][:, jt, cc], start=True, stop=True),
                                      reads=[Apcb, Y1T[par][1]], writes=[psb[bz_]], signal=(h == 7))
                            zt_, ztb, _ = zst.next()
                            cx.op("act", lambda e: e.activation(out=zt_[:], in_=ps[bz_][:], func=AF.Copy), reads=[psb[bz_]], writes=[ztb])
                            cx.op("pool", lambda e: e.dma_start(out=self.ZS.t[nck], in_=zt_[:]), reads=[ztb], writes=self.ZS.b(nck * 64, nck * 64 + 64), dma=ztb)

                        def u_h():
                            Hc, Hcb = H[stt["h"]]
                            Hn, Hnb = H[1 - stt["h"]]
                            bh_ = bank()
                            for jt in range(4):
                                cx.op("pe", lambda e: e.matmul(out=ps[bh_][:, jt * 64:(jt + 1) * 64], lhsT=TnT[:, c, jt, :], rhs=Hc[:, jt, :], start=True, stop=True),
                                      reads=[TnTb, Hcb], writes=[psb[bh_]], signal=(jt == 3))
                            cx.op("pool", lambda e: e.tensor_tensor(out=Ht[:], in0=Hc[:], in1=WC[:, :, c:c + 1].broadcast_to([128, 4, 64]), op=ALU.mult),
                                  reads=[Hcb, WCb], writes=[Htb])
                            cx.op("pool", lambda e: e.tensor_tensor(out=Ht[:], in0=Ht[:], in1=GnD[:, c, :, :], op=ALU.add), reads=[Htb, GnDb], writes=[Htb])
                            cx.op("dve", lambda e: e.tensor_tensor(out=Hn[:], in0=Ht[:], in1=ps[bh_][:, 0:256].rearrange("p (a c) -> p a c", c=64), op=ALU.add),
                                  reads=[Htb, psb[bh_]], writes=[Hnb])
                            stt["h"] = 1 - stt["h"]

                        def u_ap():
                            Apc, Apcb = Ap[stt["a"]]
                            Apn, Apnb = Ap[1 - stt["a"]]
                            ba_ = bank()
                            for jt in range(4):
                                cx.op("pe", lambda e: e.matmul(out=ps[ba_][:, jt * 128:(jt + 1) * 128], lhsT=TnT[:, c, jt, :], rhs=Apc[:, jt, :], start=True, stop=True),
                                      reads=[TnTb, Apcb], writes=[psb[ba_]], signal=(jt == 3))
                            cx.op("pool", lambda e: e.tensor_tensor(out=ApT[:], in0=Apc[:], in1=WC[:, :, c:c + 1].broadcast_to([128, 4, 128]), op=ALU.mult),
                                  reads=[Apcb, WCb], writes=[ApTb])
                            cx.op("dve", lambda e: e.tensor_tensor(out=Apn[:], in0=ApT[:], in1=ps[ba_][:].rearrange("p (a c) -> p a c", c=128), op=ALU.add),
                                  reads=[ApTb, psb[ba_]], writes=[Apnb])
                            stt["a"] = 1 - stt["a"]
                        return [u_y0, u_zt, u_h, u_ap]
                    pump(len(deferred))
                    deferred.extend(mk_units(0) + mk_units(1))
                    continue
                by_ = bank()
                for c in range(2):
                    cc = slice(c * 64, (c + 1) * 64)
                    Hc, Hcb = H[hcur]
                    Hn, Hnb = H[1 - hcur]
                    byc = bank()
                    for h in (range(8) if emit_y else []):
                        jt, par = h // 2, h % 2
                        o_ = ps[byc][0:64, h * 64:(h + 1) * 64]
                        cx.op("pe", lambda e: e.matmul(out=o_, lhsT=Y1T[par][0][:, jt, cc], rhs=Hc[:, jt, :], start=True, stop=False),
                              reads=[Y1T[par][1], Hcb], writes=[psb[byc]], signal=False)
                        cx.op("pe", lambda e: e.matmul(out=o_, lhsT=Amat[:, h, 4, cc], rhs=Vtm[:, pr, h * 64:(h + 1) * 64], start=False, stop=False),
                              reads=[Amb, Vtmb], writes=[psb[byc]], signal=False)
                        cx.op("pe", lambda e: e.matmul(out=o_, lhsT=Amat[:, h, 3, cc], rhs=Vt[:, h, :], start=False, stop=True),
                              reads=[Amb, Vtb], writes=[psb[byc]], signal=(h == 7))
                    bh_ = bank()
                    for jt in range(4):
                        cx.op("pe", lambda e: e.matmul(out=ps[bh_][:, jt * 64:(jt + 1) * 64], lhsT=TnT[:, c, jt, :], rhs=Hc[:, jt, :], start=True, stop=True),
                              reads=[TnTb, Hcb], writes=[psb[bh_]], signal=(jt == 3))
                    ci = pr * 2 + c
                    wc1 = E1[:, :, ci * 64 + 63:ci * 64 + 64].broadcast_to([128, 4, 64])
                    cx.op("pool", lambda e: e.tensor_tensor(out=Ht[:], in0=Hc[:], in1=wc1, op=ALU.mult), reads=[Hcb, E1b], writes=[Htb])
                    cx.op("pool", lambda e: e.tensor_tensor(out=Ht[:], in0=Ht[:], in1=GnD[:, c, :, :], op=ALU.add), reads=[Htb, GnDb], writes=[Htb])
                    cx.op("dve", lambda e: e.tensor_tensor(out=Hn[:], in0=Ht[:], in1=ps[bh_][:, 0:256].rearrange("p (a c) -> p a c", c=64), op=ALU.add),
                          reads=[Htb, psb[bh_]], writes=[Hnb])
                    hcur = 1 - hcur
                    if mode == "A":
                        Apc, Apcb = Ap[apcur]
                        Apn, Apnb = Ap[1 - apcur]
                        y0t, y0b, _ = y0st.next()
                        cx.op("act", lambda e: e.activation(out=y0t[:], in_=ps[byc][0:64, :], func=AF.Copy), reads=[psb[byc]], writes=[y0b])
                        nchunk = (t0 // 64) + pr * 2 + c
                        cx.op("pool", lambda e: e.dma_start(out=self.Y0.t[nchunk], in_=y0t[:]), reads=[y0b], writes=self.Y0.b(nchunk * 64, nchunk * 64 + 64), dma=y0b)
                        bz_ = bank()
                        for h in range(8):
                            jt, par = h // 2, h % 2
                            cx.op("pe", lambda e: e.matmul(out=ps[bz_][:, h * 64:(h + 1) * 64], lhsT=Apc[:, jt, :], rhs=Y1T[par][0][:, jt, cc], start=True, stop=True),
                                  reads=[Apcb, Y1T[par][1]], writes=[psb[bz_]], signal=(h == 7))
                        zt_, ztb, _ = zst.next()
                        cx.op("act", lambda e: e.activation(out=zt_[:], in_=ps[bz_][:], func=AF.Copy), reads=[psb[bz_]], writes=[ztb])
                        cx.op("pool", lambda e: e.dma_start(out=self.ZS.t[nchunk], in_=zt_[:]), reads=[ztb], writes=self.ZS.b(nchunk * 64, nchunk * 64 + 64), dma=ztb)
                        ba_ = bank()
                        for jt in range(4):
                            cx.op("pe", lambda e: e.matmul(out=ps[ba_][:, jt * 128:(jt + 1) * 128], lhsT=TnT[:, c, jt, :], rhs=Apc[:, jt, :], start=True, stop=True),
                                  reads=[TnTb, Apcb], writes=[psb[ba_]], signal=(jt == 3))
                        wc2 = E1[:, :, ci * 64 + 63:ci * 64 + 64].broadcast_to([128, 4, 128])
                        cx.op("pool", lambda e: e.tensor_tensor(out=ApT[:], in0=Apc[:], in1=wc2, op=ALU.mult), reads=[Apcb, E1b], writes=[ApTb])
                        cx.op("dve", lambda e: e.tensor_tensor(out=Apn[:], in0=ApT[:], in1=ps[ba_][:].rearrange("p (a c) -> p a c", c=128), op=ALU.add),
                              reads=[ApTb, psb[ba_]], writes=[Apnb])
                        apcur = 1 - apcur
                    if not gn:
                        continue
                    y3 = ps[byc][0:64, :].rearrange("p (h d) -> p h d", d=64)
                    cx.op("dve", lambda e: e.tensor_reduce(out=st1[:], in_=y3, axis=AX.X, op=ALU.add), reads=[psb[byc]], writes=[st1b])
                    cx.op("act", lambda e: e.activation(out=ysq[:], in_=ps[byc][0:64, :], func=AF.Square), reads=[psb[byc]], writes=[ysqb])
                    cx.op("dve", lambda e: e.tensor_reduce(out=st2[:], in_=ysq[:].rearrange("p (h d) -> p h d", d=64), axis=AX.X, op=ALU.add),
                          reads=[ysqb], writes=[st2b])
                    cx.op("dve", lambda e: e.tensor_scalar(out=st1[:], in0=st1[:], scalar1=1.0 / 64, scalar2=None, op0=ALU.mult), reads=[st1b], writes=[st1b])
                    cx.op("dve", lambda e: e.tensor_tensor(out=st3[:], in0=st1[:], in1=st1[:], op=ALU.mult), reads=[st1b], writes=[st3b])
                    cx.op("dve", lambda e: e.scalar_tensor_tensor(out=st2[:], in0=st2[:], scalar=1.0 / 64, in1=st3[:], op0=ALU.mult, op1=ALU.subtract),
                          reads=[st2b, st3b], writes=[st2b])
                    cx.op("act", lambda e: e.activation(out=st2[:], in_=st2[:], func=AF.Sqrt, bias=self.eps_gn[0:64, :]), reads=[st2b, self.cb], writes=[st2b])
                    cx.op("dve", lambda e: e.reciprocal(out=st2[:], in_=st2[:]), reads=[st2b], writes=[st2b])
                    cx.op("dve", lambda e: e.tensor_tensor(out=yn[:].rearrange("p (h d) -> p h d", d=64), in0=y3,
                                                           in1=st1[:].rearrange("p (h o) -> p h o", o=1).broadcast_to([64, 8, 64]), op=ALU.subtract),
                          reads=[psb[byc], st1b], writes=[ynb])
                    cx.op("dve", lambda e: e.tensor_tensor(out=yn[:].rearrange("p (h d) -> p h d", d=64), in0=yn[:].rearrange("p (h d) -> p h d", d=64),
                                                           in1=st2[:].rearrange("p (h o) -> p h o", o=1).broadcast_to([64, 8, 64]), op=ALU.mult),
                          reads=[ynb, st2b], writes=[ynb])
                    for jt in range(4):
                        cx.op("pe", lambda e: e.transpose(out=ps[by_][:, jt * 128 + c * 64:jt * 128 + (c + 1) * 64], in_=yn[:, jt * 128:(jt + 1) * 128],
                                                          identity=self.ident[0:64, 0:64]), reads=[ynb, self.cb], writes=[psb[by_]], signal=(jt == 3))
                if not gn:
                    continue
                for jt in range(4):
                    cx.op("act", lambda e: e.activation(out=yo[:, jt, :], in_=ps[by_][:, jt * 128:(jt + 1) * 128], func=AF.Identity,
                                                        scale=pv("gn_g", jt), bias=pv("gn_b", jt)), reads=[psb[by_], self.pvb], writes=[yob])
                cx.op("dve", lambda e: e.tensor_tensor(out=yo[:], in0=yo[:], in1=bonus[:, :, pc], op=ALU.add), reads=[yob, bonb], writes=[yob])
                cx.op("dve", lambda e: e.tensor_tensor(out=ys[:, :, pc], in0=yo[:], in1=G[:, :, pc], op=ALU.mult), reads=[yob, Gb], writes=[ysb])
            if mode == "A":
                cx.op("pool", lambda e: e.dma_start(out=self.GS.t[:, :, t0:t0 + RB].rearrange("f p t -> p f t"), in_=G[:]),
                      reads=[Gb], writes=self.GS.b(t0, t0 + RB), dma=Gb)
                cx.op("pool", lambda e: e.dma_start(out=self.BS.t[:, :, t0:t0 + RB].rearrange("f p t -> p f t"), in_=bonus[:]),
                      reads=[bonb], writes=self.BS.b(t0, t0 + RB), dma=bonb)
            if gn:
                cx.op("pool", lambda e: e.dma_start(out=self.YR.t[:, :, t0:t0 + RB].rearrange("f p t -> p f t"), in_=ys[:]),
                      reads=[ysb], writes=self.YR.b(t0, t0 + RB), dma=ysb)
        if mode == "A":
            pump(len(deferred))
            Apc, Apcb = Ap[stt["a"]]
            Hc, Hcb = H[stt["h"]]
            bq = bank()
            for jt in range(4):
                cx.op("pe", lambda e: e.transpose(out=ps[bq][:, jt * 128:(jt + 1) * 128], in_=Apc[:, jt, :], identity=self.ident),
                      reads=[Apcb, self.cb], writes=[psb[bq]], signal=(jt == 3))
            cx.op("act", lambda e: e.activation(out=pay[:, 0:512], in_=ps[bq][:], func=AF.Copy), reads=[psb[bq]], writes=[payb])
            cx.op("dve", lambda e: e.tensor_copy(out=pay[:, 512:768], in_=Hc[:].rearrange("p a c -> p (a c)")), reads=[Hcb], writes=[payb])
            cx.op("pool", lambda e: e.dma_start(out=self.EX2s, in_=pay[:]), reads=[payb], writes=[self.EX2sb], dma=payb)
            cx.op("pool", lambda e: e.collective_compute("AllGather", ALU.bypass, replica_groups=GROUPS, ins=[self.EX2s], outs=[self.EX2d]),
                  reads=[self.EX2sb], writes=[self.EX2db], coll=True)
        cx.end_phase()


def _phase_rwkv_out(self, l):
    cx, nc, T = self.cx, self.nc, self.T
    ps, psb = self.ps, self.psb
    RB = 256
    pv = lambda name, j: self.pv(l, name, j)
    with contextlib.ExitStack() as es:
        def sb(n, sh, dt=F32):
            return es.enter_context(nc.sbuf_tensor(uniq("o_" + n), sh, dt)), Buf("o_" + n)
        g2, g2b = sb("g2", [128, 4, 768])
        H = [sb("H%d" % i, [128, 4, 64]) for i in range(2)]
        Ht, Htb = sb("Ht", [128, 4, 64])
        zr = Ring(cx, es, "o_z", [128, 512], F32, 3)
        y0r = Ring(cx, es, "o_y0", [64, 512], F32, 3)
        gr = Ring(cx, es, "o_g", [128, 4, RB], F32, 2)
        br = Ring(cx, es, "o_b", [128, 4, RB], F32, 2)
        yst = Ring(cx, es, "o_yst", [128, 4, RB], BF16, 2)
        ysum = [sb("ysum%d" % i, [64, 512]) for i in range(2)]
        ysq = [sb("ysq%d" % i, [64, 512]) for i in range(2)]
        yn = [sb("yn%d" % i, [64, 512]) for i in range(2)]
        st1 = [sb("st1_%d" % i, [64, 8]) for i in range(2)]
        st2 = [sb("st2_%d" % i, [64, 8]) for i in range(2)]
        st3 = [sb("st3_%d" % i, [64, 8]) for i in range(2)]
        yo = [sb("yo%d" % i, [128, 4, 128]) for i in range(2)]
        cx.op("dve", lambda e: e.memset(H[0][0][:], 0.0), writes=[H[0][1]])
        cx.op("sp", lambda e: e.dma_start(out=g2[:], in_=self.EX2d.rearrange("(r p) c -> p r c", r=4)), reads=[self.EX2db], writes=[g2b], dma=g2b)
        hc_ = 0
        for r in range(4):
            Hc, Hcb = H[hc_]
            Hn, Hnb = H[1 - hc_]
            bz = 7
            for jt in range(4):
                cx.op("pe", lambda e: e.matmul(out=ps[bz][:, jt * 64:(jt + 1) * 64], lhsT=g2[:, r, jt * 128:(jt + 1) * 128], rhs=Hc[:, jt, :], start=True, stop=True),
                      reads=[g2b, Hcb], writes=[psb[bz]], signal=(jt == 3))
            cx.op("dve", lambda e: e.tensor_tensor(out=Ht[:].rearrange("p a c -> p (a c)"), in0=ps[bz][:, 0:256], in1=g2[:, r, 512:768], op=ALU.add),
                  reads=[psb[bz], g2b], writes=[Htb])
            cx.op("dve", lambda e: e.tensor_tensor(out=Ht[:], in0=Ht[:], in1=Hc[:], op=ALU.subtract), reads=[Htb, Hcb], writes=[Htb])
            cx.op("dve", lambda e: e.scalar_tensor_tensor(out=Hn[:], in0=Ht[:], scalar=self.selsb[:, 4 + r:5 + r], in1=Hc[:], op0=ALU.mult, op1=ALU.add),
                  reads=[Htb, Hcb, self.selb], writes=[Hnb])
            hc_ = 1 - hc_
        Hs, Hsb = H[hc_]
        it = 0
        for blk in range(T // RB):
            t0 = blk * RB
            G, Gb, _ = gr.next()
            bonus, bonb, _ = br.next()
            ys, ysb, _ = yst.next()
            cx.op("sp", lambda e: e.dma_start(out=G[:], in_=self.GS.t[:, :, t0:t0 + RB].rearrange("f p t -> p f t")),
                  reads=self.GS.b(t0, t0 + RB), writes=[Gb], dma=Gb)
            cx.op("sp", lambda e: e.dma_start(out=bonus[:], in_=self.BS.t[:, :, t0:t0 + RB].rearrange("f p t -> p f t")),
                  reads=self.BS.b(t0, t0 + RB), writes=[bonb], dma=bonb)
            for pr in range(RB // 128):
                pc = slice(pr * 128, (pr + 1) * 128)
                by_ = 4 + (it // 2) % 2
                for c in range(2):
                    n = t0 // 64 + pr * 2 + c
                    d = it % 2
                    it += 1
                    zt, ztb, _ = zr.next()
                    y0, y0b, _ = y0r.next()
                    cx.op("sp", lambda e: e.dma_start(out=zt[:], in_=self.ZS.t[n]), reads=self.ZS.b(n * 64, n * 64 + 64), writes=[ztb], dma=ztb)
                    cx.op("sp", lambda e: e.dma_start(out=y0[:], in_=self.Y0.t[n]), reads=self.Y0.b(n * 64, n * 64 + 64), writes=[y0b], dma=y0b)
                    byc = d
                    for h in range(8):
                        cx.op("pe", lambda e: e.matmul(out=ps[byc][0:64, h * 64:(h + 1) * 64], lhsT=zt[:, h * 64:(h + 1) * 64], rhs=Hs[:, h // 2, :], start=True, stop=True),
                              reads=[ztb, Hsb], writes=[psb[byc]], signal=(h == 7))
                    ysm, ysmb = ysum[d]
                    cx.op("dve", lambda e: e.tensor_tensor(out=ysm[:], in0=ps[byc][0:64, :], in1=y0[:], op=ALU.add), reads=[psb[byc], y0b], writes=[ysmb])
                    y3 = ysm[:].rearrange("p (h d) -> p h d", d=64)
                    s1, s1b = st1[d]
                    s2, s2b = st2[d]
                    s3, s3b = st3[d]
                    yq, yqb = ysq[d]
                    ynn, ynb = yn[d]
                    cx.op("dve", lambda e: e.tensor_reduce(out=s1[:], in_=y3, axis=AX.X, op=ALU.add), reads=[ysmb], writes=[s1b])
                    cx.op("act", lambda e: e.activation(out=yq[:], in_=ysm[:], func=AF.Square), reads=[ysmb], writes=[yqb])
                    cx.op("dve", lambda e: e.tensor_reduce(out=s2[:], in_=yq[:].rearrange("p (h d) -> p h d", d=64), axis=AX.X, op=ALU.add),
                          reads=[yqb], writes=[s2b])
                    cx.op("dve", lambda e: e.tensor_scalar(out=s1[:], in0=s1[:], scalar1=1.0 / 64, scalar2=None, op0=ALU.mult), reads=[s1b], writes=[s1b])
                    cx.op("dve", lambda e: e.tensor_tensor(out=s3[:], in0=s1[:], in1=s1[:], op=ALU.mult), reads=[s1b], writes=[s3b])
                    cx.op("dve", lambda e: e.scalar_tensor_tensor(out=s2[:], in0=s2[:], scalar=1.0 / 64, in1=s3[:], op0=ALU.mult, op1=ALU.subtract),
                          reads=[s2b, s3b], writes=[s2b])
                    cx.op("act", lambda e: e.activation(out=s2[:], in_=s2[:], func=AF.Sqrt, bias=self.eps_gn[0:64, :]), reads=[s2b, self.cb], writes=[s2b])
                    cx.op("dve", lambda e: e.reciprocal(out=s2[:], in_=s2[:]), reads=[s2b], writes=[s2b])
                    cx.op("pool", lambda e: e.tensor_tensor(out=ynn[:].rearrange("p (h d) -> p h d", d=64), in0=y3,
                                                            in1=s1[:].rearrange("p (h o) -> p h o", o=1).broadcast_to([64, 8, 64]), op=ALU.subtract),
                          reads=[ysmb, s1b], writes=[ynb])
                    cx.op("dve", lambda e: e.tensor_tensor(out=ynn[:].rearrange("p (h d) -> p h d", d=64), in0=ynn[:].rearrange("p (h d) -> p h d", d=64),
                                                           in1=s2[:].rearrange("p (h o) -> p h o", o=1).broadcast_to([64, 8, 64]), op=ALU.mult),
                          reads=[ynb, s2b], writes=[ynb])
                    for jt in range(4):
                        cx.op("pe", lambda e: e.transpose(out=ps[by_][:, jt * 128 + c * 64:jt * 128 + (c + 1) * 64], in_=ynn[:, jt * 128:(jt + 1) * 128],
                                                          identity=self.ident[0:64, 0:64]), reads=[ynb, self.cb], writes=[psb[by_]], signal=(jt == 3))
                yo_, yob = yo[(it // 2) % 2]
                for jt in range(4):
                    cx.op("act", lambda e: e.activation(out=yo_[:, jt, :], in_=ps[by_][:, jt * 128:(jt + 1) * 128], func=AF.Identity,
                                                        scale=pv("gn_g", jt), bias=pv("gn_b", jt)), reads=[psb[by_], self.pvb], writes=[yob])
                cx.op("pool", lambda e: e.tensor_tensor(out=yo_[:], in0=yo_[:], in1=bonus[:, :, pc], op=ALU.add), reads=[yob, bonb], writes=[yob])
                cx.op("dve", lambda e: e.tensor_tensor(out=ys[:, :, pc], in0=yo_[:], in1=G[:, :, pc], op=ALU.mult), reads=[yob, Gb], writes=[ysb])
            cx.op("pool", lambda e: e.dma_start(out=self.YR.t[:, :, t0:t0 + RB].rearrange("f p t -> p f t"), in_=ys[:]),
                  reads=[ysb], writes=self.YR.b(t0, t0 + RB), dma=ysb)
        cx.end_phase()


Builder.phase_rwkv_out = _phase_rwkv_out


Builder.decl_rwkv = _decl_rwkv
Builder.phase_rwkv = _phase_rwkv


def prep_rwkv(inp, L, sh):
    lora = np.zeros((L, 128, 3, 512), np.float32)
    for l in range(L):
        lora[l, 0:64, 0] = inp["decay_lora_b"][l]
        lora[l, 64:128, 0] = inp["iclr_lora_b"][l]
        lora[l, :, 1] = inp["gate_lora_b"][l]
        if l > 0:
            lora[l, 0:32, 2] = inp["vres_lora_b"][l - 1]
    sh["lora"] = lora.reshape(L, 128, 3 * 512)
    i = np.arange(128)[:, None]
    j = np.arange(128)[None, :]
    same = (i // 64) == (j // 64)
    SL = (same & (j < i)).astype(np.float32)
    SU = (same & (j > i)).astype(np.float32)
    UI = (same & (j >= i)).astype(np.float32)
    rm = np.stack([-SL, SL, -SU, -UI, UI], axis=1)
    sh["rmask"] = np.ascontiguousarray(rm).reshape(128, 640)


def build_full(T, L, seg=False):
    b = Builder(T, L)
    b.decl_mix()
    b.decl_att()
    b.decl_merge()
    b.decl_rwkv()
    if seg:
        b.decl_seg()
    b.cast_weights()
    for l in range(L):
        src = b.xin if l == 0 else b.XS
        b.phase_ffn(l, 0, src, b.XS, "norm_ffn1")
        if seg:
            b.phase_proj(l, b.XS, hook=lambda: b.phase_ex1(l))
            b.phase_ex1_select(l)
            b.phase_rwkv(l, "A")
            b.phase_att(l)
            b.phase_rwkv_out(l)
        else:
            b.phase_proj(l, b.XS)
            b.phase_rwkv(l)
            b.phase_att(l)
        b.phase_merge(l, b.XS, b.XS)
        b.phase_ffn(l, 1, b.XS, b.out if l == L - 1 else b.XS, "norm_ffn2")
    b.cx.final_wait()
    return b


def seg_selectors(s):
    sel = np.zeros((128, 4), np.float32)
    pre = np.zeros((128, 4), np.float32)
    if s > 0:
        sel[:, s - 1] = 1.0
    pre[:, :s] = 1.0
    return sel, pre


def prep_all(inp, L):
    sh = prep_shared(inp, L)
    prep_mix(inp, L, sh)
    prep_att(inp, L, sh)
    prep_merge(inp, L, sh)
    prep_rwkv(inp, L, sh)
    return sh


def kernel(**inputs):
    x = np.asarray(inputs["x"], np.float32)
    mem = np.asarray(inputs["mem"], np.float32)
    B, S, _ = x.shape
    L = int(np.asarray(inputs["norm_ffn1"]).shape[0])
    NSEG = 4
    T = S // NSEG
    b = build_full(T, L, seg=True)
    sh = prep_all(inputs, L)
    in_maps = []
    for c in range(B * NSEG):
        bi, si = c // NSEG, c % NSEG
        m = {k: sh[k] for k in b.win if k in sh}
        m["sel"], m["pre"] = seg_selectors(si)
        m["xT"] = to_fm(x[bi, si * T:(si + 1) * T])
        m["memT"] = to_fm(mem[bi])
        in_maps.append(m)
    res = run_bass_kernel_spmd(b.nc, in_maps, core_ids=list(range(B * NSEG)))
    out = np.zeros((B, S, D), np.float32)
    for c, r in enumerate(res.results):
        bi, si = c // NSEG, c % NSEG
        out[bi, si * T:(si + 1) * T] = from_fm(np.asarray(r["outT"], np.float32))
    return out


GROUPS = [[0, 1, 2, 3], [4, 5, 6, 7]]
EXA = 2048
EXB = 2112


def _decl_seg(self):
    nc, es = self.nc, self.es
    self.seg = True
    self.wdecl("sel", [128, 4], cast=False)
    self.wdecl("pre", [128, 4], cast=False)
    self.EX1s = nc.dram_tensor("EX1s", [128, EXA], BF16, kind="Internal").ap()
    self.EX1d = nc.dram_tensor("EX1d", [4 * 128, EXA], BF16, kind="Internal").ap()
    self.EX1bs = nc.dram_tensor("EX1bs", [128, EXB], BF16, kind="Internal").ap()
    self.EX1bd = nc.dram_tensor("EX1bd", [4 * 128, EXB], BF16, kind="Internal").ap()
    self.EX1bsb, self.EX1bdb = Buf("EX1bs"), Buf("EX1bd")
    self.EX2s = nc.dram_tensor("EX2s", [128, 768], F32, kind="Internal").ap()
    self.EX2d = nc.dram_tensor("EX2d", [4 * 128, 768], F32, kind="Internal").ap()
    self.EX1sb, self.EX1db, self.EX2sb, self.EX2db = Buf("EX1s"), Buf("EX1d"), Buf("EX2s"), Buf("EX2d")
    self.KH = DramT(nc, "KH", [4, 128, 512], BF16)
    self.VH = DramT(nc, "VH", [4, 128, 520], BF16)
    T = self.T
    self.Y0 = DramT(nc, "Y0", [T // 64, 64, 512], F32, gran=64)
    self.ZS = DramT(nc, "ZS", [T // 64, 128, 512], F32, gran=64)
    self.GS = DramT(nc, "GS", [4, 128, T], F32)
    self.BS = DramT(nc, "BS", [4, 128, T], F32)
    self.SH = nc.dram_tensor("SH", [128, 14], F32, kind="Internal").ap()
    self.SHb = Buf("SH")
    self.selsb = es.enter_context(nc.sbuf_tensor("sb_sel", [128, 8], F32))
    self.selb = Buf("sel", const=True)
    self.shcol = es.enter_context(nc.sbuf_tensor("sb_shcol", [128, 14], F32))
    self.shcolb = Buf("shcol")
    cx = self.cx
    cx.op("sp", lambda e: e.dma_start(out=self.selsb[:, 0:4], in_=self.win["sel"]), writes=[self.selb], dma=self.selb)
    cx.op("sp", lambda e: e.dma_start(out=self.selsb[:, 4:8], in_=self.win["pre"]), writes=[self.selb], dma=self.selb)
    cx.keep_sems()


def _phase_ex1(self, l):
    cx, nc, T = self.cx, self.nc, self.T
    cx.op("sp", lambda e: e.dma_start(out=self.EX1s[:, 0:2048].rearrange("p (f t) -> p f t", f=4),
                                      in_=self.KT.t[:, :, T - 512:T].rearrange("f p t -> p f t")),
          reads=self.KT.b(T - 512, T), writes=[self.EX1sb], dma=self.EX1sb)
    cx.op("sp", lambda e: e.dma_start(out=self.EX1bs[:, 0:2080].rearrange("p (n c) -> p n c", n=4),
                                      in_=self.V1.t[T // 128 - 4:T // 128].rearrange("n p c -> p n c")),
          reads=self.V1.b(T - 512, T), writes=[self.EX1bsb], dma=self.EX1bsb)
    cx.op("sp", lambda e: e.dma_start(out=self.EX1bs[:, 2080:2108].bitcast(F32), in_=self.shcol[:]),
          reads=[self.shcolb], writes=[self.EX1bsb], dma=self.EX1bsb)
    cx.op("pool", lambda e: e.collective_compute("AllGather", ALU.bypass, replica_groups=GROUPS, ins=[self.EX1s], outs=[self.EX1d]),
          reads=[self.EX1sb], writes=[self.EX1db], coll=True)
    cx.op("pool", lambda e: e.collective_compute("AllGather", ALU.bypass, replica_groups=GROUPS, ins=[self.EX1bs], outs=[self.EX1bd]),
          reads=[self.EX1bsb], writes=[self.EX1bdb], coll=True)


def _phase_ex1_select(self, l):
    cx, nc, T = self.cx, self.nc, self.T
    with contextlib.ExitStack() as es:
        g = es.enter_context(nc.sbuf_tensor(uniq("x1_g"), [128, 4, EXA], BF16))
        g2_ = es.enter_context(nc.sbuf_tensor(uniq("x1_g2"), [128, 4, EXB], BF16))
        acc = es.enter_context(nc.sbuf_tensor(uniq("x1_acc"), [128, 4128], BF16))
        accf = es.enter_context(nc.sbuf_tensor(uniq("x1_accf"), [128, 14], F32))
        gb, g2b_, accb, accb2, accfb = Buf("x1_g"), Buf("x1_g2"), Buf("x1_acc"), Buf("x1_acc2"), Buf("x1_accf")
        cx.op("sp", lambda e: e.dma_start(out=g[:], in_=self.EX1d.rearrange("(r p) c -> p r c", r=4)), reads=[self.EX1db], writes=[gb], dma=gb)
        cx.op("sp", lambda e: e.dma_start(out=g2_[:], in_=self.EX1bd.rearrange("(r p) c -> p r c", r=4)), reads=[self.EX1bdb], writes=[g2b_], dma=g2b_)
        for r in range(4):
            sc = self.selsb[:, r:r + 1]
            gf = g2_[:, r, 2080:2108].bitcast(F32)
            if r == 0:
                cx.op("dve", lambda e: e.tensor_scalar(out=acc[:, 0:2048], in0=g[:, r, :], scalar1=sc, scalar2=None, op0=ALU.mult),
                      reads=[gb, self.selb], writes=[accb])
                cx.op("dve", lambda e: e.tensor_scalar(out=acc[:, 2048:4128], in0=g2_[:, r, 0:2080], scalar1=sc, scalar2=None, op0=ALU.mult),
                      reads=[g2b_, self.selb], writes=[accb2])
                cx.op("pool", lambda e: e.tensor_scalar(out=accf[:], in0=gf, scalar1=sc, scalar2=None, op0=ALU.mult),
                      reads=[g2b_, self.selb], writes=[accfb])
            else:
                cx.op("dve", lambda e: e.scalar_tensor_tensor(out=acc[:, 0:2048], in0=g[:, r, :], scalar=sc, in1=acc[:, 0:2048], op0=ALU.mult, op1=ALU.add),
                      reads=[gb, self.selb, accb], writes=[accb])
                cx.op("dve", lambda e: e.scalar_tensor_tensor(out=acc[:, 2048:4128], in0=g2_[:, r, 0:2080], scalar=sc, in1=acc[:, 2048:4128], op0=ALU.mult, op1=ALU.add),
                      reads=[g2b_, self.selb, accb2], writes=[accb2])
                cx.op("dve", lambda e: e.scalar_tensor_tensor(out=accf[:], in0=gf, scalar=sc, in1=accf[:], op0=ALU.mult, op1=ALU.add),
                      reads=[g2b_, self.selb, accfb], writes=[accfb])
        cx.op("pool", lambda e: e.dma_start(out=self.KH.t.rearrange("f p t -> p f t"), in_=acc[:, 0:2048].rearrange("p (f t) -> p f t", f=4)),
              reads=[accb], writes=self.KH.b(0, 512), dma=accb)
        cx.op("pool", lambda e: e.dma_start(out=self.VH.t.rearrange("n p c -> p n c"), in_=acc[:, 2048:4128].rearrange("p (n c) -> p n c", n=4)),
              reads=[accb2], writes=self.VH.b(0, 512), dma=accb2)
        cx.op("pool", lambda e: e.dma_start(out=self.SH, in_=accf[:]), reads=[accfb], writes=[self.SHb], dma=accfb)
        cx.end_phase()


Builder.decl_seg = _decl_seg
Builder.phase_ex1 = _phase_ex1
Builder.phase_ex1_select = _phase_ex1_select
```
